# Optimizing a Trainium2 kernel written in Bass

```python
import math
import jax, jax.numpy as jnp
from jax import lax
import numpy as np

D_MODEL = 1024
BATCH = 8
SEQ = 2048
DEPTH = 2

CTX_LEN = 256
GRID_W = 64
HEAD_DIM = 64
D_MIX = D_MODEL
NA_HEADS = D_MIX // (4 * HEAD_DIM)
NA_WIDTH = NA_HEADS * HEAD_DIM
CONV_CH = D_MIX // 4
WA_HEADS = D_MIX // (2 * HEAD_DIM)
WA_KV_HEADS = WA_HEADS // 4
WA_QW = WA_HEADS * HEAD_DIM
WA_KVW = WA_KV_HEADS * HEAD_DIM
NA_WIN_ROWS = 8
NA_WIN_COLS = 16
CONV_WIDTH = 3
WA_WINDOW = 128
WA_BLOCK = 128
FFN_DIM = 4 * D_MODEL
ROPE_BASE = 10000.0
EPS = 1e-6
NEG_INF = -1e30
IN_SPLITS = (NA_WIDTH, NA_WIDTH, NA_WIDTH, CONV_CH, CONV_CH, CONV_CH, WA_QW, WA_KVW, WA_KVW)
IN_OFF = tuple(int(v) for v in np.cumsum((0,) + IN_SPLITS))
IN_WIDTH = IN_OFF[-1]

kernel_name = "hymba_natten_shortconv_swa_prefix_dit"


def rms_norm(x, gain):
    x32 = x.astype(jnp.float32)
    y = x32 * lax.rsqrt(jnp.mean(x32 * x32, axis=-1, keepdims=True) + EPS)
    return (y * gain.astype(jnp.float32)).astype(x.dtype)


def split_heads(t, n_heads):
    return t.reshape(*t.shape[:-1], n_heads, HEAD_DIM)


def split_proj(p):
    return jnp.split(p, list(IN_OFF[1:-1]), axis=-1)


def modulation(cond, w, b):
    return jnp.split(jax.nn.silu(cond) @ w + b, 6, axis=-1)


def axial_rope(t, row_pos, col_pos):
    quarter = HEAD_DIM // 4
    half = HEAD_DIM // 2
    inv = ROPE_BASE ** (-jnp.arange(quarter, dtype=jnp.float32) / quarter)

    def rot(u, pos):
        ang = pos[:, None] * inv[None, :]
        cos = jnp.cos(ang)[None, :, None, :]
        sin = jnp.sin(ang)[None, :, None, :]
        u1, u2 = u[..., :quarter], u[..., quarter:]
        return jnp.concatenate([u1 * cos - u2 * sin, u2 * cos + u1 * sin], axis=-1)

    t32 = t.astype(jnp.float32)
    out = jnp.concatenate([rot(t32[..., :half], row_pos), rot(t32[..., half:], col_pos)], axis=-1)
    return out.astype(t.dtype)


def neighborhood_attention(q, k, v, k_ctx, v_ctx, rpb, rows):
    B, S, H, hd = q.shape
    kr = min(NA_WIN_ROWS, rows)
    kc = NA_WIN_COLS
    scale = hd ** -0.5
    qg = q.reshape(B, rows, GRID_W, H, hd)
    kg = k.reshape(B, rows, GRID_W, H, hd)
    vg = v.reshape(B, rows, GRID_W, H, hd)
    r = jnp.arange(rows)
    row_idx = jnp.clip(r - kr // 2, 0, rows - kr)[:, None] + jnp.arange(kr)[None, :]
    k_loc = jnp.take(kg, row_idx, axis=1)
    v_loc = jnp.take(vg, row_idx, axis=1)
    col = jnp.arange(GRID_W)
    col_start = jnp.clip(col - kc // 2, 0, GRID_W - kc)
    col_ok = (col[None, :] >= col_start[:, None]) & (col[None, :] < col_start[:, None] + kc)
    dr = row_idx - r[:, None] + (NA_WIN_ROWS - 1)
    dc = jnp.clip(col[None, :] - col[:, None], -(kc - 1), kc - 1) + (kc - 1)
    bias = rpb[:, dr[:, None, :, None], dc[None, :, None, :]]
    bias = jnp.transpose(bias, (1, 0, 2, 3, 4)).astype(jnp.float32)
    s_loc = jnp.einsum('brqhd,brjkhd->brhqjk', qg, k_loc).astype(jnp.float32) * scale + bias[None]
    s_loc = jnp.where(col_ok[:, None, :], s_loc, NEG_INF).reshape(B, rows, H, GRID_W, kr * GRID_W)
    s_ctx = jnp.einsum('brqhd,bchd->brhqc', qg, k_ctx).astype(jnp.float32) * scale
    p = jax.nn.softmax(jnp.concatenate([s_ctx, s_loc], axis=-1), axis=-1).astype(v.dtype)
    lc = k_ctx.shape[1]
    p_ctx = p[..., :lc]
    p_loc = p[..., lc:].reshape(B, rows, H, GRID_W, kr, GRID_W)
    out = (jnp.einsum('brhqc,bchd->brqhd', p_ctx, v_ctx)
           + jnp.einsum('brhqjk,brjkhd->brqhd', p_loc, v_loc))
    return out.reshape(B, S, H * hd)


def window_attention(q, k, v, k_ctx, v_ctx, sink):
    B, S, H, hd = q.shape
    kvh = k.shape[2]
    g = H // kvh
    nb = S // WA_BLOCK
    scale = hd ** -0.5
    qb = q.reshape(B, nb, WA_BLOCK, kvh, g, hd)

    def band(t):
        tp = jnp.pad(t, ((0, 0), (WA_BLOCK, WA_BLOCK), (0, 0), (0, 0))).reshape(B, nb + 2, WA_BLOCK, kvh, hd)
        return jnp.concatenate([tp[:, :-2], tp[:, 1:-1], tp[:, 2:]], axis=2)

    kb, vb = band(k), band(v)
    i = jnp.arange(WA_BLOCK)[:, None]
    j = jnp.arange(3 * WA_BLOCK)[None, :]
    kpos = (jnp.arange(nb) * WA_BLOCK - WA_BLOCK)[:, None, None] + j[None]
    mask = (jnp.abs(j - WA_BLOCK - i) <= WA_WINDOW)[None] & (kpos >= 0) & (kpos < S)
    s_loc = jnp.einsum('bnqkgd,bnjkd->bnkgqj', qb, kb).astype(jnp.float32) * scale
    s_loc = jnp.where(mask[None, :, None, None], s_loc, NEG_INF)
    s_ctx = jnp.einsum('bnqkgd,bckd->bnkgqc', qb, k_ctx).astype(jnp.float32) * scale
    s_sink = jnp.broadcast_to(sink.astype(jnp.float32).reshape(1, 1, kvh, g, 1, 1), s_ctx.shape[:-1] + (1,))
    p = jax.nn.softmax(jnp.concatenate([s_sink, s_ctx, s_loc], axis=-1), axis=-1).astype(v.dtype)
    lc = k_ctx.shape[1]
    out = (jnp.einsum('bnkgqc,bckd->bnqkgd', p[..., 1:1 + lc], v_ctx)
           + jnp.einsum('bnkgqj,bnjkd->bnqkgd', p[..., 1 + lc:], vb))
    return out.reshape(B, S, H * hd)


def context_attention(q, k, v, sink=None):
    B, L, H, hd = q.shape
    kvh = k.shape[2]
    g = H // kvh
    qg = q.reshape(B, L, kvh, g, hd)
    s = jnp.einsum('bqkgd,bckd->bkgqc', qg, k).astype(jnp.float32) * (hd ** -0.5)
    if sink is not None:
        s_sink = jnp.broadcast_to(sink.astype(jnp.float32).reshape(1, kvh, g, 1, 1), s.shape[:-1] + (1,))
        p = jax.nn.softmax(jnp.concatenate([s_sink, s], axis=-1), axis=-1)[..., 1:]
    else:
        p = jax.nn.softmax(s, axis=-1)
    out = jnp.einsum('bkgqc,bckd->bqkgd', p.astype(v.dtype), v)
    return out.reshape(B, L, H * hd)


def short_conv(u, w, b):
    pad = CONV_WIDTH // 2
    length = u.shape[1]
    up = jnp.pad(u, ((0, 0), (pad, pad), (0, 0)))
    y = b
    for tap in range(CONV_WIDTH):
        y = y + w[tap] * up[:, tap:tap + length]
    return y


def mix_out(o_na, o_cv, o_wa, g_out, w_o):
    o = jnp.concatenate([
        rms_norm(o_na, g_out[:NA_WIDTH]),
        rms_norm(o_cv, g_out[NA_WIDTH:NA_WIDTH + CONV_CH]),
        rms_norm(o_wa, g_out[NA_WIDTH + CONV_CH:]),
    ], axis=-1)
    return o @ w_o


def sq_relu_mlp(h, w1, w2):
    return jnp.square(jax.nn.relu(h @ w1)) @ w2


def setup_inputs(seed: int = 0) -> dict:
    key = jax.random.key(seed)
    ks = jax.random.split(key, 21)
    f32 = jnp.float32

    def nrm(k, shape, s):
        return jax.random.normal(k, shape, f32) * s

    return {
        "x": nrm(ks[0], (BATCH, SEQ, D_MODEL), 1.0),
        "c": nrm(ks[1], (BATCH, D_MODEL), 1.0),
        "ctx": nrm(ks[2], (BATCH, CTX_LEN, D_MODEL), 1.0),
        "c_ctx": nrm(ks[3], (D_MODEL,), 1.0),
        "w_mod": nrm(ks[4], (DEPTH, D_MODEL, 6 * D_MODEL), 0.5 * D_MODEL ** -0.5),
        "b_mod": nrm(ks[5], (DEPTH, 6 * D_MODEL), 0.02),
        "g_norm1": 1.0 + nrm(ks[6], (DEPTH, D_MODEL), 0.02),
        "g_norm2": 1.0 + nrm(ks[7], (DEPTH, D_MODEL), 0.02),
        "w_in": nrm(ks[8], (DEPTH, D_MODEL, IN_WIDTH), D_MODEL ** -0.5),
        "na_q_gain": 1.0 + nrm(ks[9], (DEPTH, HEAD_DIM), 0.02),
        "na_k_gain": 1.0 + nrm(ks[10], (DEPTH, HEAD_DIM), 0.02),
        "na_rpb": nrm(ks[11], (DEPTH, NA_HEADS, 2 * NA_WIN_ROWS - 1, 2 * NA_WIN_COLS - 1), 0.1),
        "conv_w": nrm(ks[12], (DEPTH, CONV_WIDTH, CONV_CH), CONV_WIDTH ** -0.5),
        "conv_bias": nrm(ks[13], (DEPTH, CONV_CH), 0.02),
        "wa_q_gain": 1.0 + nrm(ks[14], (DEPTH, HEAD_DIM), 0.02),
        "wa_k_gain": 1.0 + nrm(ks[15], (DEPTH, HEAD_DIM), 0.02),
        "wa_sink": nrm(ks[16], (DEPTH, WA_HEADS), 0.5),
        "g_out": 1.0 + nrm(ks[17], (DEPTH, D_MIX), 0.02),
        "w_o": nrm(ks[18], (DEPTH, D_MIX, D_MODEL), D_MIX ** -0.5),
        "w_fc1": nrm(ks[19], (DEPTH, D_MODEL, FFN_DIM), D_MODEL ** -0.5),
        "w_fc2": nrm(ks[20], (DEPTH, FFN_DIM, D_MODEL), FFN_DIM ** -0.5),
    }


def reference(x, c, ctx, c_ctx, w_mod, b_mod, g_norm1, g_norm2, w_in, na_q_gain, na_k_gain, na_rpb,
              conv_w, conv_bias, wa_q_gain, wa_k_gain, wa_sink, g_out, w_o, w_fc1, w_fc2):
    S = x.shape[1]
    rows = S // GRID_W
    t = jnp.arange(S)
    row_pos = (t // GRID_W).astype(jnp.float32)
    col_pos = (t % GRID_W).astype(jnp.float32)
    for l in range(DEPTH):
        last = l == DEPTH - 1
        sh1, sc1, gt1, sh2, sc2, gt2 = [m[:, None, :] for m in modulation(c, w_mod[l], b_mod[l])]
        csh1, csc1, cgt1, csh2, csc2, cgt2 = modulation(c_ctx, w_mod[l], b_mod[l])

        h = rms_norm(x, g_norm1[l]) * (1 + sc1) + sh1
        hc = rms_norm(ctx, g_norm1[l]) * (1 + csc1) + csh1
        na_q, na_k, na_v, cv_x, cv_b, cv_c, wa_q, wa_k, wa_v = split_proj(h @ w_in[l])
        if last:
            na_kc, na_vc, wa_kc, wa_vc = [hc @ w_in[l][:, IN_OFF[i]:IN_OFF[i + 1]] for i in (1, 2, 7, 8)]
        else:
            na_qc, na_kc, na_vc, cv_xc, cv_bc, cv_cc, wa_qc, wa_kc, wa_vc = split_proj(hc @ w_in[l])

        na_kc = rms_norm(split_heads(na_kc, NA_HEADS), na_k_gain[l])
        na_vc = split_heads(na_vc, NA_HEADS)
        wa_kc = rms_norm(split_heads(wa_kc, WA_KV_HEADS), wa_k_gain[l])
        wa_vc = split_heads(wa_vc, WA_KV_HEADS)

        o_na = neighborhood_attention(
            rms_norm(split_heads(na_q, NA_HEADS), na_q_gain[l]),
            rms_norm(split_heads(na_k, NA_HEADS), na_k_gain[l]),
            split_heads(na_v, NA_HEADS), na_kc, na_vc, na_rpb[l], rows)
        o_cv = cv_b * short_conv(cv_c * cv_x, conv_w[l], conv_bias[l])
        o_wa = window_attention(
            axial_rope(rms_norm(split_heads(wa_q, WA_HEADS), wa_q_gain[l]), row_pos, col_pos),
            axial_rope(rms_norm(split_heads(wa_k, WA_KV_HEADS), wa_k_gain[l]), row_pos, col_pos),
            split_heads(wa_v, WA_KV_HEADS), wa_kc, wa_vc, wa_sink[l])
        x = x + gt1 * mix_out(o_na, o_cv, o_wa, g_out[l], w_o[l])

        if not last:
            oc_na = context_attention(rms_norm(split_heads(na_qc, NA_HEADS), na_q_gain[l]), na_kc, na_vc)
            oc_cv = cv_bc * short_conv(cv_cc * cv_xc, conv_w[l], conv_bias[l])
            oc_wa = context_attention(rms_norm(split_heads(wa_qc, WA_HEADS), wa_q_gain[l]), wa_kc, wa_vc,
                                      wa_sink[l])
            ctx = ctx + cgt1 * mix_out(oc_na, oc_cv, oc_wa, g_out[l], w_o[l])

        x = x + gt2 * sq_relu_mlp(rms_norm(x, g_norm2[l]) * (1 + sc2) + sh2, w_fc1[l], w_fc2[l])
        if not last:
            ctx = ctx + cgt2 * sq_relu_mlp(rms_norm(ctx, g_norm2[l]) * (1 + csc2) + csh2, w_fc1[l], w_fc2[l])
    return x
```

```python
import os
import numpy as np
from contextlib import ExitStack
import concourse.bass as bass
import concourse.mybir as mybir
from concourse.bass_utils import run_bass_kernel_spmd

F32 = mybir.dt.float32
BF16 = mybir.dt.bfloat16
ALU = mybir.AluOpType
AF = mybir.ActivationFunctionType
ENGS = ("pe", "act", "dve", "pool", "sp")
L = 2
T = 2304
NEG = -30000.0


class _Op:
    __slots__ = ("id", "eng", "fn", "deps", "dma", "cum", "pos", "waits", "signal", "snap")


class _Rec:
    def __getattr__(self, name):
        def f(*a, **k):
            self.call = (name, a, k)
            return self
        return f


class Prog:
    def __init__(self, nc, same_engine_sync=True):
        self.nc = nc
        self.same = same_engine_sync
        self.ops = []
        self.streams = {e: [] for e in ENGS}
        self.last_w = {}
        self.readers = {}
        self.dma_cum = {}

    def add(self, eng, fn, reads=(), writes=(), dma=None, extra_deps=()):
        op = _Op()
        op.id = len(self.ops)
        rec = _Rec()
        fn(rec)
        op.eng, op.fn, op.dma = eng, rec.call, dma
        op.signal = False
        op.waits = []
        deps = set(extra_deps)
        for k in reads:
            w = self.last_w.get(k)
            if w is not None:
                deps.add(w)
        for k in writes:
            w = self.last_w.get(k)
            if w is not None:
                deps.add(w)
            for r in self.readers.get(k, {}).values():
                deps.update(r)
        deps.discard(op.id)
        op.deps = sorted(deps)
        for k in reads:
            rd = self.readers.setdefault(k, {})
            if dma is not None:
                rd.setdefault("dma", []).append(op.id)
            else:
                rd[eng] = [op.id]
        for k in writes:
            self.last_w[k] = op.id
            self.readers[k] = {}
        if dma is not None:
            self.dma_cum[dma] = self.dma_cum.get(dma, 0) + 16
            op.cum = self.dma_cum[dma]
        op.pos = len(self.streams[eng])
        self.streams[eng].append(op)
        self.ops.append(op)
        return op

    def barrier(self):
        deps = []
        for e in ENGS:
            for o in reversed(self.streams[e]):
                if o.dma is None:
                    deps.append(o.id)
                    break
        lastb = getattr(self, "_lastb", 0)
        deps += [o.id for o in self.ops[lastb:] if o.dma is not None]
        self._lastb = len(self.ops)
        for e in ENGS:
            self.add(e, lambda g: g.nop(), extra_deps=deps)

    def plan(self):
        K = {e: {} for e in ENGS}
        for op in self.ops:
            E = op.eng
            k = K[E]
            dmax = {}
            for d in op.deps:
                dop = self.ops[d]
                if dop.dma is not None:
                    dmax[dop.dma] = max(dmax.get(dop.dma, 0), dop.cum)
            for d in op.deps:
                dop = self.ops[d]
                if dop.dma is not None:
                    key = ("dma", dop.dma)
                    if k.get(key, 0) < dmax[dop.dma]:
                        op.waits.append((key, dmax[dop.dma]))
                        k[key] = dmax[dop.dma]
                else:
                    F = dop.eng
                    if F == E and (not self.same or E == "pe"):
                        continue
                    if k.get(F, -1) >= dop.pos:
                        continue
                    op.waits.append((F, dop.pos))
                    dop.signal = True
                    k[F] = dop.pos
                for kk, vv in dop.snap.items():
                    if k.get(kk, -1) < vv:
                        k[kk] = vv
            op.snap = dict(k)
        self.cnt = {}
        for e in ENGS:
            c = 0
            for op in self.streams[e]:
                if op.dma is None and op.signal:
                    c += 1
                    self.cnt[(e, op.pos)] = c
        nw = sum(len(o.waits) for o in self.ops)
        print(f"[prog] ops={len(self.ops)} per-eng={ {e: len(s) for e, s in self.streams.items()} } waits={nw} "
              f"signals={ {e: sum(1 for o in s if o.signal) for e, s in self.streams.items()} }", flush=True)

    def emit(self, stack):
        nc = self.nc
        self.plan()
        esem = {e: stack.enter_context(nc.semaphore("s_" + e)) for e in ENGS}
        dsem = {k: stack.enter_context(nc.semaphore("d_" + str(k))) for k in self.dma_cum}
        block = stack.enter_context(nc.Block())

        def run(e, g):
            for op in self.streams[e]:
                for (key, val) in op.waits:
                    if isinstance(key, tuple):
                        g.wait_ge(dsem[key[1]], val)
                    else:
                        g.wait_ge(esem[key], self.cnt[(key, val)])
                nm, a_, k_ = op.fn
                ins = getattr(g, nm)(*a_, **k_)
                if op.dma is not None:
                    ins.then_inc(dsem[op.dma], 16)
                elif op.signal:
                    ins.then_inc(esem[e], 1)

        block.tensor(lambda g: run("pe", g))
        block.scalar(lambda g: run("act", g))
        block.vector(lambda g: run("dve", g))
        block.gpsimd(lambda g: run("pool", g))
        block.sync(lambda g: run("sp", g))


BLKS = [(0, 512), (512, 512), (1024, 512), (1536, 512), (2048, 256)]
PAIR_VAR = [0, 1] + [2] * 12 + [3, 4]
VAR_REP = [0, 1, 5, 14, 15]


def build(debug=None):
    nc = bass.Bass("TRN2", target_bir_lowering=False)
    di = lambda n, s: nc.dram_tensor(n, list(s), F32, kind="ExternalInput").ap()
    x_d = di("x", (2048, 1024))
    ctx_d = di("ctx", (256, 1024))
    cc_d = di("cc", (128, 16))
    wmod_d = di("w_mod", (L, 1024, 6144))
    bm_d = di("bm", (128, L * 48))
    vec_d = di("vecs", (128, L * 36))
    sink_d = di("sink", (128, L * 8))
    win_d = di("w_in", (L, 1024, 2304))
    wo_d = di("w_o", (L, 1024, 1024))
    fc1_d = di("w_fc1", (L, 1024, 4096))
    fc2_d = di("w_fc2", (L, 4096, 1024))
    nab_d = di("nabias", (L, 4, 128, 5 * 5 * 128))
    ident_d = di("ident", (128, 128))
    rot_d = di("rotT", (128, 128))
    wam_d = di("wamask", (128, 256))
    cos_d = di("ropecos", (128, 2048))
    sin_d = di("ropesin", (128, 2048))
    y_d = nc.dram_tensor("y", [2048, 1024], F32, kind="ExternalOutput").ap()
    dbg_d = {}
    if debug:
        for n, s in debug.items():
            dbg_d[n] = nc.dram_tensor("dbg_" + n, list(s), F32, kind="ExternalOutput").ap()

    st = ExitStack()
    P = Prog(nc)
    sb = lambda name, shape, dt: st.enter_context(nc.sbuf_tensor("sb_" + name, shape, dt))
    xT = sb("xT", [128, 8 * T], F32)
    xTv = xT[:, :].rearrange("p (k t) -> p k t", k=8)
    hT = sb("hT", [128, 8 * T], BF16)
    hTv = hT[:, :].rearrange("p (k t) -> p k t", k=8)
    ident = sb("ident", [128, 128], F32)
    ones = sb("ones", [128, 128], BF16)
    bd = sb("bd", [128, 128], BF16)
    rotT = sb("rotT", [128, 128], BF16)
    wam = sb("wam", [128, 256], BF16)
    epsc = sb("epsc", [128, 1], F32)
    cc = sb("cc", [128, 16], F32)
    ccb = sb("ccb", [128, 16], BF16)
    bm = sb("bm", [128, L * 48], F32)
    vec = sb("vec", [128, L * 36], F32)
    sinkv = sb("sinkv", [128, L * 8], F32)
    modT = sb("modT", [128, L * 96], F32)
    A12 = sb("A12", [128, L * 32], F32)
    gq8 = sb("gq8", [128, L * 2], F32)
    AR_F = 19328
    AR = sb("arena", [128, AR_F], F32)
    PS = [st.enter_context(nc.psum_tensor("ps%d" % i, [128, 1024], F32)) for i in range(4)]

    class Arena:
        def __init__(s):
            s.off = 0

        def reset(s):
            P.barrier()
            s.off = 0

        def f32(s, n):
            a = s.off
            s.off += n
            assert s.off <= AR_F, ("arena overflow", s.off)
            return AR[:, a:a + n]

        def bf(s, n):
            m = (n + 1) // 2
            a = s.off
            s.off += m
            assert s.off <= AR_F, ("arena overflow", s.off)
            return AR[:, a:a + m].bitcast(BF16)[:, 0:n]

    ar = Arena()
    uid = [0]

    def U(p="k"):
        uid[0] += 1
        return "%s%d" % (p, uid[0])

    def modcol(l, chunk, kc, s):
        i = l * 96 + (chunk * 8 + kc) * 2 + s
        return modT[:, i:i + 1]

    def Acol(l, which, kc, s):
        i = l * 32 + which * 16 + kc * 2 + s
        return A12[:, i:i + 1]

    def vcol(l, i):
        j = l * 36 + i
        return vec[:, j:j + 1]

    P.add("sp", lambda g: g.dma_start(out=ident[:, :], in_=ident_d), writes=["ident"], dma="c0ident")
    P.add("sp", lambda g: g.dma_start(out=cc[:, :], in_=cc_d), writes=["cc"], dma="c0cc")
    P.add("sp", lambda g: g.dma_start(out=bm[:, :], in_=bm_d), writes=["bm"], dma="c0bm")
    P.add("sp", lambda g: g.dma_start(out=vec[:, :], in_=vec_d), writes=["vec"], dma="c0vec")
    P.add("sp", lambda g: g.dma_start(out=sinkv[:, :], in_=sink_d), writes=["sinkv"], dma="c0sinkv")
    P.add("pool", lambda g: g.dma_start(out=rotT[:, :], in_=rot_d), writes=["rotT"], dma="c1rotT")
    P.add("pool", lambda g: g.dma_start(out=wam[:, :], in_=wam_d), writes=["wam"], dma="c1wam")
    P.add("pool", lambda g: g.memset(ones[:, :], 1.0), writes=["ones"])
    P.add("pool", lambda g: g.memset(bd[:, :], 0.0), writes=["bd"])
    P.add("pool", lambda g: g.memset(bd[0:64, 0:64], 1.0), reads=["bd"], writes=["bd"])
    P.add("pool", lambda g: g.memset(bd[64:128, 64:128], 1.0), reads=["bd"], writes=["bd"])
    P.add("pool", lambda g: g.memset(epsc[:, :], 1e-6), writes=["epsc"])
    P.add("act", lambda g: g.activation(out=ccb[:, :], in_=cc[:, :], func=AF.Silu), reads=["cc"], writes=["ccb"])
    P.add("act", lambda g: g.activation(out=sinkv[:, :], in_=sinkv[:, :], func=AF.Exp), reads=["sinkv"], writes=["sinkv"])
    for l in range(L):
        for i, c in enumerate((24, 26)):
            P.add("dve", lambda g, l=l, i=i, c=c: g.tensor_scalar(out=gq8[:, l * 2 + i:l * 2 + i + 1], in0=vcol(l, c), scalar1=0.125, scalar2=None, op0=ALU.mult),
                  reads=["vec"], writes=["gq8"])

    stg = [ar.f32(1024), ar.f32(1024)]
    for ti in range(18):
        s = stg[ti % 2]
        sk = "stg%d" % (ti % 2)
        src = x_d[ti * 128:(ti + 1) * 128, :] if ti < 16 else ctx_d[(ti - 16) * 128:(ti - 15) * 128, :]
        P.add("sp", lambda g, s=s, src=src: g.dma_start(out=s, in_=src), writes=[sk], dma=sk)
        pp = PS[ti % 2]
        pk = "ps%d" % (ti % 2)
        for kc in range(8):
            P.add("pe", lambda g, pp=pp, s=s, kc=kc: g.transpose(pp[:, kc * 128:(kc + 1) * 128], s[:, kc * 128:(kc + 1) * 128], ident[:, :]),
                  reads=[sk, "ident"], writes=[pk])
        for hf in range(2):
            eng = "act" if hf == 0 else "dve"
            outv = xTv[:, hf * 4:(hf + 1) * 4, ti * 128:(ti + 1) * 128]
            inv = pp[:, hf * 512:(hf + 1) * 512].rearrange("p (k t) -> p k t", k=4)
            wk = [("xT", kc, ti // 4 if ti < 16 else 4) for kc in range(hf * 4, hf * 4 + 4)]
            if eng == "act":
                P.add("act", lambda g, o=outv, i=inv: g.copy(out=o, in_=i), reads=[pk], writes=wk)
            else:
                P.add("dve", lambda g, o=outv, i=inv: g.tensor_copy(out=o, in_=i), reads=[pk], writes=wk)

    pp_i = [0]
    mod_cnt = [0]
    mod_q = {"pieces": [], "inflight": []}

    def mod_pieces(l, chunks, bufs):
        out = []
        mod_q["l"], mod_q["chunks"] = l, list(chunks)
        mod_q["need_evac"] = True
        for ch in chunks:
            for q4 in range(2):
                def dma_fn(ch=ch, q4=q4, st_={}):
                    bi_ = mod_cnt[0] % 2
                    mod_cnt[0] += 1
                    st_["b"] = bi_
                    wv = bufs[bi_].rearrange("p (k n) -> p k n", k=8)
                    c0 = ch * 1024 + q4 * 512
                    src = wmod_d[l].rearrange("(k p) n -> p k n", p=128)[:, :, c0:c0 + 512]
                    P.add("pool", lambda g: g.dma_start(out=wv, in_=src), writes=["wmb%d" % bi_] + list(mod_q.get("alias%d" % bi_, [])), dma="wmb%d" % bi_)
                    return st_

                def mm_fn(st_, ch=ch, q4=q4):
                    bi_ = st_["b"]
                    wk = "wmb%d" % bi_
                    wv = bufs[bi_].rearrange("p (k n) -> p k n", k=8)
                    for j in range(4):
                        nt = ch * 8 + q4 * 4 + j
                        for kc in range(8):
                            P.add("pe", lambda g: g.matmul(PS[3][:, 512 + nt * 2:512 + nt * 2 + 2], wv[:, kc, j * 128:(j + 1) * 128], ccb[:, kc * 2:kc * 2 + 2], start=(kc == 0), stop=(kc == 7)),
                                  reads=[wk, "ccb"], writes=["ps3b"])
                out.append((dma_fn, mm_fn))
        return out


    def mod_drain(k):
        for st_, m in mod_q["inflight"]:
            m(st_)
        mod_q["inflight"] = []
        while mod_q["pieces"] and len(mod_q["inflight"]) < 2:
            d, m = mod_q["pieces"].pop(0)
            mod_q["inflight"].append((d(), m))

    def mod_flush():
        if not mod_q.get("need_evac"):
            return
        mod_q["need_evac"] = False
        while mod_q["pieces"] or mod_q["inflight"]:
            mod_drain(1)
        l, chunks = mod_q["l"], mod_q["chunks"]
        c0, c1 = chunks[0], chunks[-1] + 1
        P.add("dve", lambda g: g.tensor_tensor(out=modT[:, l * 96 + c0 * 16:l * 96 + c1 * 16].rearrange("p (c s) -> p c s", s=2), in0=PS[3][:, 512 + c0 * 16:512 + c1 * 16].rearrange("p (c s) -> p c s", s=2),
                                               in1=bm[:, l * 48 + c0 * 8:l * 48 + c1 * 8].unsqueeze(2).to_broadcast([128, (c1 - c0) * 8, 2]), op=ALU.add),
              reads=["ps3b", "bm"], writes=["modT"])
        for ch in chunks:
            if ch in (1, 4):
                which = 0 if ch == 1 else 1
                gcol = 0 if ch == 1 else 8
                P.add("dve", lambda g: g.scalar_tensor_tensor(
                    out=A12[:, l * 32 + which * 16:l * 32 + which * 16 + 16].rearrange("p (k s) -> p k s", s=2),
                    in0=modT[:, l * 96 + ch * 16:l * 96 + ch * 16 + 16].rearrange("p (k s) -> p k s", s=2), scalar=1.0,
                    in1=vec[:, l * 36 + gcol:l * 36 + gcol + 8].unsqueeze(2).to_broadcast([128, 8, 2]), op0=ALU.add, op1=ALU.mult),
                    reads=["modT", "vec"], writes=["A12"])

    wmb0 = [ar.bf(8 * 512), ar.bf(8 * 512)]
    mod_q["pieces"] = mod_pieces(0, [0, 1], wmb0)
    mod_flush()

    def norm_all(l, which, blks, hook=None):
        sq = [ar.bf(512), ar.bf(512), ar.bf(512)]
        lnv = ar.f32(512)
        rstd = [ar.f32(512), ar.f32(512)]
        tmp = [ar.f32(512), ar.f32(512)]
        shc = 0 if which == 0 else 3
        for bi in blks:
            t0, n = BLKS[bi]
            s = 1 if bi == 4 else 0
            pst = PS[3]
            for kc in range(8):
                q = sq[kc % 3]
                qk = "nsq%d" % (kc % 3)
                if kc % 8 in (1, 4, 6):
                    P.add("act", lambda g, q=q, kc=kc: g.activation(out=q[:, 0:n], in_=xTv[:, kc, t0:t0 + n], func=AF.Square), reads=[("xT", kc, bi)], writes=[qk])
                elif kc % 8 in (3, 7):
                    P.add("dve", lambda g, q=q, kc=kc: g.tensor_tensor(out=q[:, 0:n], in0=xTv[:, kc, t0:t0 + n], in1=xTv[:, kc, t0:t0 + n], op=ALU.mult),
                          reads=[("xT", kc, bi)], writes=[qk])
                else:
                    P.add("pool", lambda g, q=q, kc=kc: g.tensor_tensor(out=q[:, 0:n], in0=xTv[:, kc, t0:t0 + n], in1=xTv[:, kc, t0:t0 + n], op=ALU.mult),
                          reads=[("xT", kc, bi)], writes=[qk])
                P.add("pe", lambda g, q=q, kc=kc: g.matmul(pst[:, 0:n], ones[:, :], q[:, 0:n], start=(kc == 0), stop=(kc == 7)),
                      reads=[qk, "ones"], writes=["ps3a"])
            r = rstd[bi % 2]
            rk = "nrstd%d" % (bi % 2)
            P.add("act", lambda g: g.activation(out=lnv[:, 0:n], in_=pst[:, 0:n], func=AF.Ln, bias=epsc[:, 0:1], scale=1.0 / 1024), reads=["ps3a", "epsc"], writes=["nlnv"])
            P.add("act", lambda g, r=r: g.activation(out=r[:, 0:n], in_=lnv[:, 0:n], func=AF.Exp, scale=-0.5), reads=["nlnv"], writes=[rk])
            for kc in range(8):
                tm = tmp[kc % 2]
                tk = "ntmp%d" % (kc % 2)
                P.add("dve", lambda g, tm=tm, kc=kc, r=r: g.scalar_tensor_tensor(out=tm[:, 0:n], in0=xTv[:, kc, t0:t0 + n], scalar=Acol(l, which, kc, s), in1=r[:, 0:n], op0=ALU.mult, op1=ALU.mult),
                      reads=[("xT", kc, bi), rk, "A12"], writes=[tk])
                P.add("act", lambda g, tm=tm, kc=kc: g.activation(out=hTv[:, kc, t0:t0 + n], in_=tm[:, 0:n], func=AF.Identity, bias=modcol(l, shc, kc, s), scale=1.0),
                      reads=[tk, "modT"], writes=[("hT", kc, bi)])
            if hook is not None:
                hook()

    def proj(wv, col0, bi, ncols=128):
        t0, n = BLKS[bi]
        i = pp_i[0] % 4
        pp_i[0] += 1
        pp = PS[i // 2][:, (i % 2) * 512:(i % 2) * 512 + n]
        pk = "pp%d" % i
        for kc in range(8):
            P.add("pe", lambda g, pp=pp, kc=kc: g.matmul(pp, wv[:, kc, col0:col0 + 128], hTv[:, kc, t0:t0 + n], start=(kc == 0), stop=(kc == 7)),
                  reads=[("hT", kc, bi), "wb"], writes=[pk])
        return pp, pk

    hn_bufs = {}
    hn_single = [False]

    def headnorm(pp, pk, n, gain_ap, out_ap, out_keys, out_reads=()):
        if "sq" not in hn_bufs:
            if hn_single[0]:
                a_, b_ = ar.bf(512), ar.f32(512)
                hn_bufs["sq"] = [a_, a_]
                hn_bufs["r"] = [b_, b_]
            else:
                hn_bufs["sq"] = [ar.bf(512), ar.bf(512)]
                hn_bufs["r"] = [ar.f32(512), ar.f32(512)]
            hn_bufs["i"] = 0
        i = 0 if hn_single[0] else hn_bufs["i"] % 2
        hn_bufs["i"] += 1
        q, r = hn_bufs["sq"][i], hn_bufs["r"][i]
        qk, rk = "hsq%d" % i, "hr%d" % i
        pst = PS[3][:, 512:512 + n]
        P.add("act", lambda g: g.activation(out=q[:, 0:n], in_=pp, func=AF.Square), reads=[pk], writes=[qk])
        P.add("pe", lambda g: g.matmul(pst, bd[:, :], q[:, 0:n], start=True, stop=True), reads=[qk, "bd"], writes=["ps3b"])
        P.add("act", lambda g: g.activation(out=r[:, 0:n], in_=pst, func=AF.Ln, bias=epsc[:, 0:1], scale=1.0 / 64), reads=["ps3b", "epsc"], writes=[rk])
        P.add("act", lambda g: g.activation(out=r[:, 0:n], in_=r[:, 0:n], func=AF.Exp, scale=-0.5), reads=[rk], writes=[rk])
        if len(out_ap.shape) == 2:
            i0, i1 = pp, r[:, 0:n]
        else:
            i0, i1 = pp.rearrange("p (a b) -> p a b", b=128), r[:, 0:n].rearrange("p (a b) -> p a b", b=128)
        P.add("dve", lambda g: g.scalar_tensor_tensor(out=out_ap, in0=i0, scalar=gain_ap, in1=i1, op0=ALU.mult, op1=ALU.mult),
              reads=[pk, rk, "vec", "gq8"] + list(out_reads), writes=out_keys)

    def vproj(wv, col0, ncols, bi, dst_fn):
        t0, n = BLKS[bi]
        for tt in range(n // 128):
            tile = (t0 // 128) + tt
            i = pp_i[0] % 4
            pp_i[0] += 1
            pp = PS[i // 2][:, (i % 2) * 512:(i % 2) * 512 + ncols]
            pk = "pp%d" % i
            for kc in range(8):
                P.add("pe", lambda g, pp=pp, kc=kc, tile=tile: g.matmul(pp, hTv[:, kc, tile * 128:(tile + 1) * 128], wv[:, kc, col0:col0 + ncols], start=(kc == 0), stop=(kc == 7)),
                      reads=[("hT", kc, bi), "wb"], writes=[pk])
            dst_fn(tile, pp, pk)

    def group_out(l, o32, okey, ntl, n, bi, gcol0, wov, nch):
        t0, _ = BLKS[bi]
        s = 1 if bi == 4 else 0
        okeys = list(okey) if isinstance(okey, list) else [okey]
        sq = [ar_go["sq"][0], ar_go["sq"][1]]
        r = ar_go["r"]
        onb = ar_go["onb"]
        pst = PS[3][:, 0:n]
        for t in range(ntl):
            P.add("pool", lambda g, t=t: g.tensor_tensor(out=sq[t % 2][:, 0:n], in0=o32[:, t, :], in1=o32[:, t, :], op=ALU.mult), reads=okeys, writes=["gsq%d" % (t % 2)])
            P.add("pe", lambda g, t=t: g.matmul(pst, ones[:, :], sq[t % 2][:, 0:n], start=(t == 0), stop=(t == ntl - 1)), reads=["gsq%d" % (t % 2), "ones"], writes=["ps3a"])
        P.add("act", lambda g: g.activation(out=r[:, 0:n], in_=pst, func=AF.Ln, bias=epsc[:, 0:1], scale=1.0 / nch), reads=["ps3a", "epsc"], writes=["gr"])
        P.add("act", lambda g: g.activation(out=r[:, 0:n], in_=r[:, 0:n], func=AF.Exp, scale=-0.5), reads=["gr"], writes=["gr"])
        for t in range(ntl):
            P.add("dve", lambda g, t=t: g.scalar_tensor_tensor(out=onb[:, t * 512:t * 512 + n], in0=o32[:, t, :], scalar=vcol(l, 16 + gcol0 + t), in1=r[:, 0:n], op0=ALU.mult, op1=ALU.mult),
                  reads=okeys + ["gr", "vec"], writes=[("onb", t)])
        for nn in range(8):
            i = pp_i[0] % 4
            pp_i[0] += 1
            pp = PS[i // 2][:, (i % 2) * 512:(i % 2) * 512 + n]
            pk = "pp%d" % i
            for t in range(ntl):
                P.add("pe", lambda g, pp=pp, t=t, nn=nn: g.matmul(pp, wov[:, t, nn * 128:(nn + 1) * 128], onb[:, t * 512:t * 512 + n], start=(t == 0), stop=(t == ntl - 1)),
                      reads=[("onb", t), "wo"], writes=[pk])
            P.add("dve", lambda g, pp=pp, nn=nn: g.scalar_tensor_tensor(out=xTv[:, nn, t0:t0 + n], in0=pp, scalar=modcol(l, 2, nn, s), in1=xTv[:, nn, t0:t0 + n], op0=ALU.mult, op1=ALU.add),
                  reads=[pk, "modT", ("xT", nn, bi)], writes=[("xT", nn, bi)])

    ar_go = {}

    def go_alloc():
        ar_go["sq"] = [ar.bf(512), ar.bf(512)]
        ar_go["r"] = ar.f32(512)
        ar_go["onb"] = ar.bf(4 * 512)

    def load_w(dst_view, src, key="wb"):
        P.add("pool", lambda g: g.dma_start(out=dst_view, in_=src), writes=[key], dma=key)

    def dump(name, ap, reads=()):
        if name in dbg_d:
            P.barrier()
            P.add("pool", lambda g: g.dma_start(out=dbg_d[name], in_=ap), reads=list(reads), writes=[], dma="out_" + name)

    for l in range(L):
        last = (l == L - 1)
        lat = [0, 1, 2, 3]
        allb = [0, 1, 2, 3, 4]
        qb = lat if last else allb
        ar.reset()
        if l == 0:
            wmb1 = [ar.bf(8 * 512), ar.bf(8 * 512)]
            mod_q["pieces"] = mod_pieces(0, [2, 3, 4, 5], wmb1)
        norm_all(l, 0, allb, hook=(lambda: mod_drain(2)))
        mod_flush()
        if l == 0:
            dump("h0", hT[:, :])

        o_na = None
        for tpair in range(2):
            ar.reset()
            hn_bufs.clear()
            wbt = ar.bf(8 * 384)
            wv = wbt.rearrange("p (k n) -> p k n", k=8)
            wsrc = win_d[l].rearrange("(k p) n -> p k n", p=128)
            load_w(wv[:, :, 0:128], wsrc[:, :, tpair * 128:(tpair + 1) * 128])
            load_w(wv[:, :, 128:256], wsrc[:, :, 256 + tpair * 128:256 + (tpair + 1) * 128])
            load_w(wv[:, :, 256:384], wsrc[:, :, 512 + tpair * 128:512 + (tpair + 1) * 128])
            nak = ar.bf(T)
            vna = ar.bf(18 * 192)
            vnav = vna.rearrange("p (t c) -> p t c", c=192)
            naq = [ar.bf(512), ar.bf(512)]
            Eint = ar.f32(2 * 640)
            Eedge = [ar.f32(640), ar.f32(640)]
            ptmp = [ar.f32(640) for _ in range(4)]
            PT = [ar.bf(896) for _ in range(4)]
            rec = [ar.f32(256) for _ in range(4)]
            ona = AR[:, AR_F - 2 * T:AR_F]
            onav = ona.rearrange("p (t n) -> p t n", t=2)
            assert ar.off <= AR_F - 2 * T
            P.add("pool", lambda g: g.memset(vnav[:, :, 64:128], 1.0), writes=["vna"])
            for hh in range(2):
                h = tpair * 2 + hh
                src = nab_d[l, h].rearrange("p (v x) -> p v x", v=5)[:, 2, :]
                P.add("sp", lambda g, hh=hh, src=src: g.dma_start(out=Eint[:, hh * 640:(hh + 1) * 640], in_=src), writes=["Eint"], dma="Eint")
            P.add("act", lambda g: g.activation(out=Eint, in_=Eint, func=AF.Exp), reads=["Eint"], writes=["Eint"])
            for bi in allb:
                t0, n = BLKS[bi]
                pp, pk = proj(wv, 128, bi)
                headnorm(pp, pk, n, vcol(l, 25), nak[:, t0:t0 + n], [("nak", bi)])

                def vdst(tile, pp, pk):
                    P.add("act", lambda g: g.copy(out=vnav[:, tile, :].rearrange("p (a b) -> p a b", b=64)[:, 0:3:2, :], in_=pp.rearrange("p (a b) -> p a b", b=64)),
                          reads=[pk, "vna"], writes=[("vna", tile)])
                vproj(wv, 256, 128, bi, vdst)
            pend = []
            na_cnt = [0]
            for bi in qb:
                t0, n = BLKS[bi]
                if pend:
                    pend.pop()()
                q = naq[bi % 2]
                qk = "naq%d" % (bi % 2)
                pp, pk = proj(wv, 0, bi)
                headnorm(pp, pk, n, gq8[:, l * 2:l * 2 + 1], q[:, 0:n], [qk])
                if bi < 4:
                    for pi in range(4):
                        p = bi * 4 + pi
                        var = PAIR_VAR[p]
                        base = min(max(2 * p - 4, 0), 22)
                        st_i = na_cnt[0] % 2
                        na_cnt[0] += 1
                        ktiles = [base // 2 + c for c in range(5)] + [16, 17]
                        Es = []
                        for hh in range(2):
                            h = tpair * 2 + hh
                            if var == 2:
                                Es.append((Eint[:, hh * 640:(hh + 1) * 640], "Eint"))
                            else:
                                E = Eedge[hh]
                                Ek = "Eedge%d" % hh
                                src = nab_d[l, h].rearrange("p (v x) -> p v x", v=5)[:, var, :]
                                P.add("sp", lambda g: g.dma_start(out=E, in_=src), writes=[Ek], dma=Ek)
                                P.add("act", lambda g: g.activation(out=E, in_=E, func=AF.Exp), reads=[Ek], writes=[Ek])
                                Es.append((E, Ek))
                        scs = [(PS[2], ["ps2"]), (PS[1], ["pp2", "pp3"])]
                        for c, kt in enumerate(ktiles):
                            for hh in range(2):
                                lo, hi = hh * 64, hh * 64 + 64
                                sc, sck = scs[hh]
                                P.add("pe", lambda g: g.matmul(sc[:, c * 128:(c + 1) * 128], nak[lo:hi, kt * 128:(kt + 1) * 128], q[lo:hi, pi * 128:(pi + 1) * 128], start=True, stop=True),
                                      reads=[("nak", kt // 4 if kt < 16 else 4), qk], writes=sck)
                        for hh in range(2):
                            sc, sck = scs[hh]
                            E, Ek = Es[hh]
                            pt = ptmp[st_i * 2 + hh]
                            ptk = "ptmp%d" % (st_i * 2 + hh)
                            PTi = PT[st_i * 2 + hh]
                            PTk = "PT%d" % (st_i * 2 + hh)
                            P.add("act", lambda g: g.activation(out=pt[:, 0:512], in_=sc[:, 0:512], func=AF.Exp), reads=sck, writes=[ptk])
                            P.add("act", lambda g: g.activation(out=pt[:, 512:640], in_=sc[:, 512:640], func=AF.Exp), reads=sck + [ptk], writes=[ptk])
                            P.add("act", lambda g: g.activation(out=PTi[:, 640:896], in_=sc[:, 640:896], func=AF.Exp), reads=sck, writes=[PTk])
                            P.add("pool", lambda g: g.tensor_tensor(out=PTi[:, 0:640], in0=pt[:, 0:640], in1=E, op=ALU.mult), reads=[ptk, Ek, PTk], writes=[PTk])

                        def stageB(st_i=st_i, ktiles=ktiles, pi=pi, t0=t0, bi=bi):
                            for hh in range(2):
                                lo, hi = hh * 64, hh * 64 + 64
                                dlo, dhi = (64, 128) if hh == 0 else (0, 64)
                                vc0 = 0 if hh == 0 else 64
                                PTi = PT[st_i * 2 + hh]
                                PTk = "PT%d" % (st_i * 2 + hh)
                                pv = PS[3][:, 512 + hh * 128:512 + (hh + 1) * 128]
                                pvk = "pv%d" % hh
                                for c, kt in enumerate(ktiles):
                                    P.add("pe", lambda g: g.matmul(pv, vnav[:, kt, vc0:vc0 + 128], PTi[:, c * 128:(c + 1) * 128], start=(c == 0), stop=(c == 6)),
                                          reads=[("vna", kt), PTk], writes=[pvk, "ps3b"])
                                rc = rec[st_i * 2 + hh]
                                rck = "rec%d" % (st_i * 2 + hh)
                                P.add("dve", lambda g: g.reciprocal(out=rc[lo:hi, 0:128], in_=pv[dlo:dhi, :]), reads=[pvk, "ps3b"], writes=[rck])
                                P.add("dve", lambda g: g.tensor_tensor(out=onav[lo:hi, tpair, t0 + pi * 128:t0 + (pi + 1) * 128], in0=pv[lo:hi, :], in1=rc[lo:hi, 0:128], op=ALU.mult),
                                      reads=[pvk, "ps3b", rck], writes=[("ona", tpair, bi, hh)])
                        if pend:
                            pend.pop()()
                        pend.append(stageB)
                for hh in range(2):
                    h = tpair * 2 + hh
                    lo, hi = hh * 64, hh * 64 + 64
                    dlo, dhi = (64, 128) if hh == 0 else (0, 64)
                    vc0 = 0 if hh == 0 else 64
                    if bi < 4:
                        pass
                    else:
                        sc = PS[2]
                        sck = "ps2"
                        for c in range(2):
                            P.add("pe", lambda g, c=c: g.matmul(sc[:, c * 256:(c + 1) * 256], nak[lo:hi, 2048 + c * 128:2048 + (c + 1) * 128], q[lo:hi, 0:256], start=True, stop=True),
                                  reads=[("nak", 4), qk], writes=[sck])
                        PTi = PT[hh]
                        PTk = "PT%d" % hh
                        P.add("act", lambda g, PTi=PTi: g.activation(out=PTi[:, 0:512], in_=sc[:, 0:512], func=AF.Exp), reads=[sck], writes=[PTk])
                        pv = PS[3][:, 512 + hh * 256:512 + (hh + 1) * 256]
                        pvk = "pv%d" % hh
                        for c in range(2):
                            P.add("pe", lambda g, c=c, pv=pv, PTi=PTi: g.matmul(pv, vnav[:, 16 + c, vc0:vc0 + 128], PTi[:, c * 256:(c + 1) * 256], start=(c == 0), stop=(c == 1)),
                                  reads=[("vna", 16 + c), PTk], writes=[pvk, "ps3b"])
                        rc = rec[hh]
                        rck = "rec%d" % hh
                        P.add("dve", lambda g, rc=rc, pv=pv: g.reciprocal(out=rc[lo:hi, 0:256], in_=pv[dlo:dhi, :]), reads=[pvk, "ps3b"], writes=[rck])
                        P.add("dve", lambda g, rc=rc, pv=pv: g.tensor_tensor(out=onav[lo:hi, tpair, 2048:2304], in0=pv[lo:hi, :], in1=rc[lo:hi, 0:256], op=ALU.mult),
                              reads=[pvk, "ps3b", rck], writes=[("ona", tpair, bi, hh)])
            if pend:
                pend.pop()()
        if l == 0:
            dump("ona", ona)
        ar.reset()
        go_alloc()
        wo = ar.bf(2 * 1024)
        wov = wo.rearrange("p (t n) -> p t n", t=2)
        load_w(wov, wo_d[l].rearrange("(t p) n -> p t n", p=128)[:, 0:2, :], key="wo")
        assert ar.off <= AR_F - 2 * T
        for bi in qb:
            t0, n = BLKS[bi]
            okeys = [("ona", tp, bi, hh) for tp in range(2) for hh in range(2)]
            group_out(l, onav[:, :, t0:t0 + n], okeys, 2, n, bi, 0, wov, 256.0)
        if l == 0:
            dump("xna", xT[:, :])

        ar.reset()
        hn_bufs.clear()
        go_alloc()
        wbt = ar.bf(8 * 768)
        wv = wbt.rearrange("p (k n) -> p k n", k=8)
        load_w(wv, win_d[l].rearrange("(k p) n -> p k n", p=128)[:, :, 768:1536])
        wo = ar.bf(2 * 1024)
        wov = wo.rearrange("p (t n) -> p t n", t=2)
        load_w(wov, wo_d[l].rearrange("(t p) n -> p t n", p=128)[:, 2:4, :], key="wo")
        UW = 2308
        u = ar.f32(2 * UW)
        uv = u.rearrange("p (t n) -> p t n", t=2)
        bb = ar.f32(2 * T)
        bbv = bb.rearrange("p (t n) -> p t n", t=2)
        xs = [ar.f32(512), ar.f32(512)]
        yb = [ar.f32(1024), ar.f32(1024)]
        P.add("pool", lambda g: g.memset(u, 0.0), writes=["u"])
        cb = qb
        for bi in cb:
            t0, n = BLKS[bi]
            uo = 1 + t0 if bi < 4 else 2 + t0
            for ti in range(2):
                px, pxk = proj(wv, ti * 128, bi)
                xsi = xs[ti]
                P.add("act", lambda g, px=px, xsi=xsi: g.copy(out=xsi[:, 0:n], in_=px), reads=[pxk], writes=["xs%d" % ti])
                pc, pck = proj(wv, 512 + ti * 128, bi)
                P.add("dve", lambda g, pc=pc, xsi=xsi, ti=ti: g.tensor_tensor(out=uv[:, ti, uo:uo + n], in0=pc, in1=xsi[:, 0:n], op=ALU.mult),
                      reads=[pck, "xs%d" % ti, "u"], writes=[("u", ti, bi)])
                pb, pbk = proj(wv, 256 + ti * 128, bi)
                P.add("act", lambda g, pb=pb, ti=ti: g.copy(out=bbv[:, ti, t0:t0 + n], in_=pb), reads=[pbk], writes=[("bb", ti, bi)])
        pieces = [(0, 1024, [0, 1]), (1024, 1024, [2, 3])] + ([] if last else [(2048, 256, [4])])
        for ti in range(2):
            for pi_, (t0, n, bis) in enumerate(pieces):
                uo = 1 + t0 if t0 < 2048 else 2 + t0
                y = yb[pi_ % 2]
                yk = "yb%d" % (pi_ % 2)
                ukeys = [("u", ti, b) for b in range(5) if (b in cb)]
                P.add("act", lambda g, y=y, ti=ti, uo=uo, n=n: g.activation(out=y[:, 0:n], in_=uv[:, ti, uo:uo + n], func=AF.Identity, bias=vcol(l, 34 + ti), scale=vcol(l, 28 + ti * 3 + 1)),
                      reads=ukeys + ["vec", "u"], writes=[yk])
                P.add("dve", lambda g, y=y, ti=ti, uo=uo, n=n: g.scalar_tensor_tensor(out=y[:, 0:n], in0=uv[:, ti, uo - 1:uo - 1 + n], scalar=vcol(l, 28 + ti * 3 + 0), in1=y[:, 0:n], op0=ALU.mult, op1=ALU.add),
                      reads=ukeys + ["vec", yk, "u"], writes=[yk])
                P.add("dve", lambda g, y=y, ti=ti, uo=uo, n=n: g.scalar_tensor_tensor(out=y[:, 0:n], in0=uv[:, ti, uo + 1:uo + 1 + n], scalar=vcol(l, 28 + ti * 3 + 2), in1=y[:, 0:n], op0=ALU.mult, op1=ALU.add),
                      reads=ukeys + ["vec", yk, "u"], writes=[yk])
                P.add("pool", lambda g, y=y, ti=ti, t0=t0, n=n: g.tensor_tensor(out=bbv[:, ti, t0:t0 + n], in0=bbv[:, ti, t0:t0 + n], in1=y[:, 0:n], op=ALU.mult),
                      reads=[yk] + [("bb", ti, b) for b in bis], writes=[("bb", ti, b) for b in bis])
        if l == 0:
            dump("ocv", bb)
        for bi in cb:
            t0, n = BLKS[bi]
            group_out(l, bbv[:, :, t0:t0 + n], [("bb", 0, bi), ("bb", 1, bi)], 2, n, bi, 2, wov, 256.0)
        if l == 0:
            dump("xcv", xT[:, :])

        ar.reset()
        hn_bufs.clear()
        hn_single[0] = True
        go_alloc()
        wbt = ar.bf(8 * 768)
        wv = wbt.rearrange("p (k n) -> p k n", k=8)
        load_w(wv, win_d[l].rearrange("(k p) n -> p k n", p=128)[:, :, 1536:2304])
        wo = ar.bf(4 * 1024)
        wov = wo.rearrange("p (t n) -> p t n", t=4)
        load_w(wov, wo_d[l].rearrange("(t p) n -> p t n", p=128)[:, 4:8, :], key="wo")
        wak = ar.bf(T)
        vwa = ar.bf(18 * 192)
        vwav = vwa.rearrange("p (t c) -> p t c", c=192)
        waq = ar.bf(2048)
        waqv = waq.rearrange("p (a j q) -> p a j q", a=4, j=4)
        cosb = ar.f32(512)
        sinb = ar.f32(512)
        tn = ar.f32(512)
        tnb = ar.bf(512)
        t1 = ar.f32(512)
        PTv2 = [ar.bf(5 * 512).rearrange("p (c n) -> p c n", c=5) for _ in range(2)]
        owa = ar.f32(4 * 512)
        owav = owa.rearrange("p (j n) -> p j n", j=4)
        recw = ar.f32(512)
        t2 = recw
        wa_cnt = [0]
        sc_cnt = [0]
        P.add("pool", lambda g: g.memset(vwav[:, :, 64:128], 1.0), writes=["vwa"])

        def rope_to(pp, pk, n, gain_ap, t0, out3, okeys):
            headnorm(pp, pk, n, gain_ap, tn[:, 0:n], ["tn"])
            P.add("act", lambda g: g.copy(out=tnb[:, 0:n], in_=tn[:, 0:n]), reads=["tn"], writes=["tnb"])
            i = pp_i[0] % 4
            pp_i[0] += 1
            pr = PS[i // 2][:, (i % 2) * 512:(i % 2) * 512 + n]
            prk = "pp%d" % i
            P.add("pe", lambda g: g.matmul(pr, rotT[:, :], tnb[:, 0:n], start=True, stop=True), reads=["tnb", "rotT"], writes=[prk])
            P.add("dve", lambda g: g.tensor_tensor(out=t1[:, 0:n], in0=tn[:, 0:n], in1=cosb[:, 0:n], op=ALU.mult), reads=["tn", "cosb"], writes=["t1"])
            P.add("dve", lambda g: g.tensor_tensor(out=t2[:, 0:n], in0=pr, in1=sinb[:, 0:n], op=ALU.mult), reads=[prk, "sinb"], writes=["recw"])
            if len(out3.shape) == 2:
                P.add("pool", lambda g: g.tensor_tensor(out=out3, in0=t1[:, 0:n], in1=t2[:, 0:n], op=ALU.add), reads=["t1", "recw"], writes=okeys)
            else:
                P.add("pool", lambda g: g.tensor_tensor(out=out3, in0=t1[:, 0:n].rearrange("p (a b) -> p a b", b=128), in1=t2[:, 0:n].rearrange("p (a b) -> p a b", b=128), op=ALU.add),
                      reads=["t1", "recw"], writes=okeys)

        def load_rope(t0, n):
            P.add("sp", lambda g: g.dma_start(out=cosb[:, 0:n], in_=cos_d[:, t0:t0 + n]), writes=["cosb"], dma="cosb")
            P.add("sp", lambda g: g.dma_start(out=sinb[:, 0:n], in_=sin_d[:, t0:t0 + n]), writes=["sinb"], dma="sinb")

        for bi in allb:
            t0, n = BLKS[bi]
            pp, pk = proj(wv, 512, bi)
            if bi < 4:
                load_rope(t0, n)
                rope_to(pp, pk, n, vcol(l, 27), t0, wak[:, t0:t0 + n], [("wak", bi)])
            else:
                headnorm(pp, pk, n, vcol(l, 27), wak[:, t0:t0 + n], [("wak", bi)])

            def vdst(tile, pp, pk):
                P.add("act", lambda g: g.copy(out=vwav[:, tile, :].rearrange("p (a b) -> p a b", b=64)[:, 0:3:2, :], in_=pp.rearrange("p (a b) -> p a b", b=64)),
                      reads=[pk, "vwa"], writes=[("vwa", tile)])
            vproj(wv, 640, 128, bi, vdst)
        for bi in qb:
            t0, n = BLKS[bi]
            nq = n // 128
            if bi < 4:
                load_rope(t0, n)
            for j in range(4):
                pp, pk = proj(wv, j * 128, bi)
                if bi < 4:
                    rope_to(pp, pk, n, gq8[:, l * 2 + 1:l * 2 + 2], t0, waqv[:, 0:nq, j, :], [("waq", j)])
                else:
                    headnorm(pp, pk, n, gq8[:, l * 2 + 1:l * 2 + 2], waqv[:, 0:nq, j, :], [("waq", j)])
            pendw = []
            for a in range(nq):
                nb = (t0 // 128) + a
                for gi in range(2):
                    lo, hi = gi * 64, gi * 64 + 64
                    dlo, dhi = (64, 128) if gi == 0 else (0, 64)
                    vc0 = 0 if gi == 0 else 64
                    chunks = []
                    if bi < 4:
                        if nb > 0:
                            chunks.append((nb - 1, 0))
                        chunks.append((nb, None))
                        if nb < 15:
                            chunks.append((nb + 1, 1))
                    chunks += [(16, None), (17, None)]
                    rhs = waq[lo:hi, a * 512:(a + 1) * 512]
                    wi = wa_cnt[0] % 2
                    wa_cnt[0] += 1
                    PTc = PTv2[wi]
                    for ci, (kt, mk) in enumerate(chunks):
                        si = sc_cnt[0] % 2
                        sc_cnt[0] += 1
                        sc = PS[2][:, si * 512:si * 512 + 512]
                        sck = "sc%d" % si
                        P.add("pe", lambda g: g.matmul(sc, wak[lo:hi, kt * 128:(kt + 1) * 128], rhs, start=True, stop=True),
                              reads=[("wak", kt // 4 if kt < 16 else 4)] + [("waq", j) for j in range(4)], writes=[sck])
                        P.add("act", lambda g: g.activation(out=PTc[:, ci, :], in_=sc, func=AF.Exp), reads=[sck], writes=[("PTw", wi, ci)])
                        if mk is not None:
                            P.add("pool", lambda g: g.tensor_tensor(out=PTc[:, ci, :].rearrange("p (j q) -> p j q", j=4), in0=PTc[:, ci, :].rearrange("p (j q) -> p j q", j=4),
                                                                    in1=wam[:, mk * 128:(mk + 1) * 128].unsqueeze(1).to_broadcast([128, 4, 128]), op=ALU.mult),
                                  reads=[("PTw", wi, ci), "wam"], writes=[("PTw", wi, ci)])

                    def stageBw(chunks=chunks, PTc=PTc, wi=wi, lo=lo, hi=hi, dlo=dlo, dhi=dhi, vc0=vc0, a=a, gi=gi):
                        pv = PS[3][:, 512:1024]
                        for ci, (kt, mk) in enumerate(chunks):
                            P.add("pe", lambda g: g.matmul(pv, vwav[:, kt, vc0:vc0 + 128], PTc[:, ci, :], start=(ci == 0), stop=(ci == len(chunks) - 1)),
                                  reads=[("vwa", kt), ("PTw", wi, ci)], writes=["ps3b"])
                        P.add("dve", lambda g: g.tensor_tensor(out=recw[lo:hi, :].rearrange("p (j q) -> p j q", j=4), in0=pv[dlo:dhi, :].rearrange("p (j q) -> p j q", j=4),
                                                               in1=sinkv[dlo:dhi, l * 8 + gi * 4:l * 8 + gi * 4 + 4].unsqueeze(2).to_broadcast([64, 4, 128]), op=ALU.add),
                              reads=["ps3b", "sinkv"], writes=["recw"])
                        P.add("dve", lambda g: g.reciprocal(out=recw[lo:hi, :], in_=recw[lo:hi, :]), reads=["recw"], writes=["recw"])
                        P.add("dve", lambda g: g.tensor_tensor(out=owav[lo:hi, :, a * 128:(a + 1) * 128], in0=pv[lo:hi, :].rearrange("p (j q) -> p j q", j=4), in1=recw[lo:hi, :].rearrange("p (j q) -> p j q", j=4), op=ALU.mult),
                              reads=["ps3b", "recw"], writes=["owa"])
                    if pendw:
                        pendw.pop()()
                    pendw.append(stageBw)
            if pendw:
                pendw.pop()()
            if l == 0 and bi == 0:
                dump("owa0", owa)
            group_out(l, owav[:, :, 0:n], "owa", 4, n, bi, 4, wov, 512.0)
        if l == 0:
            dump("xmid", xT[:, :])

        hn_single[0] = False
        ar.reset()
        mb = lat if last else allb
        norm_all(l, 1, mb)
        f1 = [ar.bf(8 * 512), ar.bf(8 * 512)]
        f2 = [ar.bf(4 * 1024), ar.bf(4 * 1024)]
        HT = ar.bf(4 * T)
        HTv = HT.rearrange("p (i t) -> p i t", i=4)
        rl = [ar.f32(512), ar.f32(512)]
        if l + 1 < L:
            wmbA = AR[:, 0:2048].bitcast(BF16)
            wmbB = ar.bf(8 * 512)
            mod_q["alias0"] = ["nsq0", "nsq1", "nsq2", "nlnv", "nrstd0", "nrstd1", "ntmp0", "ntmp1"]
            mod_q["alias1"] = []
            mod_cnt[0] = 0
            mod_q["pieces"] = mod_pieces(l + 1, [0, 1, 2, 3, 4, 5], [wmbA, wmbB])
        for hc in range(8):
            f1v = f1[hc % 2].rearrange("p (k n) -> p k n", k=8)
            f2v = f2[hc % 2].rearrange("p (i n) -> p i n", i=4)
            load_w(f1v, fc1_d[l].rearrange("(k p) n -> p k n", p=128)[:, :, hc * 512:(hc + 1) * 512], key="f1_%d" % (hc % 2))
            load_w(f2v, fc2_d[l].rearrange("(i p) n -> p i n", p=128)[:, hc * 4:(hc + 1) * 4, :], key="f2_%d" % (hc % 2))
            mod_drain(2)
            for bi in mb:
                t0, n = BLKS[bi]
                for i4 in range(4):
                    i = pp_i[0] % 4
                    pp_i[0] += 1
                    pp = PS[i // 2][:, (i % 2) * 512:(i % 2) * 512 + n]
                    pk = "pp%d" % i
                    for kc in range(8):
                        P.add("pe", lambda g, pp=pp, kc=kc, i4=i4, f1v=f1v: g.matmul(pp, f1v[:, kc, i4 * 128:(i4 + 1) * 128], hTv[:, kc, t0:t0 + n], start=(kc == 0), stop=(kc == 7)),
                              reads=[("hT", kc, bi), "f1_%d" % (hc % 2)], writes=[pk])
                    r = rl[i4 % 2]
                    rk = "rl%d" % (i4 % 2)
                    P.add("act", lambda g, pp=pp, r=r: g.activation(out=r[:, 0:n], in_=pp, func=AF.Relu), reads=[pk], writes=[rk])
                    P.add("pool", lambda g, r=r, i4=i4: g.tensor_tensor(out=HTv[:, i4, t0:t0 + n], in0=r[:, 0:n], in1=r[:, 0:n], op=ALU.mult), reads=[rk], writes=[("HT", i4, bi)])
            for bi in mb:
                t0, n = BLKS[bi]
                s = 1 if bi == 4 else 0
                for nn in range(8):
                    i = pp_i[0] % 4
                    pp_i[0] += 1
                    pp = PS[i // 2][:, (i % 2) * 512:(i % 2) * 512 + n]
                    pk = "pp%d" % i
                    for i4 in range(4):
                        P.add("pe", lambda g, pp=pp, nn=nn, i4=i4, f2v=f2v: g.matmul(pp, f2v[:, i4, nn * 128:(nn + 1) * 128], HTv[:, i4, t0:t0 + n], start=(i4 == 0), stop=(i4 == 3)),
                              reads=[("HT", i4, bi), "f2_%d" % (hc % 2)], writes=[pk])
                    P.add("dve", lambda g, pp=pp, nn=nn: g.scalar_tensor_tensor(out=xTv[:, nn, t0:t0 + n], in0=pp, scalar=modcol(l, 5, nn, s), in1=xTv[:, nn, t0:t0 + n], op0=ALU.mult, op1=ALU.add),
                          reads=[pk, "modT", ("xT", nn, bi)], writes=[("xT", nn, bi)])
        mod_flush()
        mod_q["alias0"] = []

    ar.reset()
    stg = [ar.f32(1024), ar.f32(1024)]
    for ti in range(16):
        s = stg[ti % 2]
        sk = "ostg%d" % (ti % 2)
        pp = PS[ti % 2]
        pk = "ps%d" % (ti % 2)
        for kc in range(8):
            P.add("pe", lambda g, pp=pp, kc=kc, ti=ti: g.transpose(pp[:, kc * 128:(kc + 1) * 128], xTv[:, kc, ti * 128:(ti + 1) * 128], ident[:, :]),
                  reads=[("xT", kc, ti // 4), "ident"], writes=[pk])
        P.add("act", lambda g, s=s, pp=pp: g.copy(out=s[:, 0:512], in_=pp[:, 0:512]), reads=[pk], writes=[sk + "a"])
        P.add("dve", lambda g, s=s, pp=pp: g.tensor_copy(out=s[:, 512:1024], in_=pp[:, 512:1024]), reads=[pk], writes=[sk + "b"])
        P.add("sp", lambda g, s=s, ti=ti: g.dma_start(out=y_d[ti * 128:(ti + 1) * 128, :], in_=s), reads=[sk + "a", sk + "b"], dma="out%d" % (ti % 2))
    P.add("sp", lambda g: g.nop(), extra_deps=[o.id for o in P.ops if o.dma is not None and str(o.dma).startswith("out")])
    P.emit(st)
    st.close()
    return nc


def _host_consts():
    import ml_dtypes
    ident = np.eye(128, dtype=np.float32)
    R = np.zeros((64, 64), np.float32)
    for f in range(64):
        if (f % 32) < 16:
            R[f, f + 16] = -1.0
        else:
            R[f, f - 16] = 1.0
    R2 = np.zeros((128, 128), np.float32)
    R2[:64, :64] = R
    R2[64:, 64:] = R
    rotT = np.ascontiguousarray(R2.T)
    k = np.arange(128)[:, None]
    q = np.arange(128)[None, :]
    wam = np.concatenate([(k >= q).astype(np.float32), (k <= q).astype(np.float32)], axis=1)
    t = np.arange(2048)
    row = (t // 64).astype(np.float32)
    col = (t % 64).astype(np.float32)
    inv = (np.float32(10000.0) ** (-np.arange(16, dtype=np.float32) / np.float32(16))).astype(np.float32)
    ang = np.zeros((64, 2048), np.float32)
    for f in range(64):
        pos = row if f < 32 else col
        ang[f] = pos * inv[f % 16]
    cos = np.cos(ang).astype(np.float32)
    sin = np.sin(ang).astype(np.float32)
    return ident, rotT, wam, np.concatenate([cos, cos], 0), np.concatenate([sin, sin], 0)


def _nabias(rpb):
    out = np.full((L, 4, 128, 5, 5, 128), NEG, np.float32)
    jj = np.arange(2)[:, None, None, None]
    ck = np.arange(64)[None, :, None, None]
    rr = np.arange(2)[None, None, :, None]
    cq = np.arange(64)[None, None, None, :]
    for v, p in enumerate(VAR_REP):
        base = min(max(2 * p - 4, 0), 22)
        for c in range(5):
            rk = base + 2 * c + jj
            rq = 2 * p + rr
            start = np.clip(rq - 4, 0, 24)
            cs = np.clip(cq - 8, 0, 48)
            valid = (rk >= start) & (rk <= start + 7) & (rk <= 31) & (ck >= cs) & (ck < cs + 16)
            dr = np.clip(rk - rq + 7, 0, 14)
            dc = np.clip(ck - cq, -15, 15) + 15
            valid = np.broadcast_to(valid, (2, 64, 2, 64))
            drb = np.broadcast_to(dr, (2, 64, 2, 64))
            dcb = np.broadcast_to(dc, (2, 64, 2, 64))
            g = rpb[:, :, drb, dcb]
            g = np.where(valid[None, None], g, np.float32(NEG))
            out[:, :, :, v, c, :] = g.reshape(L, 4, 128, 128)
    return out.reshape(L, 4, 128, 5 * 5 * 128)


def _prep(inputs):
    f = lambda a: np.ascontiguousarray(np.asarray(a, dtype=np.float32))
    x, c, ctx, c_ctx = f(inputs["x"]), f(inputs["c"]), f(inputs["ctx"]), f(inputs["c_ctx"])
    w_in, w_o = f(inputs["w_in"]), f(inputs["w_o"])
    wa_q = 1536
    perm = list(range(0, 1536))
    for j in range(4):
        perm += list(range(wa_q + 64 * j, wa_q + 64 * j + 64)) + list(range(wa_q + 64 * (j + 4), wa_q + 64 * (j + 4) + 64))
    perm += list(range(2048, 2304))
    w_in_p = np.ascontiguousarray(w_in[:, :, perm])
    rperm = list(range(0, 512))
    for j in range(4):
        rperm += list(range(512 + 64 * j, 512 + 64 * j + 64)) + list(range(512 + 64 * (j + 4), 512 + 64 * (j + 4) + 64))
    w_o_p = np.ascontiguousarray(w_o[:, rperm, :])
    gout_p = f(inputs["g_out"])[:, rperm]
    col = lambda v: np.ascontiguousarray(v.reshape(-1, 128).T)
    vecs = np.zeros((128, L * 36), np.float32)
    for l in range(L):
        o = l * 36
        vecs[:, o:o + 8] = col(f(inputs["g_norm1"])[l])
        vecs[:, o + 8:o + 16] = col(f(inputs["g_norm2"])[l])
        vecs[:, o + 16:o + 24] = col(gout_p[l])
        for i, nm in enumerate(("na_q_gain", "na_k_gain", "wa_q_gain", "wa_k_gain")):
            vecs[:, o + 24 + i] = np.tile(f(inputs[nm])[l], 2)
        cw = f(inputs["conv_w"])[l]
        for ti in range(2):
            for tap in range(3):
                vecs[:, o + 28 + ti * 3 + tap] = cw[tap, ti * 128:(ti + 1) * 128]
            vecs[:, o + 34 + ti] = f(inputs["conv_bias"])[l, ti * 128:(ti + 1) * 128]
    bm = np.concatenate([col(f(inputs["b_mod"])[l]) for l in range(L)], axis=1)
    sink = np.ascontiguousarray(np.broadcast_to(f(inputs["wa_sink"]).reshape(1, L * 8), (128, L * 8)))
    ident, rotT, wam, cos, sin = _host_consts()
    nab = _nabias(f(inputs["na_rpb"]))
    shared = dict(w_mod=f(inputs["w_mod"]), bm=bm, vecs=vecs, sink=sink, w_in=w_in_p, w_o=w_o_p, w_fc1=f(inputs["w_fc1"]),
                  w_fc2=f(inputs["w_fc2"]), nabias=nab, ident=ident, rotT=rotT, wamask=wam, ropecos=cos, ropesin=sin)
    maps = []
    for b in range(8):
        ccm = np.zeros((128, 16), np.float32)
        ccm[:, 0::2] = c[b].reshape(8, 128).T
        ccm[:, 1::2] = c_ctx.reshape(8, 128).T
        m = dict(shared)
        m.update(x=x[b], ctx=ctx[b], cc=ccm)
        maps.append(m)
    return maps


_NC = {}


def kernel(**inputs):
    maps = _prep(inputs)
    if "nc" not in _NC:
        _NC["nc"] = build()
    res = run_bass_kernel_spmd(_NC["nc"], maps, core_ids=list(range(8)))
    return np.stack([np.asarray(r["y"], dtype=np.float32) for r in res.results], axis=0)
```

```python
import os
import numpy as np
from contextlib import ExitStack
import concourse.bass as bass
import concourse.mybir as mybir
from concourse.bass_utils import run_bass_kernel_spmd

F32 = mybir.dt.float32
BF16 = mybir.dt.bfloat16
ALU = mybir.AluOpType
AF = mybir.ActivationFunctionType
ENGS = ("pe", "act", "dve", "pool", "sp")
L = 2
T = 2304
NEG = -30000.0


class _Op:
    __slots__ = ("id", "eng", "fn", "deps", "dma", "cum", "pos", "waits", "signal", "snap")


class _Rec:
    def __getattr__(self, name):
        def f(*a, **k):
            self.call = (name, a, k)
            return self
        return f


class Prog:
    def __init__(self, nc, same_engine_sync=True):
        self.nc = nc
        self.same = same_engine_sync
        self.ops = []
        self.streams = {e: [] for e in ENGS}
        self.last_w = {}
        self.readers = {}
        self.dma_cum = {}

    def add(self, eng, fn, reads=(), writes=(), dma=None, extra_deps=()):
        op = _Op()
        op.id = len(self.ops)
        rec = _Rec()
        fn(rec)
        op.eng, op.fn, op.dma = eng, rec.call, dma
        op.signal = False
        op.waits = []
        deps = set(extra_deps)
        for k in reads:
            w = self.last_w.get(k)
            if w is not None:
                deps.add(w)
        for k in writes:
            w = self.last_w.get(k)
            if w is not None:
                deps.add(w)
            for r in self.readers.get(k, {}).values():
                deps.update(r)
        deps.discard(op.id)
        op.deps = sorted(deps)
        for k in reads:
            rd = self.readers.setdefault(k, {})
            if dma is not None:
                rd.setdefault("dma", []).append(op.id)
            else:
                rd[eng] = [op.id]
        for k in writes:
            self.last_w[k] = op.id
            self.readers[k] = {}
        if dma is not None:
            self.dma_cum[dma] = self.dma_cum.get(dma, 0) + 16
            op.cum = self.dma_cum[dma]
        op.pos = len(self.streams[eng])
        self.streams[eng].append(op)
        self.ops.append(op)
        return op

    def barrier(self):
        deps = []
        for e in ENGS:
            for o in reversed(self.streams[e]):
                if o.dma is None:
                    deps.append(o.id)
                    break
        lastb = getattr(self, "_lastb", 0)
        deps += [o.id for o in self.ops[lastb:] if o.dma is not None]
        self._lastb = len(self.ops)
        for e in ENGS:
            self.add(e, lambda g: g.nop(), extra_deps=deps)

    def plan(self):
        K = {e: {} for e in ENGS}
        for op in self.ops:
            E = op.eng
            k = K[E]
            dmax = {}
            for d in op.deps:
                dop = self.ops[d]
                if dop.dma is not None:
                    dmax[dop.dma] = max(dmax.get(dop.dma, 0), dop.cum)
            for d in op.deps:
                dop = self.ops[d]
                if dop.dma is not None:
                    key = ("dma", dop.dma)
                    if k.get(key, 0) < dmax[dop.dma]:
                        op.waits.append((key, dmax[dop.dma]))
                        k[key] = dmax[dop.dma]
                else:
                    F = dop.eng
                    if F == E and (not self.same or E == "pe"):
                        continue
                    if k.get(F, -1) >= dop.pos:
                        continue
                    op.waits.append((F, dop.pos))
                    dop.signal = True
                    k[F] = dop.pos
                for kk, vv in dop.snap.items():
                    if k.get(kk, -1) < vv:
                        k[kk] = vv
            op.snap = dict(k)
        self.cnt = {}
        for e in ENGS:
            c = 0
            for op in self.streams[e]:
                if op.dma is None and op.signal:
                    c += 1
                    self.cnt[(e, op.pos)] = c
        nw = sum(len(o.waits) for o in self.ops)
        print(f"[prog] ops={len(self.ops)} per-eng={ {e: len(s) for e, s in self.streams.items()} } waits={nw} "
              f"signals={ {e: sum(1 for o in s if o.signal) for e, s in self.streams.items()} }", flush=True)

    def emit(self, stack):
        nc = self.nc
        self.plan()
        esem = {e: stack.enter_context(nc.semaphore("s_" + e)) for e in ENGS}
        dsem = {k: stack.enter_context(nc.semaphore("d_" + str(k))) for k in self.dma_cum}
        block = stack.enter_context(nc.Block())

        def run(e, g):
            for op in self.streams[e]:
                for (key, val) in op.waits:
                    if isinstance(key, tuple):
                        g.wait_ge(dsem[key[1]], val)
                    else:
                        g.wait_ge(esem[key], self.cnt[(key, val)])
                nm, a_, k_ = op.fn
                ins = getattr(g, nm)(*a_, **k_)
                if op.dma is not None:
                    ins.then_inc(dsem[op.dma], 16)
                elif op.signal:
                    ins.then_inc(esem[e], 1)

        block.tensor(lambda g: run("pe", g))
        block.scalar(lambda g: run("act", g))
        block.vector(lambda g: run("dve", g))
        block.gpsimd(lambda g: run("pool", g))
        block.sync(lambda g: run("sp", g))


BLKS = [(0, 512), (512, 512), (1024, 512), (1536, 512), (2048, 256)]
PAIR_VAR = [0, 1] + [2] * 12 + [3, 4]
VAR_REP = [0, 1, 5, 14, 15]


def build(debug=None):
    nc = bass.Bass("TRN2", target_bir_lowering=False)
    di = lambda n, s: nc.dram_tensor(n, list(s), F32, kind="ExternalInput").ap()
    x_d = di("x", (2048, 1024))
    ctx_d = di("ctx", (256, 1024))
    cc_d = di("cc", (128, 16))
    wmod_d = di("w_mod", (L, 1024, 6144))
    bm_d = di("bm", (128, L * 48))
    vec_d = di("vecs", (128, L * 36))
    sink_d = di("sink", (128, L * 8))
    win_d = di("w_in", (L, 1024, 2304))
    wo_d = di("w_o", (L, 1024, 1024))
    fc1_d = di("w_fc1", (L, 1024, 4096))
    fc2_d = di("w_fc2", (L, 4096, 1024))
    nab_d = di("nabias", (L, 4, 128, 5 * 5 * 128))
    ident_d = di("ident", (128, 128))
    rot_d = di("rotT", (128, 128))
    wam_d = di("wamask", (128, 256))
    cos_d = di("ropecos", (128, 2048))
    sin_d = di("ropesin", (128, 2048))
    y_d = nc.dram_tensor("y", [2048, 1024], F32, kind="ExternalOutput").ap()
    dbg_d = {}
    if debug:
        for n, s in debug.items():
            dbg_d[n] = nc.dram_tensor("dbg_" + n, list(s), F32, kind="ExternalOutput").ap()

    st = ExitStack()
    P = Prog(nc)
    sb = lambda name, shape, dt: st.enter_context(nc.sbuf_tensor("sb_" + name, shape, dt))
    xT = sb("xT", [128, 8 * T], F32)
    xTv = xT[:, :].rearrange("p (k t) -> p k t", k=8)
    hT = sb("hT", [128, 8 * T], BF16)
    hTv = hT[:, :].rearrange("p (k t) -> p k t", k=8)
    ident = sb("ident", [128, 128], F32)
    ones = sb("ones", [128, 128], BF16)
    bd = sb("bd", [128, 128], BF16)
    rotT = sb("rotT", [128, 128], BF16)
    wam = sb("wam", [128, 256], BF16)
    epsc = sb("epsc", [128, 1], F32)
    cc = sb("cc", [128, 16], F32)
    ccb = sb("ccb", [128, 16], BF16)
    bm = sb("bm", [128, L * 48], F32)
    vec = sb("vec", [128, L * 36], F32)
    sinkv = sb("sinkv", [128, L * 8], F32)
    modT = sb("modT", [128, L * 96], F32)
    A12 = sb("A12", [128, L * 32], F32)
    gq8 = sb("gq8", [128, L * 2], F32)
    AR_F = 19328
    AR = sb("arena", [128, AR_F], F32)
    PS = [st.enter_context(nc.psum_tensor("ps%d" % i, [128, 1024], F32)) for i in range(4)]

    class Arena:
        def __init__(s):
            s.off = 0

        def reset(s):
            P.barrier()
            s.off = 0

        def f32(s, n):
            a = s.off
            s.off += n
            assert s.off <= AR_F, ("arena overflow", s.off)
            return AR[:, a:a + n]

        def bf(s, n):
            m = (n + 1) // 2
            a = s.off
            s.off += m
            assert s.off <= AR_F, ("arena overflow", s.off)
            return AR[:, a:a + m].bitcast(BF16)[:, 0:n]

    ar = Arena()
    uid = [0]

    def U(p="k"):
        uid[0] += 1
        return "%s%d" % (p, uid[0])

    def modcol(l, chunk, kc, s):
        i = l * 96 + (chunk * 8 + kc) * 2 + s
        return modT[:, i:i + 1]

    def Acol(l, which, kc, s):
        i = l * 32 + which * 16 + kc * 2 + s
        return A12[:, i:i + 1]

    def vcol(l, i):
        j = l * 36 + i
        return vec[:, j:j + 1]

    P.add("sp", lambda g: g.dma_start(out=ident[:, :], in_=ident_d), writes=["ident"], dma="c0ident")
    P.add("sp", lambda g: g.dma_start(out=cc[:, :], in_=cc_d), writes=["cc"], dma="c0cc")
    P.add("sp", lambda g: g.dma_start(out=bm[:, :], in_=bm_d), writes=["bm"], dma="c0bm")
    P.add("sp", lambda g: g.dma_start(out=vec[:, :], in_=vec_d), writes=["vec"], dma="c0vec")
    P.add("sp", lambda g: g.dma_start(out=sinkv[:, :], in_=sink_d), writes=["sinkv"], dma="c0sinkv")
    P.add("pool", lambda g: g.dma_start(out=rotT[:, :], in_=rot_d), writes=["rotT"], dma="c1rotT")
    P.add("pool", lambda g: g.dma_start(out=wam[:, :], in_=wam_d), writes=["wam"], dma="c1wam")
    P.add("pool", lambda g: g.memset(ones[:, :], 1.0), writes=["ones"])
    P.add("pool", lambda g: g.memset(bd[:, :], 0.0), writes=["bd"])
    P.add("pool", lambda g: g.memset(bd[0:64, 0:64], 1.0), reads=["bd"], writes=["bd"])
    P.add("pool", lambda g: g.memset(bd[64:128, 64:128], 1.0), reads=["bd"], writes=["bd"])
    P.add("pool", lambda g: g.memset(epsc[:, :], 1e-6), writes=["epsc"])
    P.add("act", lambda g: g.activation(out=ccb[:, :], in_=cc[:, :], func=AF.Silu), reads=["cc"], writes=["ccb"])
    P.add("act", lambda g: g.activation(out=sinkv[:, :], in_=sinkv[:, :], func=AF.Exp), reads=["sinkv"], writes=["sinkv"])
    for l in range(L):
        for i, c in enumerate((24, 26)):
            P.add("dve", lambda g, l=l, i=i, c=c: g.tensor_scalar(out=gq8[:, l * 2 + i:l * 2 + i + 1], in0=vcol(l, c), scalar1=0.125, scalar2=None, op0=ALU.mult),
                  reads=["vec"], writes=["gq8"])

    stg = [ar.f32(1024), ar.f32(1024)]
    for ti in range(18):
        s = stg[ti % 2]
        sk = "stg%d" % (ti % 2)
        src = x_d[ti * 128:(ti + 1) * 128, :] if ti < 16 else ctx_d[(ti - 16) * 128:(ti - 15) * 128, :]
        P.add("sp", lambda g, s=s, src=src: g.dma_start(out=s, in_=src), writes=[sk], dma=sk)
        pp = PS[ti % 2]
        pk = "ps%d" % (ti % 2)
        for kc in range(8):
            P.add("pe", lambda g, pp=pp, s=s, kc=kc: g.transpose(pp[:, kc * 128:(kc + 1) * 128], s[:, kc * 128:(kc + 1) * 128], ident[:, :]),
                  reads=[sk, "ident"], writes=[pk])
        for hf in range(2):
            eng = "act" if hf == 0 else "dve"
            outv = xTv[:, hf * 4:(hf + 1) * 4, ti * 128:(ti + 1) * 128]
            inv = pp[:, hf * 512:(hf + 1) * 512].rearrange("p (k t) -> p k t", k=4)
            wk = [("xT", kc, ti // 4 if ti < 16 else 4) for kc in range(hf * 4, hf * 4 + 4)]
            if eng == "act":
                P.add("act", lambda g, o=outv, i=inv: g.copy(out=o, in_=i), reads=[pk], writes=wk)
            else:
                P.add("dve", lambda g, o=outv, i=inv: g.tensor_copy(out=o, in_=i), reads=[pk], writes=wk)

    pp_i = [0]
    mod_cnt = [0]
    mod_q = {"pieces": [], "inflight": []}

    def mod_pieces(l, chunks, bufs):
        out = []
        mod_q["l"], mod_q["chunks"] = l, list(chunks)
        mod_q["need_evac"] = True
        for ch in chunks:
            for q4 in range(2):
                def dma_fn(ch=ch, q4=q4, st_={}):
                    bi_ = mod_cnt[0] % 2
                    mod_cnt[0] += 1
                    st_["b"] = bi_
                    wv = bufs[bi_].rearrange("p (k n) -> p k n", k=8)
                    c0 = ch * 1024 + q4 * 512
                    src = wmod_d[l].rearrange("(k p) n -> p k n", p=128)[:, :, c0:c0 + 512]
                    P.add("pool", lambda g: g.dma_start(out=wv, in_=src), writes=["wmb%d" % bi_] + list(mod_q.get("alias%d" % bi_, [])), dma="wmb%d" % bi_)
                    return st_

                def mm_fn(st_, ch=ch, q4=q4):
                    bi_ = st_["b"]
                    wk = "wmb%d" % bi_
                    wv = bufs[bi_].rearrange("p (k n) -> p k n", k=8)
                    for j in range(4):
                        nt = ch * 8 + q4 * 4 + j
                        for kc in range(8):
                            P.add("pe", lambda g: g.matmul(PS[3][:, 512 + nt * 2:512 + nt * 2 + 2], wv[:, kc, j * 128:(j + 1) * 128], ccb[:, kc * 2:kc * 2 + 2], start=(kc == 0), stop=(kc == 7)),
                                  reads=[wk, "ccb"], writes=["ps3b"])
                out.append((dma_fn, mm_fn))
        return out


    def mod_drain(k):
        for st_, m in mod_q["inflight"]:
            m(st_)
        mod_q["inflight"] = []
        while mod_q["pieces"] and len(mod_q["inflight"]) < 2:
            d, m = mod_q["pieces"].pop(0)
            mod_q["inflight"].append((d(), m))

    def mod_flush():
        if not mod_q.get("need_evac"):
            return
        mod_q["need_evac"] = False
        while mod_q["pieces"] or mod_q["inflight"]:
            mod_drain(1)
        l, chunks = mod_q["l"], mod_q["chunks"]
        c0, c1 = chunks[0], chunks[-1] + 1
        P.add("dve", lambda g: g.tensor_tensor(out=modT[:, l * 96 + c0 * 16:l * 96 + c1 * 16].rearrange("p (c s) -> p c s", s=2), in0=PS[3][:, 512 + c0 * 16:512 + c1 * 16].rearrange("p (c s) -> p c s", s=2),
                                               in1=bm[:, l * 48 + c0 * 8:l * 48 + c1 * 8].unsqueeze(2).to_broadcast([128, (c1 - c0) * 8, 2]), op=ALU.add),
              reads=["ps3b", "bm"], writes=["modT"])
        for ch in chunks:
            if ch in (1, 4):
                which = 0 if ch == 1 else 1
                gcol = 0 if ch == 1 else 8
                P.add("dve", lambda g: g.scalar_tensor_tensor(
                    out=A12[:, l * 32 + which * 16:l * 32 + which * 16 + 16].rearrange("p (k s) -> p k s", s=2),
                    in0=modT[:, l * 96 + ch * 16:l * 96 + ch * 16 + 16].rearrange("p (k s) -> p k s", s=2), scalar=1.0,
                    in1=vec[:, l * 36 + gcol:l * 36 + gcol + 8].unsqueeze(2).to_broadcast([128, 8, 2]), op0=ALU.add, op1=ALU.mult),
                    reads=["modT", "vec"], writes=["A12"])

    wmb0 = [ar.bf(8 * 512), ar.bf(8 * 512)]
    mod_q["pieces"] = mod_pieces(0, [0, 1], wmb0)
    mod_flush()

    def norm_all(l, which, blks, hook=None):
        sq = [ar.bf(512), ar.bf(512)]
        lnv = ar.f32(512)
        rstd = [ar.f32(512), ar.f32(512)]
        tmp = [ar.f32(512), ar.f32(512)]
        shc = 0 if which == 0 else 3
        for bi in blks:
            t0, n = BLKS[bi]
            s = 1 if bi == 4 else 0
            pst = PS[3]
            for kc in range(8):
                q = sq[kc % 2]
                qk = "nsq%d" % (kc % 2)
                P.add("pool", lambda g, q=q, kc=kc: g.tensor_tensor(out=q[:, 0:n], in0=xTv[:, kc, t0:t0 + n], in1=xTv[:, kc, t0:t0 + n], op=ALU.mult),
                      reads=[("xT", kc, bi)], writes=[qk])
                P.add("pe", lambda g, q=q, kc=kc: g.matmul(pst[:, 0:n], ones[:, :], q[:, 0:n], start=(kc == 0), stop=(kc == 7)),
                      reads=[qk, "ones"], writes=["ps3a"])
            r = rstd[bi % 2]
            rk = "nrstd%d" % (bi % 2)
            P.add("act", lambda g: g.activation(out=lnv[:, 0:n], in_=pst[:, 0:n], func=AF.Ln, bias=epsc[:, 0:1], scale=1.0 / 1024), reads=["ps3a", "epsc"], writes=["nlnv"])
            P.add("act", lambda g, r=r: g.activation(out=r[:, 0:n], in_=lnv[:, 0:n], func=AF.Exp, scale=-0.5), reads=["nlnv"], writes=[rk])
            for kc in range(8):
                tm = tmp[kc % 2]
                tk = "ntmp%d" % (kc % 2)
                P.add("dve", lambda g, tm=tm, kc=kc, r=r: g.scalar_tensor_tensor(out=tm[:, 0:n], in0=xTv[:, kc, t0:t0 + n], scalar=Acol(l, which, kc, s), in1=r[:, 0:n], op0=ALU.mult, op1=ALU.mult),
                      reads=[("xT", kc, bi), rk, "A12"], writes=[tk])
                P.add("act", lambda g, tm=tm, kc=kc: g.activation(out=hTv[:, kc, t0:t0 + n], in_=tm[:, 0:n], func=AF.Identity, bias=modcol(l, shc, kc, s), scale=1.0),
                      reads=[tk, "modT"], writes=[("hT", kc, bi)])
            if hook is not None:
                hook()

    def proj(wv, col0, bi, ncols=128, wkey="wb"):
        t0, n = BLKS[bi]
        i = pp_i[0] % 4
        pp_i[0] += 1
        pp = PS[i // 2][:, (i % 2) * 512:(i % 2) * 512 + n]
        pk = "pp%d" % i
        for kc in range(8):
            P.add("pe", lambda g, pp=pp, kc=kc: g.matmul(pp, wv[:, kc, col0:col0 + 128], hTv[:, kc, t0:t0 + n], start=(kc == 0), stop=(kc == 7)),
                  reads=[("hT", kc, bi), wkey], writes=[pk])
        return pp, pk

    hn_bufs = {}
    hn_single = [False]

    def headnorm(pp, pk, n, gain_ap, out_ap, out_keys, out_reads=()):
        if "sq" not in hn_bufs:
            if hn_single[0]:
                a_, b_ = ar.bf(512), ar.f32(512)
                hn_bufs["sq"] = [a_, a_]
                hn_bufs["r"] = [b_, b_]
            else:
                hn_bufs["sq"] = [ar.bf(512), ar.bf(512)]
                hn_bufs["r"] = [ar.f32(512), ar.f32(512)]
            hn_bufs["i"] = 0
        i = 0 if hn_single[0] else hn_bufs["i"] % 2
        hn_bufs["i"] += 1
        q, r = hn_bufs["sq"][i], hn_bufs["r"][i]
        qk, rk = "hsq%d" % i, "hr%d" % i
        pst = PS[3][:, 512:512 + n]
        P.add("act", lambda g: g.activation(out=q[:, 0:n], in_=pp, func=AF.Square), reads=[pk], writes=[qk])
        P.add("pe", lambda g: g.matmul(pst, bd[:, :], q[:, 0:n], start=True, stop=True), reads=[qk, "bd"], writes=["ps3b"])
        P.add("act", lambda g: g.activation(out=r[:, 0:n], in_=pst, func=AF.Ln, bias=epsc[:, 0:1], scale=1.0 / 64), reads=["ps3b", "epsc"], writes=[rk])
        P.add("act", lambda g: g.activation(out=r[:, 0:n], in_=r[:, 0:n], func=AF.Exp, scale=-0.5), reads=[rk], writes=[rk])
        if len(out_ap.shape) == 2:
            i0, i1 = pp, r[:, 0:n]
        else:
            i0, i1 = pp.rearrange("p (a b) -> p a b", b=128), r[:, 0:n].rearrange("p (a b) -> p a b", b=128)
        P.add("dve", lambda g: g.scalar_tensor_tensor(out=out_ap, in0=i0, scalar=gain_ap, in1=i1, op0=ALU.mult, op1=ALU.mult),
              reads=[pk, rk, "vec", "gq8"] + list(out_reads), writes=out_keys)

    def vproj(wv, col0, ncols, bi, dst_fn, wkey="wb"):
        t0, n = BLKS[bi]
        for tt in range(n // 128):
            tile = (t0 // 128) + tt
            i = pp_i[0] % 4
            pp_i[0] += 1
            pp = PS[i // 2][:, (i % 2) * 512:(i % 2) * 512 + ncols]
            pk = "pp%d" % i
            for kc in range(8):
                P.add("pe", lambda g, pp=pp, kc=kc, tile=tile: g.matmul(pp, hTv[:, kc, tile * 128:(tile + 1) * 128], wv[:, kc, col0:col0 + ncols], start=(kc == 0), stop=(kc == 7)),
                      reads=[("hT", kc, bi), wkey], writes=[pk])
            dst_fn(tile, pp, pk)

    def group_out(l, o32, okey, ntl, n, bi, gcol0, wov, nch):
        t0, _ = BLKS[bi]
        s = 1 if bi == 4 else 0
        okeys = list(okey) if isinstance(okey, list) else [okey]
        sq = [ar_go["sq"][0], ar_go["sq"][1]]
        r = ar_go["r"]
        onb = ar_go["onb"]
        pst = PS[3][:, 0:n]
        for t in range(ntl):
            P.add("pool", lambda g, t=t: g.tensor_tensor(out=sq[t % 2][:, 0:n], in0=o32[:, t, :], in1=o32[:, t, :], op=ALU.mult), reads=okeys, writes=["gsq%d" % (t % 2)])
            P.add("pe", lambda g, t=t: g.matmul(pst, ones[:, :], sq[t % 2][:, 0:n], start=(t == 0), stop=(t == ntl - 1)), reads=["gsq%d" % (t % 2), "ones"], writes=["ps3a"])
        P.add("act", lambda g: g.activation(out=r[:, 0:n], in_=pst, func=AF.Ln, bias=epsc[:, 0:1], scale=1.0 / nch), reads=["ps3a", "epsc"], writes=["gr"])
        P.add("act", lambda g: g.activation(out=r[:, 0:n], in_=r[:, 0:n], func=AF.Exp, scale=-0.5), reads=["gr"], writes=["gr"])
        for t in range(ntl):
            P.add("dve", lambda g, t=t: g.scalar_tensor_tensor(out=onb[:, t * 512:t * 512 + n], in0=o32[:, t, :], scalar=vcol(l, 16 + gcol0 + t), in1=r[:, 0:n], op0=ALU.mult, op1=ALU.mult),
                  reads=okeys + ["gr", "vec"], writes=[("onb", t)])
        for nn in range(8):
            i = pp_i[0] % 4
            pp_i[0] += 1
            pp = PS[i // 2][:, (i % 2) * 512:(i % 2) * 512 + n]
            pk = "pp%d" % i
            for t in range(ntl):
                P.add("pe", lambda g, pp=pp, t=t, nn=nn: g.matmul(pp, wov[:, t, nn * 128:(nn + 1) * 128], onb[:, t * 512:t * 512 + n], start=(t == 0), stop=(t == ntl - 1)),
                      reads=[("onb", t), "wo"], writes=[pk])
            P.add("dve", lambda g, pp=pp, nn=nn: g.scalar_tensor_tensor(out=xTv[:, nn, t0:t0 + n], in0=pp, scalar=modcol(l, 2, nn, s), in1=xTv[:, nn, t0:t0 + n], op0=ALU.mult, op1=ALU.add),
                  reads=[pk, "modT", ("xT", nn, bi)], writes=[("xT", nn, bi)])

    ar_go = {}

    def go_alloc():
        ar_go["sq"] = [ar.bf(512), ar.bf(512)]
        ar_go["r"] = ar.f32(512)
        ar_go["onb"] = ar.bf(4 * 512)

    def load_w(dst_view, src, key="wb"):
        P.add("pool", lambda g: g.dma_start(out=dst_view, in_=src), writes=[key], dma=key)

    def dump(name, ap, reads=()):
        if name in dbg_d:
            P.barrier()
            P.add("pool", lambda g: g.dma_start(out=dbg_d[name], in_=ap), reads=list(reads), writes=[], dma="out_" + name)

    for l in range(L):
        last = (l == L - 1)
        lat = [0, 1, 2, 3]
        allb = [0, 1, 2, 3, 4]
        qb = lat if last else allb
        ar.reset()
        if l == 0:
            wmb1 = [ar.bf(8 * 512), ar.bf(8 * 512)]
            mod_q["pieces"] = mod_pieces(0, [2, 3, 4, 5], wmb1)
        norm_all(l, 0, allb, hook=(lambda: mod_drain(2)))
        mod_flush()
        if l == 0:
            dump("h0", hT[:, :])

        o_na = None
        for tpair in range(2):
            ar.reset()
            hn_bufs.clear()
            wbt = ar.bf(8 * 384)
            wv = wbt.rearrange("p (k n) -> p k n", k=8)
            wsrc = win_d[l].rearrange("(k p) n -> p k n", p=128)
            load_w(wv[:, :, 0:128], wsrc[:, :, tpair * 128:(tpair + 1) * 128])
            load_w(wv[:, :, 128:256], wsrc[:, :, 256 + tpair * 128:256 + (tpair + 1) * 128])
            load_w(wv[:, :, 256:384], wsrc[:, :, 512 + tpair * 128:512 + (tpair + 1) * 128])
            nak = ar.bf(T)
            vna = ar.bf(18 * 192)
            vnav = vna.rearrange("p (t c) -> p t c", c=192)
            naq = [ar.bf(512), ar.bf(512)]
            Eint = ar.f32(2 * 640)
            Eedge = [ar.f32(640), ar.f32(640)]
            ptmp = [ar.f32(640) for _ in range(4)]
            PT = [ar.bf(896) for _ in range(4)]
            rec = [ar.f32(256) for _ in range(4)]
            ona = AR[:, AR_F - 2 * T:AR_F]
            onav = ona.rearrange("p (t n) -> p t n", t=2)
            assert ar.off <= AR_F - 2 * T
            P.add("pool", lambda g: g.memset(vnav[:, :, 64:128], 1.0), writes=["vna"])
            for hh in range(2):
                h = tpair * 2 + hh
                src = nab_d[l, h].rearrange("p (v x) -> p v x", v=5)[:, 2, :]
                P.add("sp", lambda g, hh=hh, src=src: g.dma_start(out=Eint[:, hh * 640:(hh + 1) * 640], in_=src), writes=["Eint"], dma="Eint")
            P.add("act", lambda g: g.activation(out=Eint, in_=Eint, func=AF.Exp), reads=["Eint"], writes=["Eint"])
            for bi in allb:
                t0, n = BLKS[bi]
                pp, pk = proj(wv, 128, bi)
                headnorm(pp, pk, n, vcol(l, 25), nak[:, t0:t0 + n], [("nak", bi)])

                def vdst(tile, pp, pk):
                    P.add("act", lambda g: g.copy(out=vnav[:, tile, :].rearrange("p (a b) -> p a b", b=64)[:, 0:3:2, :], in_=pp.rearrange("p (a b) -> p a b", b=64)),
                          reads=[pk, "vna"], writes=[("vna", tile)])
                vproj(wv, 256, 128, bi, vdst)
            pend = []
            na_cnt = [0]
            for bi in qb:
                t0, n = BLKS[bi]
                if pend:
                    pend.pop()()
                q = naq[bi % 2]
                qk = "naq%d" % (bi % 2)
                pp, pk = proj(wv, 0, bi)
                headnorm(pp, pk, n, gq8[:, l * 2:l * 2 + 1], q[:, 0:n], [qk])
                if bi < 4:
                    for pi in range(4):
                        p = bi * 4 + pi
                        var = PAIR_VAR[p]
                        base = min(max(2 * p - 4, 0), 22)
                        st_i = na_cnt[0] % 2
                        na_cnt[0] += 1
                        ktiles = [base // 2 + c for c in range(5)] + [16, 17]
                        Es = []
                        for hh in range(2):
                            h = tpair * 2 + hh
                            if var == 2:
                                Es.append((Eint[:, hh * 640:(hh + 1) * 640], "Eint"))
                            else:
                                E = Eedge[hh]
                                Ek = "Eedge%d" % hh
                                src = nab_d[l, h].rearrange("p (v x) -> p v x", v=5)[:, var, :]
                                P.add("sp", lambda g: g.dma_start(out=E, in_=src), writes=[Ek], dma=Ek)
                                P.add("act", lambda g: g.activation(out=E, in_=E, func=AF.Exp), reads=[Ek], writes=[Ek])
                                Es.append((E, Ek))
                        scs = [(PS[2], ["ps2"]), (PS[1], ["pp2", "pp3"])]
                        for c, kt in enumerate(ktiles):
                            for hh in range(2):
                                lo, hi = hh * 64, hh * 64 + 64
                                sc, sck = scs[hh]
                                P.add("pe", lambda g: g.matmul(sc[:, c * 128:(c + 1) * 128], nak[lo:hi, kt * 128:(kt + 1) * 128], q[lo:hi, pi * 128:(pi + 1) * 128], start=True, stop=True),
                                      reads=[("nak", kt // 4 if kt < 16 else 4), qk], writes=sck)
                        for hh in range(2):
                            sc, sck = scs[hh]
                            E, Ek = Es[hh]
                            pt = ptmp[st_i * 2 + hh]
                            ptk = "ptmp%d" % (st_i * 2 + hh)
                            PTi = PT[st_i * 2 + hh]
                            PTk = "PT%d" % (st_i * 2 + hh)
                            P.add("act", lambda g: g.activation(out=pt[:, 0:512], in_=sc[:, 0:512], func=AF.Exp), reads=sck, writes=[ptk])
                            P.add("act", lambda g: g.activation(out=pt[:, 512:640], in_=sc[:, 512:640], func=AF.Exp), reads=sck + [ptk], writes=[ptk])
                            P.add("act", lambda g: g.activation(out=PTi[:, 640:896], in_=sc[:, 640:896], func=AF.Exp), reads=sck, writes=[PTk])
                            P.add("pool", lambda g: g.tensor_tensor(out=PTi[:, 0:640], in0=pt[:, 0:640], in1=E, op=ALU.mult), reads=[ptk, Ek, PTk], writes=[PTk])

                        def stageB(st_i=st_i, ktiles=ktiles, pi=pi, t0=t0, bi=bi):
                            for hh in range(2):
                                lo, hi = hh * 64, hh * 64 + 64
                                dlo, dhi = (64, 128) if hh == 0 else (0, 64)
                                vc0 = 0 if hh == 0 else 64
                                PTi = PT[st_i * 2 + hh]
                                PTk = "PT%d" % (st_i * 2 + hh)
                                pv = PS[3][:, 512 + hh * 128:512 + (hh + 1) * 128]
                                pvk = "pv%d" % hh
                                for c, kt in enumerate(ktiles):
                                    P.add("pe", lambda g: g.matmul(pv, vnav[:, kt, vc0:vc0 + 128], PTi[:, c * 128:(c + 1) * 128], start=(c == 0), stop=(c == 6)),
                                          reads=[("vna", kt), PTk], writes=[pvk, "ps3b"])
                                rc = rec[st_i * 2 + hh]
                                rck = "rec%d" % (st_i * 2 + hh)
                                P.add("dve", lambda g: g.reciprocal(out=rc[lo:hi, 0:128], in_=pv[dlo:dhi, :]), reads=[pvk, "ps3b"], writes=[rck])
                                P.add("dve", lambda g: g.tensor_tensor(out=onav[lo:hi, tpair, t0 + pi * 128:t0 + (pi + 1) * 128], in0=pv[lo:hi, :], in1=rc[lo:hi, 0:128], op=ALU.mult),
                                      reads=[pvk, "ps3b", rck], writes=[("ona", tpair, bi, hh)])
                        if pend:
                            pend.pop()()
                        pend.append(stageB)
                for hh in range(2):
                    h = tpair * 2 + hh
                    lo, hi = hh * 64, hh * 64 + 64
                    dlo, dhi = (64, 128) if hh == 0 else (0, 64)
                    vc0 = 0 if hh == 0 else 64
                    if bi < 4:
                        pass
                    else:
                        sc = PS[2]
                        sck = "ps2"
                        for c in range(2):
                            P.add("pe", lambda g, c=c: g.matmul(sc[:, c * 256:(c + 1) * 256], nak[lo:hi, 2048 + c * 128:2048 + (c + 1) * 128], q[lo:hi, 0:256], start=True, stop=True),
                                  reads=[("nak", 4), qk], writes=[sck])
                        PTi = PT[hh]
                        PTk = "PT%d" % hh
                        P.add("act", lambda g, PTi=PTi: g.activation(out=PTi[:, 0:512], in_=sc[:, 0:512], func=AF.Exp), reads=[sck], writes=[PTk])
                        pv = PS[3][:, 512 + hh * 256:512 + (hh + 1) * 256]
                        pvk = "pv%d" % hh
                        for c in range(2):
                            P.add("pe", lambda g, c=c, pv=pv, PTi=PTi: g.matmul(pv, vnav[:, 16 + c, vc0:vc0 + 128], PTi[:, c * 256:(c + 1) * 256], start=(c == 0), stop=(c == 1)),
                                  reads=[("vna", 16 + c), PTk], writes=[pvk, "ps3b"])
                        rc = rec[hh]
                        rck = "rec%d" % hh
                        P.add("dve", lambda g, rc=rc, pv=pv: g.reciprocal(out=rc[lo:hi, 0:256], in_=pv[dlo:dhi, :]), reads=[pvk, "ps3b"], writes=[rck])
                        P.add("dve", lambda g, rc=rc, pv=pv: g.tensor_tensor(out=onav[lo:hi, tpair, 2048:2304], in0=pv[lo:hi, :], in1=rc[lo:hi, 0:256], op=ALU.mult),
                              reads=[pvk, "ps3b", rck], writes=[("ona", tpair, bi, hh)])
            if pend:
                pend.pop()()
        if l == 0:
            dump("ona", ona)
        ar.reset()
        go_alloc()
        wo = ar.bf(2 * 1024)
        wov = wo.rearrange("p (t n) -> p t n", t=2)
        load_w(wov, wo_d[l].rearrange("(t p) n -> p t n", p=128)[:, 0:2, :], key="wo")
        assert ar.off <= AR_F - 2 * T
        for bi in qb:
            t0, n = BLKS[bi]
            okeys = [("ona", tp, bi, hh) for tp in range(2) for hh in range(2)]
            group_out(l, onav[:, :, t0:t0 + n], okeys, 2, n, bi, 0, wov, 256.0)
        if l == 0:
            dump("xna", xT[:, :])

        ar.reset()
        hn_bufs.clear()
        go_alloc()
        wbt = ar.bf(8 * 768)
        wv = wbt.rearrange("p (k n) -> p k n", k=8)
        load_w(wv, win_d[l].rearrange("(k p) n -> p k n", p=128)[:, :, 768:1536])
        wo = ar.bf(2 * 1024)
        wov = wo.rearrange("p (t n) -> p t n", t=2)
        load_w(wov, wo_d[l].rearrange("(t p) n -> p t n", p=128)[:, 2:4, :], key="wo")
        UW = 2308
        u = ar.f32(2 * UW)
        uv = u.rearrange("p (t n) -> p t n", t=2)
        bb = ar.f32(2 * T)
        bbv = bb.rearrange("p (t n) -> p t n", t=2)
        xs = [ar.f32(512), ar.f32(512)]
        yb = [ar.f32(1024), ar.f32(1024)]
        P.add("pool", lambda g: g.memset(u, 0.0), writes=["u"])
        cb = qb
        for bi in cb:
            t0, n = BLKS[bi]
            uo = 1 + t0 if bi < 4 else 2 + t0
            for ti in range(2):
                px, pxk = proj(wv, ti * 128, bi)
                xsi = xs[ti]
                P.add("act", lambda g, px=px, xsi=xsi: g.copy(out=xsi[:, 0:n], in_=px), reads=[pxk], writes=["xs%d" % ti])
                pc, pck = proj(wv, 512 + ti * 128, bi)
                P.add("dve", lambda g, pc=pc, xsi=xsi, ti=ti: g.tensor_tensor(out=uv[:, ti, uo:uo + n], in0=pc, in1=xsi[:, 0:n], op=ALU.mult),
                      reads=[pck, "xs%d" % ti, "u"], writes=[("u", ti, bi)])
                pb, pbk = proj(wv, 256 + ti * 128, bi)
                P.add("act", lambda g, pb=pb, ti=ti: g.copy(out=bbv[:, ti, t0:t0 + n], in_=pb), reads=[pbk], writes=[("bb", ti, bi)])
        pieces = [(0, 1024, [0, 1]), (1024, 1024, [2, 3])] + ([] if last else [(2048, 256, [4])])
        for ti in range(2):
            for pi_, (t0, n, bis) in enumerate(pieces):
                uo = 1 + t0 if t0 < 2048 else 2 + t0
                y = yb[pi_ % 2]
                yk = "yb%d" % (pi_ % 2)
                ukeys = [("u", ti, b) for b in range(5) if (b in cb)]
                P.add("act", lambda g, y=y, ti=ti, uo=uo, n=n: g.activation(out=y[:, 0:n], in_=uv[:, ti, uo:uo + n], func=AF.Identity, bias=vcol(l, 34 + ti), scale=vcol(l, 28 + ti * 3 + 1)),
                      reads=ukeys + ["vec", "u"], writes=[yk])
                P.add("dve", lambda g, y=y, ti=ti, uo=uo, n=n: g.scalar_tensor_tensor(out=y[:, 0:n], in0=uv[:, ti, uo - 1:uo - 1 + n], scalar=vcol(l, 28 + ti * 3 + 0), in1=y[:, 0:n], op0=ALU.mult, op1=ALU.add),
                      reads=ukeys + ["vec", yk, "u"], writes=[yk])
                P.add("dve", lambda g, y=y, ti=ti, uo=uo, n=n: g.scalar_tensor_tensor(out=y[:, 0:n], in0=uv[:, ti, uo + 1:uo + 1 + n], scalar=vcol(l, 28 + ti * 3 + 2), in1=y[:, 0:n], op0=ALU.mult, op1=ALU.add),
                      reads=ukeys + ["vec", yk, "u"], writes=[yk])
                P.add("pool", lambda g, y=y, ti=ti, t0=t0, n=n: g.tensor_tensor(out=bbv[:, ti, t0:t0 + n], in0=bbv[:, ti, t0:t0 + n], in1=y[:, 0:n], op=ALU.mult),
                      reads=[yk] + [("bb", ti, b) for b in bis], writes=[("bb", ti, b) for b in bis])
        if l == 0:
            dump("ocv", bb)
        for bi in cb:
            t0, n = BLKS[bi]
            group_out(l, bbv[:, :, t0:t0 + n], [("bb", 0, bi), ("bb", 1, bi)], 2, n, bi, 2, wov, 256.0)
        if l == 0:
            dump("xcv", xT[:, :])

        ar.reset()
        hn_bufs.clear()
        hn_single[0] = True
        go_alloc()
        wqv = ar.bf(8 * 512).rearrange("p (k n) -> p k n", k=8)
        wkv_flat = ar.bf(8 * 256)
        wkvv = wkv_flat.rearrange("p (k n) -> p k n", k=8)
        load_w(wkvv, win_d[l].rearrange("(k p) n -> p k n", p=128)[:, :, 2048:2304], key="wbkv")
        load_w(wqv, win_d[l].rearrange("(k p) n -> p k n", p=128)[:, :, 1536:2048], key="wbq")
        wo = ar.bf(4 * 1024)
        wov = wo.rearrange("p (t n) -> p t n", t=4)
        load_w(wov, wo_d[l].rearrange("(t p) n -> p t n", p=128)[:, 4:8, :], key="wo")
        wak = ar.bf(T)
        vwa = ar.bf(18 * 192)
        vwav = vwa.rearrange("p (t c) -> p t c", c=192)
        waqs = [ar.bf(2048), wkv_flat]
        waqvs = [w_.rearrange("p (a j q) -> p a j q", a=4, j=4) for w_ in waqs]
        cosb = ar.f32(512)
        sinb = ar.f32(512)
        tn = ar.f32(512)
        tnb = ar.bf(512)
        t1 = ar.f32(512)
        PTv2 = [ar.bf(5 * 512).rearrange("p (c n) -> p c n", c=5) for _ in range(2)]
        owa = ar.f32(4 * 512)
        owav = owa.rearrange("p (j n) -> p j n", j=4)
        recw = ar.f32(512)
        t2 = recw
        wa_cnt = [0]
        sc_cnt = [0]
        P.add("pool", lambda g: g.memset(vwav[:, :, 64:128], 1.0), writes=["vwa"])

        def rope_to(pp, pk, n, gain_ap, t0, out3, okeys):
            headnorm(pp, pk, n, gain_ap, tn[:, 0:n], ["tn"])
            P.add("act", lambda g: g.copy(out=tnb[:, 0:n], in_=tn[:, 0:n]), reads=["tn"], writes=["tnb"])
            i = pp_i[0] % 4
            pp_i[0] += 1
            pr = PS[i // 2][:, (i % 2) * 512:(i % 2) * 512 + n]
            prk = "pp%d" % i
            P.add("pe", lambda g: g.matmul(pr, rotT[:, :], tnb[:, 0:n], start=True, stop=True), reads=["tnb", "rotT"], writes=[prk])
            P.add("dve", lambda g: g.tensor_tensor(out=t1[:, 0:n], in0=tn[:, 0:n], in1=cosb[:, 0:n], op=ALU.mult), reads=["tn", "cosb"], writes=["t1"])
            P.add("dve", lambda g: g.tensor_tensor(out=t2[:, 0:n], in0=pr, in1=sinb[:, 0:n], op=ALU.mult), reads=[prk, "sinb"], writes=["recw"])
            if len(out3.shape) == 2:
                P.add("pool", lambda g: g.tensor_tensor(out=out3, in0=t1[:, 0:n], in1=t2[:, 0:n], op=ALU.add), reads=["t1", "recw"], writes=okeys)
            else:
                P.add("pool", lambda g: g.tensor_tensor(out=out3, in0=t1[:, 0:n].rearrange("p (a b) -> p a b", b=128), in1=t2[:, 0:n].rearrange("p (a b) -> p a b", b=128), op=ALU.add),
                      reads=["t1", "recw"], writes=okeys)

        def load_rope(t0, n):
            P.add("sp", lambda g: g.dma_start(out=cosb[:, 0:n], in_=cos_d[:, t0:t0 + n]), writes=["cosb"], dma="cosb")
            P.add("sp", lambda g: g.dma_start(out=sinb[:, 0:n], in_=sin_d[:, t0:t0 + n]), writes=["sinb"], dma="sinb")

        for bi in allb:
            t0, n = BLKS[bi]
            pp, pk = proj(wkvv, 0, bi, wkey="wbkv")
            if bi < 4:
                load_rope(t0, n)
                rope_to(pp, pk, n, vcol(l, 27), t0, wak[:, t0:t0 + n], [("wak", bi)])
            else:
                headnorm(pp, pk, n, vcol(l, 27), wak[:, t0:t0 + n], [("wak", bi)])

            def vdst(tile, pp, pk):
                P.add("act", lambda g: g.copy(out=vwav[:, tile, :].rearrange("p (a b) -> p a b", b=64)[:, 0:3:2, :], in_=pp.rearrange("p (a b) -> p a b", b=64)),
                      reads=[pk, "vwa"], writes=[("vwa", tile)])
            vproj(wkvv, 128, 128, bi, vdst, wkey="wbkv")

        def q_chain(bi, j):
            t0, n = BLKS[bi]
            nq = n // 128
            wb_ = bi % 2
            okeys = [("waq", wb_, j)] + (["wbkv"] if wb_ == 1 else [])
            if j == 0 and bi < 4:
                load_rope(t0, n)
            pp, pk = proj(wqv, j * 128, bi, wkey="wbq")
            if bi < 4:
                rope_to(pp, pk, n, gq8[:, l * 2 + 1:l * 2 + 2], t0, waqvs[wb_][:, 0:nq, j, :], okeys)
            else:
                headnorm(pp, pk, n, gq8[:, l * 2 + 1:l * 2 + 2], waqvs[wb_][:, 0:nq, j, :], okeys)

        chainq = []
        for j in range(4):
            q_chain(qb[0], j)
        for qi, bi in enumerate(qb):
            t0, n = BLKS[bi]
            nq = n // 128
            waq = waqs[bi % 2]
            wqk = [("waq", bi % 2, j) for j in range(4)]
            if qi + 1 < len(qb):
                chainq = [(qb[qi + 1], j) for j in range(4)]
            else:
                chainq = []
            pendw = []
            for a in range(nq):
                nb = (t0 // 128) + a
                for gi in range(2):
                    lo, hi = gi * 64, gi * 64 + 64
                    dlo, dhi = (64, 128) if gi == 0 else (0, 64)
                    vc0 = 0 if gi == 0 else 64
                    chunks = []
                    if bi < 4:
                        if nb > 0:
                            chunks.append((nb - 1, 0))
                        chunks.append((nb, None))
                        if nb < 15:
                            chunks.append((nb + 1, 1))
                    chunks += [(16, None), (17, None)]
                    rhs = waq[lo:hi, a * 512:(a + 1) * 512]
                    wi = wa_cnt[0] % 2
                    wa_cnt[0] += 1
                    PTc = PTv2[wi]
                    for ci, (kt, mk) in enumerate(chunks):
                        si = sc_cnt[0] % 2
                        sc_cnt[0] += 1
                        sc = PS[2][:, si * 512:si * 512 + 512]
                        sck = "sc%d" % si
                        P.add("pe", lambda g: g.matmul(sc, wak[lo:hi, kt * 128:(kt + 1) * 128], rhs, start=True, stop=True),
                              reads=[("wak", kt // 4 if kt < 16 else 4)] + wqk, writes=[sck])
                        P.add("act", lambda g: g.activation(out=PTc[:, ci, :], in_=sc, func=AF.Exp), reads=[sck], writes=[("PTw", wi, ci)])
                        if mk is not None:
                            P.add("pool", lambda g: g.tensor_tensor(out=PTc[:, ci, :].rearrange("p (j q) -> p j q", j=4), in0=PTc[:, ci, :].rearrange("p (j q) -> p j q", j=4),
                                                                    in1=wam[:, mk * 128:(mk + 1) * 128].unsqueeze(1).to_broadcast([128, 4, 128]), op=ALU.mult),
                                  reads=[("PTw", wi, ci), "wam"], writes=[("PTw", wi, ci)])

                    def stageBw(chunks=chunks, PTc=PTc, wi=wi, lo=lo, hi=hi, dlo=dlo, dhi=dhi, vc0=vc0, a=a, gi=gi):
                        pv = PS[3][:, 512:1024]
                        for ci, (kt, mk) in enumerate(chunks):
                            P.add("pe", lambda g: g.matmul(pv, vwav[:, kt, vc0:vc0 + 128], PTc[:, ci, :], start=(ci == 0), stop=(ci == len(chunks) - 1)),
                                  reads=[("vwa", kt), ("PTw", wi, ci)], writes=["ps3b"])
                        P.add("dve", lambda g: g.tensor_tensor(out=recw[lo:hi, :].rearrange("p (j q) -> p j q", j=4), in0=pv[dlo:dhi, :].rearrange("p (j q) -> p j q", j=4),
                                                               in1=sinkv[dlo:dhi, l * 8 + gi * 4:l * 8 + gi * 4 + 4].unsqueeze(2).to_broadcast([64, 4, 128]), op=ALU.add),
                              reads=["ps3b", "sinkv"], writes=["recw"])
                        P.add("dve", lambda g: g.reciprocal(out=recw[lo:hi, :], in_=recw[lo:hi, :]), reads=["recw"], writes=["recw"])
                        P.add("dve", lambda g: g.tensor_tensor(out=owav[lo:hi, :, a * 128:(a + 1) * 128], in0=pv[lo:hi, :].rearrange("p (j q) -> p j q", j=4), in1=recw[lo:hi, :].rearrange("p (j q) -> p j q", j=4), op=ALU.mult),
                              reads=["ps3b", "recw"], writes=["owa"])
                    if pendw:
                        pendw.pop()()
                    pendw.append(stageBw)
                    if chainq and gi == 1:
                        q_chain(*chainq.pop(0))
            if pendw:
                pendw.pop()()
            while chainq:
                q_chain(*chainq.pop(0))
            if l == 0 and bi == 0:
                dump("owa0", owa)
            group_out(l, owav[:, :, 0:n], "owa", 4, n, bi, 4, wov, 512.0)
        if l == 0:
            dump("xmid", xT[:, :])

        hn_single[0] = False
        ar.reset()
        mb = lat if last else allb
        norm_all(l, 1, mb)
        f1 = [ar.bf(8 * 512), ar.bf(8 * 512)]
        f2 = [ar.bf(4 * 1024), ar.bf(4 * 1024)]
        HT = ar.bf(4 * T)
        HTv = HT.rearrange("p (i t) -> p i t", i=4)
        rl = [ar.f32(512), ar.f32(512)]
        if l + 1 < L:
            wmbA = AR[:, 0:2048].bitcast(BF16)
            wmbB = ar.bf(8 * 512)
            mod_q["alias0"] = ["nsq0", "nsq1", "nlnv", "nrstd0", "nrstd1", "ntmp0", "ntmp1"]
            mod_q["alias1"] = []
            mod_cnt[0] = 0
            mod_q["pieces"] = mod_pieces(l + 1, [0, 1, 2, 3, 4, 5], [wmbA, wmbB])
        for hc in range(8):
            f1v = f1[hc % 2].rearrange("p (k n) -> p k n", k=8)
            f2v = f2[hc % 2].rearrange("p (i n) -> p i n", i=4)
            load_w(f1v, fc1_d[l].rearrange("(k p) n -> p k n", p=128)[:, :, hc * 512:(hc + 1) * 512], key="f1_%d" % (hc % 2))
            load_w(f2v, fc2_d[l].rearrange("(i p) n -> p i n", p=128)[:, hc * 4:(hc + 1) * 4, :], key="f2_%d" % (hc % 2))
            mod_drain(2)
            for bi in mb:
                t0, n = BLKS[bi]
                for i4 in range(4):
                    i = pp_i[0] % 4
                    pp_i[0] += 1
                    pp = PS[i // 2][:, (i % 2) * 512:(i % 2) * 512 + n]
                    pk = "pp%d" % i
                    for kc in range(8):
                        P.add("pe", lambda g, pp=pp, kc=kc, i4=i4, f1v=f1v: g.matmul(pp, f1v[:, kc, i4 * 128:(i4 + 1) * 128], hTv[:, kc, t0:t0 + n], start=(kc == 0), stop=(kc == 7)),
                              reads=[("hT", kc, bi), "f1_%d" % (hc % 2)], writes=[pk])
                    r = rl[i4 % 2]
                    rk = "rl%d" % (i4 % 2)
                    P.add("act", lambda g, pp=pp, r=r: g.activation(out=r[:, 0:n], in_=pp, func=AF.Relu), reads=[pk], writes=[rk])
                    P.add("pool", lambda g, r=r, i4=i4: g.tensor_tensor(out=HTv[:, i4, t0:t0 + n], in0=r[:, 0:n], in1=r[:, 0:n], op=ALU.mult), reads=[rk], writes=[("HT", i4, bi)])
            for bi in mb:
                t0, n = BLKS[bi]
                s = 1 if bi == 4 else 0
                for nn in range(8):
                    i = pp_i[0] % 4
                    pp_i[0] += 1
                    pp = PS[i // 2][:, (i % 2) * 512:(i % 2) * 512 + n]
                    pk = "pp%d" % i
                    for i4 in range(4):
                        P.add("pe", lambda g, pp=pp, nn=nn, i4=i4, f2v=f2v: g.matmul(pp, f2v[:, i4, nn * 128:(nn + 1) * 128], HTv[:, i4, t0:t0 + n], start=(i4 == 0), stop=(i4 == 3)),
                              reads=[("HT", i4, bi), "f2_%d" % (hc % 2)], writes=[pk])
                    P.add("dve", lambda g, pp=pp, nn=nn: g.scalar_tensor_tensor(out=xTv[:, nn, t0:t0 + n], in0=pp, scalar=modcol(l, 5, nn, s), in1=xTv[:, nn, t0:t0 + n], op0=ALU.mult, op1=ALU.add),
                          reads=[pk, "modT", ("xT", nn, bi)], writes=[("xT", nn, bi)])
        mod_flush()
        mod_q["alias0"] = []

    ar.reset()
    stg = [ar.f32(1024), ar.f32(1024)]
    for ti in range(16):
        s = stg[ti % 2]
        sk = "ostg%d" % (ti % 2)
        pp = PS[ti % 2]
        pk = "ps%d" % (ti % 2)
        for kc in range(8):
            P.add("pe", lambda g, pp=pp, kc=kc, ti=ti: g.transpose(pp[:, kc * 128:(kc + 1) * 128], xTv[:, kc, ti * 128:(ti + 1) * 128], ident[:, :]),
                  reads=[("xT", kc, ti // 4), "ident"], writes=[pk])
        P.add("act", lambda g, s=s, pp=pp: g.copy(out=s[:, 0:512], in_=pp[:, 0:512]), reads=[pk], writes=[sk + "a"])
        P.add("dve", lambda g, s=s, pp=pp: g.tensor_copy(out=s[:, 512:1024], in_=pp[:, 512:1024]), reads=[pk], writes=[sk + "b"])
        P.add("sp", lambda g, s=s, ti=ti: g.dma_start(out=y_d[ti * 128:(ti + 1) * 128, :], in_=s), reads=[sk + "a", sk + "b"], dma="out%d" % (ti % 2))
    P.add("sp", lambda g: g.nop(), extra_deps=[o.id for o in P.ops if o.dma is not None and str(o.dma).startswith("out")])
    P.emit(st)
    st.close()
    return nc


def _host_consts():
    import ml_dtypes
    ident = np.eye(128, dtype=np.float32)
    R = np.zeros((64, 64), np.float32)
    for f in range(64):
        if (f % 32) < 16:
            R[f, f + 16] = -1.0
        else:
            R[f, f - 16] = 1.0
    R2 = np.zeros((128, 128), np.float32)
    R2[:64, :64] = R
    R2[64:, 64:] = R
    rotT = np.ascontiguousarray(R2.T)
    k = np.arange(128)[:, None]
    q = np.arange(128)[None, :]
    wam = np.concatenate([(k >= q).astype(np.float32), (k <= q).astype(np.float32)], axis=1)
    t = np.arange(2048)
    row = (t // 64).astype(np.float32)
    col = (t % 64).astype(np.float32)
    inv = (np.float32(10000.0) ** (-np.arange(16, dtype=np.float32) / np.float32(16))).astype(np.float32)
    ang = np.zeros((64, 2048), np.float32)
    for f in range(64):
        pos = row if f < 32 else col
        ang[f] = pos * inv[f % 16]
    cos = np.cos(ang).astype(np.float32)
    sin = np.sin(ang).astype(np.float32)
    return ident, rotT, wam, np.concatenate([cos, cos], 0), np.concatenate([sin, sin], 0)


def _nabias(rpb):
    out = np.full((L, 4, 128, 5, 5, 128), NEG, np.float32)
    jj = np.arange(2)[:, None, None, None]
    ck = np.arange(64)[None, :, None, None]
    rr = np.arange(2)[None, None, :, None]
    cq = np.arange(64)[None, None, None, :]
    for v, p in enumerate(VAR_REP):
        base = min(max(2 * p - 4, 0), 22)
        for c in range(5):
            rk = base + 2 * c + jj
            rq = 2 * p + rr
            start = np.clip(rq - 4, 0, 24)
            cs = np.clip(cq - 8, 0, 48)
            valid = (rk >= start) & (rk <= start + 7) & (rk <= 31) & (ck >= cs) & (ck < cs + 16)
            dr = np.clip(rk - rq + 7, 0, 14)
            dc = np.clip(ck - cq, -15, 15) + 15
            valid = np.broadcast_to(valid, (2, 64, 2, 64))
            drb = np.broadcast_to(dr, (2, 64, 2, 64))
            dcb = np.broadcast_to(dc, (2, 64, 2, 64))
            g = rpb[:, :, drb, dcb]
            g = np.where(valid[None, None], g, np.float32(NEG))
            out[:, :, :, v, c, :] = g.reshape(L, 4, 128, 128)
    return out.reshape(L, 4, 128, 5 * 5 * 128)


def _prep(inputs):
    f = lambda a: np.ascontiguousarray(np.asarray(a, dtype=np.float32))
    x, c, ctx, c_ctx = f(inputs["x"]), f(inputs["c"]), f(inputs["ctx"]), f(inputs["c_ctx"])
    w_in, w_o = f(inputs["w_in"]), f(inputs["w_o"])
    wa_q = 1536
    perm = list(range(0, 1536))
    for j in range(4):
        perm += list(range(wa_q + 64 * j, wa_q + 64 * j + 64)) + list(range(wa_q + 64 * (j + 4), wa_q + 64 * (j + 4) + 64))
    perm += list(range(2048, 2304))
    w_in_p = np.ascontiguousarray(w_in[:, :, perm])
    rperm = list(range(0, 512))
    for j in range(4):
        rperm += list(range(512 + 64 * j, 512 + 64 * j + 64)) + list(range(512 + 64 * (j + 4), 512 + 64 * (j + 4) + 64))
    w_o_p = np.ascontiguousarray(w_o[:, rperm, :])
    gout_p = f(inputs["g_out"])[:, rperm]
    col = lambda v: np.ascontiguousarray(v.reshape(-1, 128).T)
    vecs = np.zeros((128, L * 36), np.float32)
    for l in range(L):
        o = l * 36
        vecs[:, o:o + 8] = col(f(inputs["g_norm1"])[l])
        vecs[:, o + 8:o + 16] = col(f(inputs["g_norm2"])[l])
        vecs[:, o + 16:o + 24] = col(gout_p[l])
        for i, nm in enumerate(("na_q_gain", "na_k_gain", "wa_q_gain", "wa_k_gain")):
            vecs[:, o + 24 + i] = np.tile(f(inputs[nm])[l], 2)
        cw = f(inputs["conv_w"])[l]
        for ti in range(2):
            for tap in range(3):
                vecs[:, o + 28 + ti * 3 + tap] = cw[tap, ti * 128:(ti + 1) * 128]
            vecs[:, o + 34 + ti] = f(inputs["conv_bias"])[l, ti * 128:(ti + 1) * 128]
    bm = np.concatenate([col(f(inputs["b_mod"])[l]) for l in range(L)], axis=1)
    sink = np.ascontiguousarray(np.broadcast_to(f(inputs["wa_sink"]).reshape(1, L * 8), (128, L * 8)))
    ident, rotT, wam, cos, sin = _host_consts()
    nab = _nabias(f(inputs["na_rpb"]))
    shared = dict(w_mod=f(inputs["w_mod"]), bm=bm, vecs=vecs, sink=sink, w_in=w_in_p, w_o=w_o_p, w_fc1=f(inputs["w_fc1"]),
                  w_fc2=f(inputs["w_fc2"]), nabias=nab, ident=ident, rotT=rotT, wamask=wam, ropecos=cos, ropesin=sin)
    maps = []
    for b in range(8):
        ccm = np.zeros((128, 16), np.float32)
        ccm[:, 0::2] = c[b].reshape(8, 128).T
        ccm[:, 1::2] = c_ctx.reshape(8, 128).T
        m = dict(shared)
        m.update(x=x[b], ctx=ctx[b], cc=ccm)
        maps.append(m)
    return maps


_NC = {}


def kernel(**inputs):
    maps = _prep(inputs)
    if "nc" not in _NC:
        _NC["nc"] = build()
    res = run_bass_kernel_spmd(_NC["nc"], maps, core_ids=list(range(8)))
    return np.stack([np.asarray(r["y"], dtype=np.float32) for r in res.results], axis=0)
```

```python
import os
import numpy as np
from contextlib import ExitStack
import concourse.bass as bass
import concourse.mybir as mybir
from concourse.bass_utils import run_bass_kernel_spmd

F32 = mybir.dt.float32
BF16 = mybir.dt.bfloat16
ALU = mybir.AluOpType
AF = mybir.ActivationFunctionType
ENGS = ("pe", "act", "dve", "pool", "sp")
L = 2
T = 2304
NEG = -30000.0


class _Op:
    __slots__ = ("id", "eng", "fn", "deps", "dma", "cum", "pos", "waits", "signal", "snap")


class _Rec:
    def __getattr__(self, name):
        def f(*a, **k):
            self.call = (name, a, k)
            return self
        return f


class Prog:
    def __init__(self, nc, same_engine_sync=True):
        self.nc = nc
        self.same = same_engine_sync
        self.ops = []
        self.streams = {e: [] for e in ENGS}
        self.last_w = {}
        self.readers = {}
        self.dma_cum = {}

    def add(self, eng, fn, reads=(), writes=(), dma=None, extra_deps=()):
        op = _Op()
        op.id = len(self.ops)
        rec = _Rec()
        fn(rec)
        op.eng, op.fn, op.dma = eng, rec.call, dma
        op.signal = False
        op.waits = []
        deps = set(extra_deps)
        for k in reads:
            w = self.last_w.get(k)
            if w is not None:
                deps.add(w)
        for k in writes:
            w = self.last_w.get(k)
            if w is not None:
                deps.add(w)
            for r in self.readers.get(k, {}).values():
                deps.update(r)
        deps.discard(op.id)
        op.deps = sorted(deps)
        for k in reads:
            rd = self.readers.setdefault(k, {})
            if dma is not None:
                rd.setdefault("dma", []).append(op.id)
            else:
                rd[eng] = [op.id]
        for k in writes:
            self.last_w[k] = op.id
            self.readers[k] = {}
        if dma is not None:
            self.dma_cum[dma] = self.dma_cum.get(dma, 0) + 16
            op.cum = self.dma_cum[dma]
        op.pos = len(self.streams[eng])
        self.streams[eng].append(op)
        self.ops.append(op)
        return op

    def barrier(self):
        deps = []
        for e in ENGS:
            for o in reversed(self.streams[e]):
                if o.dma is None:
                    deps.append(o.id)
                    break
        lastb = getattr(self, "_lastb", 0)
        deps += [o.id for o in self.ops[lastb:] if o.dma is not None]
        self._lastb = len(self.ops)
        for e in ENGS:
            self.add(e, lambda g: g.nop(), extra_deps=deps)

    def plan(self):
        K = {e: {} for e in ENGS}
        for op in self.ops:
            E = op.eng
            k = K[E]
            dmax = {}
            for d in op.deps:
                dop = self.ops[d]
                if dop.dma is not None:
                    dmax[dop.dma] = max(dmax.get(dop.dma, 0), dop.cum)
            for d in op.deps:
                dop = self.ops[d]
                if dop.dma is not None:
                    key = ("dma", dop.dma)
                    if k.get(key, 0) < dmax[dop.dma]:
                        op.waits.append((key, dmax[dop.dma]))
                        k[key] = dmax[dop.dma]
                else:
                    F = dop.eng
                    if F == E and (not self.same or E == "pe"):
                        continue
                    if k.get(F, -1) >= dop.pos:
                        continue
                    op.waits.append((F, dop.pos))
                    dop.signal = True
                    k[F] = dop.pos
                for kk, vv in dop.snap.items():
                    if k.get(kk, -1) < vv:
                        k[kk] = vv
            op.snap = dict(k)
        self.cnt = {}
        for e in ENGS:
            c = 0
            for op in self.streams[e]:
                if op.dma is None and op.signal:
                    c += 1
                    self.cnt[(e, op.pos)] = c
        nw = sum(len(o.waits) for o in self.ops)
        print(f"[prog] ops={len(self.ops)} per-eng={ {e: len(s) for e, s in self.streams.items()} } waits={nw} "
              f"signals={ {e: sum(1 for o in s if o.signal) for e, s in self.streams.items()} }", flush=True)

    def emit(self, stack):
        nc = self.nc
        self.plan()
        esem = {e: stack.enter_context(nc.semaphore("s_" + e)) for e in ENGS}
        dsem = {k: stack.enter_context(nc.semaphore("d_" + str(k))) for k in self.dma_cum}
        block = stack.enter_context(nc.Block())

        def run(e, g):
            for op in self.streams[e]:
                for (key, val) in op.waits:
                    if isinstance(key, tuple):
                        g.wait_ge(dsem[key[1]], val)
                    else:
                        g.wait_ge(esem[key], self.cnt[(key, val)])
                nm, a_, k_ = op.fn
                ins = getattr(g, nm)(*a_, **k_)
                if op.dma is not None:
                    ins.then_inc(dsem[op.dma], 16)
                elif op.signal:
                    ins.then_inc(esem[e], 1)

        block.tensor(lambda g: run("pe", g))
        block.scalar(lambda g: run("act", g))
        block.vector(lambda g: run("dve", g))
        block.gpsimd(lambda g: run("pool", g))
        block.sync(lambda g: run("sp", g))


BLKS = [(0, 512), (512, 512), (1024, 512), (1536, 512), (2048, 256)]
PAIR_VAR = [0, 1] + [2] * 12 + [3, 4]
VAR_REP = [0, 1, 5, 14, 15]


def build(debug=None):
    nc = bass.Bass("TRN2", target_bir_lowering=False)
    di = lambda n, s: nc.dram_tensor(n, list(s), F32, kind="ExternalInput").ap()
    x_d = di("x", (2048, 1024))
    ctx_d = di("ctx", (256, 1024))
    cc_d = di("cc", (128, 16))
    wmod_d = di("w_mod", (L, 1024, 6144))
    bm_d = di("bm", (128, L * 48))
    vec_d = di("vecs", (128, L * 36))
    sink_d = di("sink", (128, L * 8))
    win_d = di("w_in", (L, 1024, 2304))
    wo_d = di("w_o", (L, 1024, 1024))
    fc1_d = di("w_fc1", (L, 1024, 4096))
    fc2_d = di("w_fc2", (L, 4096, 1024))
    nab_d = di("nabias", (L, 4, 128, 5 * 5 * 128))
    ident_d = di("ident", (128, 128))
    rot_d = di("rotT", (128, 128))
    wam_d = di("wamask", (128, 256))
    cos_d = di("ropecos", (128, 2048))
    sin_d = di("ropesin", (128, 2048))
    y_d = nc.dram_tensor("y", [2048, 1024], F32, kind="ExternalOutput").ap()
    dbg_d = {}
    if debug:
        for n, s in debug.items():
            dbg_d[n] = nc.dram_tensor("dbg_" + n, list(s), F32, kind="ExternalOutput").ap()

    st = ExitStack()
    P = Prog(nc)
    sb = lambda name, shape, dt: st.enter_context(nc.sbuf_tensor("sb_" + name, shape, dt))
    xT = sb("xT", [128, 8 * T], F32)
    xTv = xT[:, :].rearrange("p (k t) -> p k t", k=8)
    hT = sb("hT", [128, 8 * T], BF16)
    hTv = hT[:, :].rearrange("p (k t) -> p k t", k=8)
    ident = sb("ident", [128, 128], F32)
    ones = sb("ones", [128, 128], BF16)
    bd = sb("bd", [128, 128], BF16)
    rotT = sb("rotT", [128, 128], BF16)
    wam = sb("wam", [128, 256], BF16)
    epsc = sb("epsc", [128, 1], F32)
    cc = sb("cc", [128, 16], F32)
    ccb = sb("ccb", [128, 16], BF16)
    bm = sb("bm", [128, L * 48], F32)
    vec = sb("vec", [128, L * 36], F32)
    sinkv = sb("sinkv", [128, L * 8], F32)
    modT = sb("modT", [128, L * 96], F32)
    A12 = sb("A12", [128, L * 32], F32)
    gq8 = sb("gq8", [128, L * 2], F32)
    AR_F = 19328
    AR = sb("arena", [128, AR_F], F32)
    PS = [st.enter_context(nc.psum_tensor("ps%d" % i, [128, 1024], F32)) for i in range(4)]

    class Arena:
        def __init__(s):
            s.off = 0

        def reset(s):
            P.barrier()
            s.off = 0

        def f32(s, n):
            a = s.off
            s.off += n
            assert s.off <= AR_F, ("arena overflow", s.off)
            return AR[:, a:a + n]

        def bf(s, n):
            m = (n + 1) // 2
            a = s.off
            s.off += m
            assert s.off <= AR_F, ("arena overflow", s.off)
            return AR[:, a:a + m].bitcast(BF16)[:, 0:n]

    ar = Arena()
    uid = [0]

    def U(p="k"):
        uid[0] += 1
        return "%s%d" % (p, uid[0])

    def modcol(l, chunk, kc, s):
        i = l * 96 + (chunk * 8 + kc) * 2 + s
        return modT[:, i:i + 1]

    def Acol(l, which, kc, s):
        i = l * 32 + which * 16 + kc * 2 + s
        return A12[:, i:i + 1]

    def vcol(l, i):
        j = l * 36 + i
        return vec[:, j:j + 1]

    P.add("sp", lambda g: g.dma_start(out=ident[:, :], in_=ident_d), writes=["ident"], dma="c0ident")
    P.add("sp", lambda g: g.dma_start(out=cc[:, :], in_=cc_d), writes=["cc"], dma="c0cc")
    P.add("sp", lambda g: g.dma_start(out=bm[:, :], in_=bm_d), writes=["bm"], dma="c0bm")
    P.add("sp", lambda g: g.dma_start(out=vec[:, :], in_=vec_d), writes=["vec"], dma="c0vec")
    P.add("sp", lambda g: g.dma_start(out=sinkv[:, :], in_=sink_d), writes=["sinkv"], dma="c0sinkv")
    P.add("pool", lambda g: g.dma_start(out=rotT[:, :], in_=rot_d), writes=["rotT"], dma="c1rotT")
    P.add("pool", lambda g: g.dma_start(out=wam[:, :], in_=wam_d), writes=["wam"], dma="c1wam")
    P.add("pool", lambda g: g.memset(ones[:, :], 1.0), writes=["ones"])
    P.add("pool", lambda g: g.memset(bd[:, :], 0.0), writes=["bd"])
    P.add("pool", lambda g: g.memset(bd[0:64, 0:64], 1.0), reads=["bd"], writes=["bd"])
    P.add("pool", lambda g: g.memset(bd[64:128, 64:128], 1.0), reads=["bd"], writes=["bd"])
    P.add("pool", lambda g: g.memset(epsc[:, :], 1e-6), writes=["epsc"])
    P.add("act", lambda g: g.activation(out=ccb[:, :], in_=cc[:, :], func=AF.Silu), reads=["cc"], writes=["ccb"])
    P.add("act", lambda g: g.activation(out=sinkv[:, :], in_=sinkv[:, :], func=AF.Exp), reads=["sinkv"], writes=["sinkv"])
    for l in range(L):
        for i, c in enumerate((24, 26)):
            P.add("dve", lambda g, l=l, i=i, c=c: g.tensor_scalar(out=gq8[:, l * 2 + i:l * 2 + i + 1], in0=vcol(l, c), scalar1=0.125, scalar2=None, op0=ALU.mult),
                  reads=["vec"], writes=["gq8"])

    stg = [ar.f32(1024), ar.f32(1024)]
    for ti in range(18):
        s = stg[ti % 2]
        sk = "stg%d" % (ti % 2)
        src = x_d[ti * 128:(ti + 1) * 128, :] if ti < 16 else ctx_d[(ti - 16) * 128:(ti - 15) * 128, :]
        P.add("sp", lambda g, s=s, src=src: g.dma_start(out=s, in_=src), writes=[sk], dma=sk)
        pp = PS[ti % 2]
        pk = "ps%d" % (ti % 2)
        for kc in range(8):
            P.add("pe", lambda g, pp=pp, s=s, kc=kc: g.transpose(pp[:, kc * 128:(kc + 1) * 128], s[:, kc * 128:(kc + 1) * 128], ident[:, :]),
                  reads=[sk, "ident"], writes=[pk])
        for hf in range(2):
            eng = "act" if hf == 0 else "dve"
            outv = xTv[:, hf * 4:(hf + 1) * 4, ti * 128:(ti + 1) * 128]
            inv = pp[:, hf * 512:(hf + 1) * 512].rearrange("p (k t) -> p k t", k=4)
            wk = [("xT", kc, ti // 4 if ti < 16 else 4) for kc in range(hf * 4, hf * 4 + 4)]
            if eng == "act":
                P.add("act", lambda g, o=outv, i=inv: g.copy(out=o, in_=i), reads=[pk], writes=wk)
            else:
                P.add("dve", lambda g, o=outv, i=inv: g.tensor_copy(out=o, in_=i), reads=[pk], writes=wk)

    pp_i = [0]
    mod_cnt = [0]
    mod_q = {"pieces": [], "inflight": []}

    def mod_pieces(l, chunks, bufs):
        out = []
        mod_q["l"], mod_q["chunks"] = l, list(chunks)
        mod_q["need_evac"] = True
        for ch in chunks:
            for q4 in range(2):
                def dma_fn(ch=ch, q4=q4, st_={}):
                    bi_ = mod_cnt[0] % 2
                    mod_cnt[0] += 1
                    st_["b"] = bi_
                    wv = bufs[bi_].rearrange("p (k n) -> p k n", k=8)
                    c0 = ch * 1024 + q4 * 512
                    src = wmod_d[l].rearrange("(k p) n -> p k n", p=128)[:, :, c0:c0 + 512]
                    P.add("pool", lambda g: g.dma_start(out=wv, in_=src), writes=["wmb%d" % bi_] + list(mod_q.get("alias%d" % bi_, [])), dma="wmb%d" % bi_)
                    return st_

                def mm_fn(st_, ch=ch, q4=q4):
                    bi_ = st_["b"]
                    wk = "wmb%d" % bi_
                    wv = bufs[bi_].rearrange("p (k n) -> p k n", k=8)
                    for j in range(4):
                        nt = ch * 8 + q4 * 4 + j
                        for kc in range(8):
                            P.add("pe", lambda g: g.matmul(PS[3][:, 512 + nt * 2:512 + nt * 2 + 2], wv[:, kc, j * 128:(j + 1) * 128], ccb[:, kc * 2:kc * 2 + 2], start=(kc == 0), stop=(kc == 7)),
                                  reads=[wk, "ccb"], writes=["ps3b"])
                out.append((dma_fn, mm_fn))
        return out


    def mod_drain(k):
        for st_, m in mod_q["inflight"]:
            m(st_)
        mod_q["inflight"] = []
        while mod_q["pieces"] and len(mod_q["inflight"]) < 2:
            d, m = mod_q["pieces"].pop(0)
            mod_q["inflight"].append((d(), m))

    def mod_flush():
        if not mod_q.get("need_evac"):
            return
        mod_q["need_evac"] = False
        while mod_q["pieces"] or mod_q["inflight"]:
            mod_drain(1)
        l, chunks = mod_q["l"], mod_q["chunks"]
        c0, c1 = chunks[0], chunks[-1] + 1
        P.add("dve", lambda g: g.tensor_tensor(out=modT[:, l * 96 + c0 * 16:l * 96 + c1 * 16].rearrange("p (c s) -> p c s", s=2), in0=PS[3][:, 512 + c0 * 16:512 + c1 * 16].rearrange("p (c s) -> p c s", s=2),
                                               in1=bm[:, l * 48 + c0 * 8:l * 48 + c1 * 8].unsqueeze(2).to_broadcast([128, (c1 - c0) * 8, 2]), op=ALU.add),
              reads=["ps3b", "bm"], writes=["modT"])
        for ch in chunks:
            if ch in (1, 4):
                which = 0 if ch == 1 else 1
                gcol = 0 if ch == 1 else 8
                P.add("dve", lambda g: g.scalar_tensor_tensor(
                    out=A12[:, l * 32 + which * 16:l * 32 + which * 16 + 16].rearrange("p (k s) -> p k s", s=2),
                    in0=modT[:, l * 96 + ch * 16:l * 96 + ch * 16 + 16].rearrange("p (k s) -> p k s", s=2), scalar=1.0,
                    in1=vec[:, l * 36 + gcol:l * 36 + gcol + 8].unsqueeze(2).to_broadcast([128, 8, 2]), op0=ALU.add, op1=ALU.mult),
                    reads=["modT", "vec"], writes=["A12"])

    wmb0 = [ar.bf(8 * 512), ar.bf(8 * 512)]
    mod_q["pieces"] = mod_pieces(0, [0, 1], wmb0)
    mod_flush()

    def norm_all(l, which, blks, hook=None):
        sq = [ar.bf(512), ar.bf(512)]
        lnv = ar.f32(512)
        rstd = [ar.f32(512), ar.f32(512)]
        tmp = [ar.f32(512), ar.f32(512)]
        shc = 0 if which == 0 else 3
        for bi in blks:
            t0, n = BLKS[bi]
            s = 1 if bi == 4 else 0
            pst = PS[3]
            for kc in range(8):
                q = sq[kc % 2]
                qk = "nsq%d" % (kc % 2)
                P.add("pool", lambda g, q=q, kc=kc: g.tensor_tensor(out=q[:, 0:n], in0=xTv[:, kc, t0:t0 + n], in1=xTv[:, kc, t0:t0 + n], op=ALU.mult),
                      reads=[("xT", kc, bi)], writes=[qk])
                P.add("pe", lambda g, q=q, kc=kc: g.matmul(pst[:, 0:n], ones[:, :], q[:, 0:n], start=(kc == 0), stop=(kc == 7)),
                      reads=[qk, "ones"], writes=["ps3a"])
            r = rstd[bi % 2]
            rk = "nrstd%d" % (bi % 2)
            P.add("act", lambda g: g.activation(out=lnv[:, 0:n], in_=pst[:, 0:n], func=AF.Ln, bias=epsc[:, 0:1], scale=1.0 / 1024), reads=["ps3a", "epsc"], writes=["nlnv"])
            P.add("act", lambda g, r=r: g.activation(out=r[:, 0:n], in_=lnv[:, 0:n], func=AF.Exp, scale=-0.5), reads=["nlnv"], writes=[rk])
            for kc in range(8):
                tm = tmp[kc % 2]
                tk = "ntmp%d" % (kc % 2)
                P.add("dve", lambda g, tm=tm, kc=kc, r=r: g.scalar_tensor_tensor(out=tm[:, 0:n], in0=xTv[:, kc, t0:t0 + n], scalar=Acol(l, which, kc, s), in1=r[:, 0:n], op0=ALU.mult, op1=ALU.mult),
                      reads=[("xT", kc, bi), rk, "A12"], writes=[tk])
                P.add("act", lambda g, tm=tm, kc=kc: g.activation(out=hTv[:, kc, t0:t0 + n], in_=tm[:, 0:n], func=AF.Identity, bias=modcol(l, shc, kc, s), scale=1.0),
                      reads=[tk, "modT"], writes=[("hT", kc, bi)])
            if hook is not None:
                hook()

    def proj(wv, col0, bi, ncols=128, wkey="wb"):
        t0, n = BLKS[bi]
        i = pp_i[0] % 4
        pp_i[0] += 1
        pp = PS[i // 2][:, (i % 2) * 512:(i % 2) * 512 + n]
        pk = "pp%d" % i
        for kc in range(8):
            P.add("pe", lambda g, pp=pp, kc=kc: g.matmul(pp, wv[:, kc, col0:col0 + 128], hTv[:, kc, t0:t0 + n], start=(kc == 0), stop=(kc == 7)),
                  reads=[("hT", kc, bi), wkey], writes=[pk])
        return pp, pk

    hn_bufs = {}
    hn_single = [False]

    def headnorm(pp, pk, n, gain_ap, out_ap, out_keys, out_reads=()):
        if "sq" not in hn_bufs:
            if hn_single[0]:
                a_, b_ = ar.bf(512), ar.f32(512)
                hn_bufs["sq"] = [a_, a_]
                hn_bufs["r"] = [b_, b_]
            else:
                hn_bufs["sq"] = [ar.bf(512), ar.bf(512)]
                hn_bufs["r"] = [ar.f32(512), ar.f32(512)]
            hn_bufs["i"] = 0
        i = 0 if hn_single[0] else hn_bufs["i"] % 2
        hn_bufs["i"] += 1
        q, r = hn_bufs["sq"][i], hn_bufs["r"][i]
        qk, rk = "hsq%d" % i, "hr%d" % i
        hb = 0 if hn_single[0] else 512
        hbk = "ps3a" if hn_single[0] else "ps3b"
        pst = PS[3][:, hb:hb + n]
        P.add("act", lambda g: g.activation(out=q[:, 0:n], in_=pp, func=AF.Square), reads=[pk], writes=[qk])
        P.add("pe", lambda g: g.matmul(pst, bd[:, :], q[:, 0:n], start=True, stop=True), reads=[qk, "bd"], writes=[hbk])
        P.add("act", lambda g: g.activation(out=r[:, 0:n], in_=pst, func=AF.Ln, bias=epsc[:, 0:1], scale=1.0 / 64), reads=[hbk, "epsc"], writes=[rk])
        P.add("act", lambda g: g.activation(out=r[:, 0:n], in_=r[:, 0:n], func=AF.Exp, scale=-0.5), reads=[rk], writes=[rk])
        if len(out_ap.shape) == 2:
            i0, i1 = pp, r[:, 0:n]
        else:
            i0, i1 = pp.rearrange("p (a b) -> p a b", b=128), r[:, 0:n].rearrange("p (a b) -> p a b", b=128)
        P.add("dve", lambda g: g.scalar_tensor_tensor(out=out_ap, in0=i0, scalar=gain_ap, in1=i1, op0=ALU.mult, op1=ALU.mult),
              reads=[pk, rk, "vec", "gq8"] + list(out_reads), writes=out_keys)

    def vproj(wv, col0, ncols, bi, dst_fn, wkey="wb"):
        t0, n = BLKS[bi]
        for tt in range(n // 128):
            tile = (t0 // 128) + tt
            i = pp_i[0] % 4
            pp_i[0] += 1
            pp = PS[i // 2][:, (i % 2) * 512:(i % 2) * 512 + ncols]
            pk = "pp%d" % i
            for kc in range(8):
                P.add("pe", lambda g, pp=pp, kc=kc, tile=tile: g.matmul(pp, hTv[:, kc, tile * 128:(tile + 1) * 128], wv[:, kc, col0:col0 + ncols], start=(kc == 0), stop=(kc == 7)),
                      reads=[("hT", kc, bi), wkey], writes=[pk])
            dst_fn(tile, pp, pk)

    def group_out(l, o32, okey, ntl, n, bi, gcol0, wov, nch):
        t0, _ = BLKS[bi]
        s = 1 if bi == 4 else 0
        okeys = list(okey) if isinstance(okey, list) else [okey]
        sq = [ar_go["sq"][0], ar_go["sq"][1]]
        r = ar_go["r"]
        onb = ar_go["onb"]
        pst = PS[3][:, 0:n]
        for t in range(ntl):
            P.add("pool", lambda g, t=t: g.tensor_tensor(out=sq[t % 2][:, 0:n], in0=o32[:, t, :], in1=o32[:, t, :], op=ALU.mult), reads=okeys, writes=["gsq%d" % (t % 2)])
            P.add("pe", lambda g, t=t: g.matmul(pst, ones[:, :], sq[t % 2][:, 0:n], start=(t == 0), stop=(t == ntl - 1)), reads=["gsq%d" % (t % 2), "ones"], writes=["ps3a"])
        P.add("act", lambda g: g.activation(out=r[:, 0:n], in_=pst, func=AF.Ln, bias=epsc[:, 0:1], scale=1.0 / nch), reads=["ps3a", "epsc"], writes=["gr"])
        P.add("act", lambda g: g.activation(out=r[:, 0:n], in_=r[:, 0:n], func=AF.Exp, scale=-0.5), reads=["gr"], writes=["gr"])
        for t in range(ntl):
            P.add("dve", lambda g, t=t: g.scalar_tensor_tensor(out=onb[:, t * 512:t * 512 + n], in0=o32[:, t, :], scalar=vcol(l, 16 + gcol0 + t), in1=r[:, 0:n], op0=ALU.mult, op1=ALU.mult),
                  reads=okeys + ["gr", "vec"], writes=[("onb", t)])
        for nn in range(8):
            i = pp_i[0] % 4
            pp_i[0] += 1
            pp = PS[i // 2][:, (i % 2) * 512:(i % 2) * 512 + n]
            pk = "pp%d" % i
            for t in range(ntl):
                P.add("pe", lambda g, pp=pp, t=t, nn=nn: g.matmul(pp, wov[:, t, nn * 128:(nn + 1) * 128], onb[:, t * 512:t * 512 + n], start=(t == 0), stop=(t == ntl - 1)),
                      reads=[("onb", t), "wo"], writes=[pk])
            P.add("dve", lambda g, pp=pp, nn=nn: g.scalar_tensor_tensor(out=xTv[:, nn, t0:t0 + n], in0=pp, scalar=modcol(l, 2, nn, s), in1=xTv[:, nn, t0:t0 + n], op0=ALU.mult, op1=ALU.add),
                  reads=[pk, "modT", ("xT", nn, bi)], writes=[("xT", nn, bi)])

    ar_go = {}

    def go_alloc():
        ar_go["sq"] = [ar.bf(512), ar.bf(512)]
        ar_go["r"] = ar.f32(512)
        ar_go["onb"] = ar.bf(4 * 512)

    def load_w(dst_view, src, key="wb"):
        P.add("pool", lambda g: g.dma_start(out=dst_view, in_=src), writes=[key], dma=key)

    def dump(name, ap, reads=()):
        if name in dbg_d:
            P.barrier()
            P.add("pool", lambda g: g.dma_start(out=dbg_d[name], in_=ap), reads=list(reads), writes=[], dma="out_" + name)

    for l in range(L):
        last = (l == L - 1)
        lat = [0, 1, 2, 3]
        allb = [0, 1, 2, 3, 4]
        qb = lat if last else allb
        ar.reset()
        if l == 0:
            wmb1 = [ar.bf(8 * 512), ar.bf(8 * 512)]
            mod_q["pieces"] = mod_pieces(0, [2, 3, 4, 5], wmb1)
        norm_all(l, 0, allb, hook=(lambda: mod_drain(2)))
        mod_flush()
        if l == 0:
            dump("h0", hT[:, :])

        o_na = None
        for tpair in range(2):
            ar.reset()
            hn_bufs.clear()
            wbt = ar.bf(8 * 384)
            wv = wbt.rearrange("p (k n) -> p k n", k=8)
            wsrc = win_d[l].rearrange("(k p) n -> p k n", p=128)
            load_w(wv[:, :, 0:128], wsrc[:, :, tpair * 128:(tpair + 1) * 128])
            load_w(wv[:, :, 128:256], wsrc[:, :, 256 + tpair * 128:256 + (tpair + 1) * 128])
            load_w(wv[:, :, 256:384], wsrc[:, :, 512 + tpair * 128:512 + (tpair + 1) * 128])
            nak = ar.bf(T)
            vna = ar.bf(18 * 192)
            vnav = vna.rearrange("p (t c) -> p t c", c=192)
            naq = [ar.bf(512), ar.bf(512)]
            Eint = ar.f32(2 * 640)
            Eedge = [ar.f32(640), ar.f32(640)]
            ptmp = [ar.f32(640) for _ in range(4)]
            PT = [ar.bf(896) for _ in range(4)]
            rec = [ar.f32(256) for _ in range(4)]
            ona = AR[:, AR_F - 2 * T:AR_F]
            onav = ona.rearrange("p (t n) -> p t n", t=2)
            assert ar.off <= AR_F - 2 * T
            P.add("pool", lambda g: g.memset(vnav[:, :, 64:128], 1.0), writes=["vna"])
            for hh in range(2):
                h = tpair * 2 + hh
                src = nab_d[l, h].rearrange("p (v x) -> p v x", v=5)[:, 2, :]
                P.add("sp", lambda g, hh=hh, src=src: g.dma_start(out=Eint[:, hh * 640:(hh + 1) * 640], in_=src), writes=["Eint"], dma="Eint")
            P.add("act", lambda g: g.activation(out=Eint, in_=Eint, func=AF.Exp), reads=["Eint"], writes=["Eint"])
            for bi in allb:
                t0, n = BLKS[bi]
                pp, pk = proj(wv, 128, bi)
                headnorm(pp, pk, n, vcol(l, 25), nak[:, t0:t0 + n], [("nak", bi)])

                def vdst(tile, pp, pk):
                    P.add("act", lambda g: g.copy(out=vnav[:, tile, :].rearrange("p (a b) -> p a b", b=64)[:, 0:3:2, :], in_=pp.rearrange("p (a b) -> p a b", b=64)),
                          reads=[pk, "vna"], writes=[("vna", tile)])
                vproj(wv, 256, 128, bi, vdst)
            pend = []
            na_cnt = [0]
            for bi in qb:
                t0, n = BLKS[bi]
                if pend:
                    pend.pop()()
                q = naq[bi % 2]
                qk = "naq%d" % (bi % 2)
                pp, pk = proj(wv, 0, bi)
                headnorm(pp, pk, n, gq8[:, l * 2:l * 2 + 1], q[:, 0:n], [qk])
                if bi < 4:
                    for pi in range(4):
                        p = bi * 4 + pi
                        var = PAIR_VAR[p]
                        base = min(max(2 * p - 4, 0), 22)
                        st_i = na_cnt[0] % 2
                        na_cnt[0] += 1
                        ktiles = [base // 2 + c for c in range(5)] + [16, 17]
                        Es = []
                        for hh in range(2):
                            h = tpair * 2 + hh
                            if var == 2:
                                Es.append((Eint[:, hh * 640:(hh + 1) * 640], "Eint"))
                            else:
                                E = Eedge[hh]
                                Ek = "Eedge%d" % hh
                                src = nab_d[l, h].rearrange("p (v x) -> p v x", v=5)[:, var, :]
                                P.add("sp", lambda g: g.dma_start(out=E, in_=src), writes=[Ek], dma=Ek)
                                P.add("act", lambda g: g.activation(out=E, in_=E, func=AF.Exp), reads=[Ek], writes=[Ek])
                                Es.append((E, Ek))
                        scs = [(PS[2], ["ps2"]), (PS[1], ["pp2", "pp3"])]
                        for c, kt in enumerate(ktiles):
                            for hh in range(2):
                                lo, hi = hh * 64, hh * 64 + 64
                                sc, sck = scs[hh]
                                P.add("pe", lambda g: g.matmul(sc[:, c * 128:(c + 1) * 128], nak[lo:hi, kt * 128:(kt + 1) * 128], q[lo:hi, pi * 128:(pi + 1) * 128], start=True, stop=True),
                                      reads=[("nak", kt // 4 if kt < 16 else 4), qk], writes=sck)
                        for hh in range(2):
                            sc, sck = scs[hh]
                            E, Ek = Es[hh]
                            pt = ptmp[st_i * 2 + hh]
                            ptk = "ptmp%d" % (st_i * 2 + hh)
                            PTi = PT[st_i * 2 + hh]
                            PTk = "PT%d" % (st_i * 2 + hh)
                            P.add("act", lambda g: g.activation(out=pt[:, 0:512], in_=sc[:, 0:512], func=AF.Exp), reads=sck, writes=[ptk])
                            P.add("act", lambda g: g.activation(out=pt[:, 512:640], in_=sc[:, 512:640], func=AF.Exp), reads=sck + [ptk], writes=[ptk])
                            P.add("act", lambda g: g.activation(out=PTi[:, 640:896], in_=sc[:, 640:896], func=AF.Exp), reads=sck, writes=[PTk])
                            P.add("pool", lambda g: g.tensor_tensor(out=PTi[:, 0:640], in0=pt[:, 0:640], in1=E, op=ALU.mult), reads=[ptk, Ek, PTk], writes=[PTk])

                        def stageB(st_i=st_i, ktiles=ktiles, pi=pi, t0=t0, bi=bi):
                            for hh in range(2):
                                lo, hi = hh * 64, hh * 64 + 64
                                dlo, dhi = (64, 128) if hh == 0 else (0, 64)
                                vc0 = 0 if hh == 0 else 64
                                PTi = PT[st_i * 2 + hh]
                                PTk = "PT%d" % (st_i * 2 + hh)
                                pv = PS[3][:, 512 + hh * 128:512 + (hh + 1) * 128]
                                pvk = "pv%d" % hh
                                for c, kt in enumerate(ktiles):
                                    P.add("pe", lambda g: g.matmul(pv, vnav[:, kt, vc0:vc0 + 128], PTi[:, c * 128:(c + 1) * 128], start=(c == 0), stop=(c == 6)),
                                          reads=[("vna", kt), PTk], writes=[pvk, "ps3b"])
                                rc = rec[st_i * 2 + hh]
                                rck = "rec%d" % (st_i * 2 + hh)
                                P.add("dve", lambda g: g.reciprocal(out=rc[lo:hi, 0:128], in_=pv[dlo:dhi, :]), reads=[pvk, "ps3b"], writes=[rck])
                                P.add("dve", lambda g: g.tensor_tensor(out=onav[lo:hi, tpair, t0 + pi * 128:t0 + (pi + 1) * 128], in0=pv[lo:hi, :], in1=rc[lo:hi, 0:128], op=ALU.mult),
                                      reads=[pvk, "ps3b", rck], writes=[("ona", tpair, bi, hh)])
                        if pend:
                            pend.pop()()
                        pend.append(stageB)
                for hh in range(2):
                    h = tpair * 2 + hh
                    lo, hi = hh * 64, hh * 64 + 64
                    dlo, dhi = (64, 128) if hh == 0 else (0, 64)
                    vc0 = 0 if hh == 0 else 64
                    if bi < 4:
                        pass
                    else:
                        sc = PS[2]
                        sck = "ps2"
                        for c in range(2):
                            P.add("pe", lambda g, c=c: g.matmul(sc[:, c * 256:(c + 1) * 256], nak[lo:hi, 2048 + c * 128:2048 + (c + 1) * 128], q[lo:hi, 0:256], start=True, stop=True),
                                  reads=[("nak", 4), qk], writes=[sck])
                        PTi = PT[hh]
                        PTk = "PT%d" % hh
                        P.add("act", lambda g, PTi=PTi: g.activation(out=PTi[:, 0:512], in_=sc[:, 0:512], func=AF.Exp), reads=[sck], writes=[PTk])
                        pv = PS[3][:, 512 + hh * 256:512 + (hh + 1) * 256]
                        pvk = "pv%d" % hh
                        for c in range(2):
                            P.add("pe", lambda g, c=c, pv=pv, PTi=PTi: g.matmul(pv, vnav[:, 16 + c, vc0:vc0 + 128], PTi[:, c * 256:(c + 1) * 256], start=(c == 0), stop=(c == 1)),
                                  reads=[("vna", 16 + c), PTk], writes=[pvk, "ps3b"])
                        rc = rec[hh]
                        rck = "rec%d" % hh
                        P.add("dve", lambda g, rc=rc, pv=pv: g.reciprocal(out=rc[lo:hi, 0:256], in_=pv[dlo:dhi, :]), reads=[pvk, "ps3b"], writes=[rck])
                        P.add("dve", lambda g, rc=rc, pv=pv: g.tensor_tensor(out=onav[lo:hi, tpair, 2048:2304], in0=pv[lo:hi, :], in1=rc[lo:hi, 0:256], op=ALU.mult),
                              reads=[pvk, "ps3b", rck], writes=[("ona", tpair, bi, hh)])
            if pend:
                pend.pop()()
        if l == 0:
            dump("ona", ona)
        ar.reset()
        go_alloc()
        wo = ar.bf(2 * 1024)
        wov = wo.rearrange("p (t n) -> p t n", t=2)
        load_w(wov, wo_d[l].rearrange("(t p) n -> p t n", p=128)[:, 0:2, :], key="wo")
        assert ar.off <= AR_F - 2 * T
        for bi in qb:
            t0, n = BLKS[bi]
            okeys = [("ona", tp, bi, hh) for tp in range(2) for hh in range(2)]
            group_out(l, onav[:, :, t0:t0 + n], okeys, 2, n, bi, 0, wov, 256.0)
        if l == 0:
            dump("xna", xT[:, :])

        ar.reset()
        hn_bufs.clear()
        go_alloc()
        wbt = ar.bf(8 * 768)
        wv = wbt.rearrange("p (k n) -> p k n", k=8)
        load_w(wv, win_d[l].rearrange("(k p) n -> p k n", p=128)[:, :, 768:1536])
        wo = ar.bf(2 * 1024)
        wov = wo.rearrange("p (t n) -> p t n", t=2)
        load_w(wov, wo_d[l].rearrange("(t p) n -> p t n", p=128)[:, 2:4, :], key="wo")
        UW = 2308
        u = ar.f32(2 * UW)
        uv = u.rearrange("p (t n) -> p t n", t=2)
        bb = ar.f32(2 * T)
        bbv = bb.rearrange("p (t n) -> p t n", t=2)
        xs = [ar.f32(512), ar.f32(512)]
        yb = [ar.f32(1024), ar.f32(1024)]
        P.add("pool", lambda g: g.memset(u, 0.0), writes=["u"])
        cb = qb
        for bi in cb:
            t0, n = BLKS[bi]
            uo = 1 + t0 if bi < 4 else 2 + t0
            for ti in range(2):
                px, pxk = proj(wv, ti * 128, bi)
                xsi = xs[ti]
                P.add("act", lambda g, px=px, xsi=xsi: g.copy(out=xsi[:, 0:n], in_=px), reads=[pxk], writes=["xs%d" % ti])
                pc, pck = proj(wv, 512 + ti * 128, bi)
                P.add("dve", lambda g, pc=pc, xsi=xsi, ti=ti: g.tensor_tensor(out=uv[:, ti, uo:uo + n], in0=pc, in1=xsi[:, 0:n], op=ALU.mult),
                      reads=[pck, "xs%d" % ti, "u"], writes=[("u", ti, bi)])
                pb, pbk = proj(wv, 256 + ti * 128, bi)
                P.add("act", lambda g, pb=pb, ti=ti: g.copy(out=bbv[:, ti, t0:t0 + n], in_=pb), reads=[pbk], writes=[("bb", ti, bi)])
        pieces = [(0, 1024, [0, 1]), (1024, 1024, [2, 3])] + ([] if last else [(2048, 256, [4])])
        for ti in range(2):
            for pi_, (t0, n, bis) in enumerate(pieces):
                uo = 1 + t0 if t0 < 2048 else 2 + t0
                y = yb[pi_ % 2]
                yk = "yb%d" % (pi_ % 2)
                ukeys = [("u", ti, b) for b in range(5) if (b in cb)]
                P.add("act", lambda g, y=y, ti=ti, uo=uo, n=n: g.activation(out=y[:, 0:n], in_=uv[:, ti, uo:uo + n], func=AF.Identity, bias=vcol(l, 34 + ti), scale=vcol(l, 28 + ti * 3 + 1)),
                      reads=ukeys + ["vec", "u"], writes=[yk])
                P.add("dve", lambda g, y=y, ti=ti, uo=uo, n=n: g.scalar_tensor_tensor(out=y[:, 0:n], in0=uv[:, ti, uo - 1:uo - 1 + n], scalar=vcol(l, 28 + ti * 3 + 0), in1=y[:, 0:n], op0=ALU.mult, op1=ALU.add),
                      reads=ukeys + ["vec", yk, "u"], writes=[yk])
                P.add("dve", lambda g, y=y, ti=ti, uo=uo, n=n: g.scalar_tensor_tensor(out=y[:, 0:n], in0=uv[:, ti, uo + 1:uo + 1 + n], scalar=vcol(l, 28 + ti * 3 + 2), in1=y[:, 0:n], op0=ALU.mult, op1=ALU.add),
                      reads=ukeys + ["vec", yk, "u"], writes=[yk])
                P.add("pool", lambda g, y=y, ti=ti, t0=t0, n=n: g.tensor_tensor(out=bbv[:, ti, t0:t0 + n], in0=bbv[:, ti, t0:t0 + n], in1=y[:, 0:n], op=ALU.mult),
                      reads=[yk] + [("bb", ti, b) for b in bis], writes=[("bb", ti, b) for b in bis])
        if l == 0:
            dump("ocv", bb)
        for bi in cb:
            t0, n = BLKS[bi]
            group_out(l, bbv[:, :, t0:t0 + n], [("bb", 0, bi), ("bb", 1, bi)], 2, n, bi, 2, wov, 256.0)
        if l == 0:
            dump("xcv", xT[:, :])

        ar.reset()
        hn_bufs.clear()
        hn_single[0] = True
        go_alloc()
        wqv = ar.bf(8 * 512).rearrange("p (k n) -> p k n", k=8)
        wkv_flat = ar.bf(8 * 256)
        wkvv = wkv_flat.rearrange("p (k n) -> p k n", k=8)
        load_w(wkvv, win_d[l].rearrange("(k p) n -> p k n", p=128)[:, :, 2048:2304], key="wbkv")
        load_w(wqv, win_d[l].rearrange("(k p) n -> p k n", p=128)[:, :, 1536:2048], key="wbq")
        wo = ar.bf(4 * 1024)
        wov = wo.rearrange("p (t n) -> p t n", t=4)
        load_w(wov, wo_d[l].rearrange("(t p) n -> p t n", p=128)[:, 4:8, :], key="wo")
        wak = ar.bf(T)
        vwa = ar.bf(18 * 192)
        vwav = vwa.rearrange("p (t c) -> p t c", c=192)
        waqs = [ar.bf(2048), wkv_flat]
        waqvs = [w_.rearrange("p (a j q) -> p a j q", a=4, j=4) for w_ in waqs]
        cosb = ar.f32(512)
        sinb = ar.f32(512)
        tn = ar.f32(512)
        tnb = ar.bf(512)
        t1 = ar.f32(512)
        PTv2 = [ar.bf(5 * 512).rearrange("p (c n) -> p c n", c=5) for _ in range(2)]
        owa = ar.f32(4 * 512)
        owav = owa.rearrange("p (j n) -> p j n", j=4)
        recw = ar.f32(512)
        t2 = recw
        wa_cnt = [0]
        sc_cnt = [0]
        P.add("pool", lambda g: g.memset(vwav[:, :, 64:128], 1.0), writes=["vwa"])

        def rope_to(pp, pk, n, gain_ap, t0, out3, okeys):
            headnorm(pp, pk, n, gain_ap, tn[:, 0:n], ["tn"])
            P.add("act", lambda g: g.copy(out=tnb[:, 0:n], in_=tn[:, 0:n]), reads=["tn"], writes=["tnb"])
            i = pp_i[0] % 4
            pp_i[0] += 1
            pr = PS[i // 2][:, (i % 2) * 512:(i % 2) * 512 + n]
            prk = "pp%d" % i
            P.add("pe", lambda g: g.matmul(pr, rotT[:, :], tnb[:, 0:n], start=True, stop=True), reads=["tnb", "rotT"], writes=[prk])
            P.add("dve", lambda g: g.tensor_tensor(out=t1[:, 0:n], in0=tn[:, 0:n], in1=cosb[:, 0:n], op=ALU.mult), reads=["tn", "cosb"], writes=["t1"])
            P.add("dve", lambda g: g.tensor_tensor(out=t2[:, 0:n], in0=pr, in1=sinb[:, 0:n], op=ALU.mult), reads=[prk, "sinb"], writes=["recw"])
            if len(out3.shape) == 2:
                P.add("pool", lambda g: g.tensor_tensor(out=out3, in0=t1[:, 0:n], in1=t2[:, 0:n], op=ALU.add), reads=["t1", "recw"], writes=okeys)
            else:
                P.add("pool", lambda g: g.tensor_tensor(out=out3, in0=t1[:, 0:n].rearrange("p (a b) -> p a b", b=128), in1=t2[:, 0:n].rearrange("p (a b) -> p a b", b=128), op=ALU.add),
                      reads=["t1", "recw"], writes=okeys)

        def load_rope(t0, n):
            P.add("sp", lambda g: g.dma_start(out=cosb[:, 0:n], in_=cos_d[:, t0:t0 + n]), writes=["cosb"], dma="cosb")
            P.add("sp", lambda g: g.dma_start(out=sinb[:, 0:n], in_=sin_d[:, t0:t0 + n]), writes=["sinb"], dma="sinb")

        for bi in allb:
            t0, n = BLKS[bi]
            pp, pk = proj(wkvv, 0, bi, wkey="wbkv")
            if bi < 4:
                load_rope(t0, n)
                rope_to(pp, pk, n, vcol(l, 27), t0, wak[:, t0:t0 + n], [("wak", bi)])
            else:
                headnorm(pp, pk, n, vcol(l, 27), wak[:, t0:t0 + n], [("wak", bi)])

            def vdst(tile, pp, pk):
                P.add("act", lambda g: g.copy(out=vwav[:, tile, :].rearrange("p (a b) -> p a b", b=64)[:, 0:3:2, :], in_=pp.rearrange("p (a b) -> p a b", b=64)),
                      reads=[pk, "vwa"], writes=[("vwa", tile)])
            vproj(wkvv, 128, 128, bi, vdst, wkey="wbkv")

        def q_chain(bi, j):
            t0, n = BLKS[bi]
            nq = n // 128
            wb_ = bi % 2
            okeys = [("waq", wb_, j)] + (["wbkv"] if wb_ == 1 else [])
            if j == 0 and bi < 4:
                load_rope(t0, n)
            pp, pk = proj(wqv, j * 128, bi, wkey="wbq")
            if bi < 4:
                rope_to(pp, pk, n, gq8[:, l * 2 + 1:l * 2 + 2], t0, waqvs[wb_][:, 0:nq, j, :], okeys)
            else:
                headnorm(pp, pk, n, gq8[:, l * 2 + 1:l * 2 + 2], waqvs[wb_][:, 0:nq, j, :], okeys)

        chainq = []
        for j in range(4):
            q_chain(qb[0], j)
        for qi, bi in enumerate(qb):
            t0, n = BLKS[bi]
            nq = n // 128
            waq = waqs[bi % 2]
            wqk = [("waq", bi % 2, j) for j in range(4)]
            if qi + 1 < len(qb):
                chainq = [(qb[qi + 1], j) for j in range(4)]
            else:
                chainq = []
            pendw = []
            for a in range(nq):
                nb = (t0 // 128) + a
                for gi in range(2):
                    lo, hi = gi * 64, gi * 64 + 64
                    dlo, dhi = (64, 128) if gi == 0 else (0, 64)
                    vc0 = 0 if gi == 0 else 64
                    chunks = []
                    if bi < 4:
                        if nb > 0:
                            chunks.append((nb - 1, 0))
                        chunks.append((nb, None))
                        if nb < 15:
                            chunks.append((nb + 1, 1))
                    chunks += [(16, None), (17, None)]
                    rhs = waq[lo:hi, a * 512:(a + 1) * 512]
                    wi = wa_cnt[0] % 2
                    wa_cnt[0] += 1
                    PTc = PTv2[wi]
                    for ci, (kt, mk) in enumerate(chunks):
                        si = sc_cnt[0] % 2
                        sc_cnt[0] += 1
                        sc = PS[2][:, si * 512:si * 512 + 512]
                        sck = "sc%d" % si
                        P.add("pe", lambda g: g.matmul(sc, wak[lo:hi, kt * 128:(kt + 1) * 128], rhs, start=True, stop=True),
                              reads=[("wak", kt // 4 if kt < 16 else 4)] + wqk, writes=[sck])
                        P.add("act", lambda g: g.activation(out=PTc[:, ci, :], in_=sc, func=AF.Exp), reads=[sck], writes=[("PTw", wi, ci)])
                        if mk is not None:
                            P.add("pool", lambda g: g.tensor_tensor(out=PTc[:, ci, :].rearrange("p (j q) -> p j q", j=4), in0=PTc[:, ci, :].rearrange("p (j q) -> p j q", j=4),
                                                                    in1=wam[:, mk * 128:(mk + 1) * 128].unsqueeze(1).to_broadcast([128, 4, 128]), op=ALU.mult),
                                  reads=[("PTw", wi, ci), "wam"], writes=[("PTw", wi, ci)])

                    def stageBw(chunks=chunks, PTc=PTc, wi=wi, lo=lo, hi=hi, dlo=dlo, dhi=dhi, vc0=vc0, a=a, gi=gi):
                        pv = PS[3][:, 512:1024]
                        for ci, (kt, mk) in enumerate(chunks):
                            P.add("pe", lambda g: g.matmul(pv, vwav[:, kt, vc0:vc0 + 128], PTc[:, ci, :], start=(ci == 0), stop=(ci == len(chunks) - 1)),
                                  reads=[("vwa", kt), ("PTw", wi, ci)], writes=["ps3b"])
                        P.add("dve", lambda g: g.tensor_tensor(out=recw[lo:hi, :].rearrange("p (j q) -> p j q", j=4), in0=pv[dlo:dhi, :].rearrange("p (j q) -> p j q", j=4),
                                                               in1=sinkv[dlo:dhi, l * 8 + gi * 4:l * 8 + gi * 4 + 4].unsqueeze(2).to_broadcast([64, 4, 128]), op=ALU.add),
                              reads=["ps3b", "sinkv"], writes=["recw"])
                        P.add("dve", lambda g: g.reciprocal(out=recw[lo:hi, :], in_=recw[lo:hi, :]), reads=["recw"], writes=["recw"])
                        P.add("dve", lambda g: g.tensor_tensor(out=owav[lo:hi, :, a * 128:(a + 1) * 128], in0=pv[lo:hi, :].rearrange("p (j q) -> p j q", j=4), in1=recw[lo:hi, :].rearrange("p (j q) -> p j q", j=4), op=ALU.mult),
                              reads=["ps3b", "recw"], writes=["owa"])
                    if pendw:
                        pendw.pop()()
                    pendw.append(stageBw)
                    if chainq and gi == 1:
                        q_chain(*chainq.pop(0))
            if pendw:
                pendw.pop()()
            while chainq:
                q_chain(*chainq.pop(0))
            if l == 0 and bi == 0:
                dump("owa0", owa)
            group_out(l, owav[:, :, 0:n], "owa", 4, n, bi, 4, wov, 512.0)
        if l == 0:
            dump("xmid", xT[:, :])

        hn_single[0] = False
        ar.reset()
        mb = lat if last else allb
        norm_all(l, 1, mb)
        f1 = [ar.bf(8 * 512), ar.bf(8 * 512)]
        f2 = [ar.bf(4 * 1024), ar.bf(4 * 1024)]
        HT = ar.bf(4 * T)
        HTv = HT.rearrange("p (i t) -> p i t", i=4)
        rl = [ar.f32(512), ar.f32(512)]
        if l + 1 < L:
            wmbA = AR[:, 0:2048].bitcast(BF16)
            wmbB = ar.bf(8 * 512)
            mod_q["alias0"] = ["nsq0", "nsq1", "nlnv", "nrstd0", "nrstd1", "ntmp0", "ntmp1"]
            mod_q["alias1"] = []
            mod_cnt[0] = 0
            mod_q["pieces"] = mod_pieces(l + 1, [0, 1, 2, 3, 4, 5], [wmbA, wmbB])
        for hc in range(8):
            f1v = f1[hc % 2].rearrange("p (k n) -> p k n", k=8)
            f2v = f2[hc % 2].rearrange("p (i n) -> p i n", i=4)
            load_w(f1v, fc1_d[l].rearrange("(k p) n -> p k n", p=128)[:, :, hc * 512:(hc + 1) * 512], key="f1_%d" % (hc % 2))
            load_w(f2v, fc2_d[l].rearrange("(i p) n -> p i n", p=128)[:, hc * 4:(hc + 1) * 4, :], key="f2_%d" % (hc % 2))
            mod_drain(2)
            for bi in mb:
                t0, n = BLKS[bi]
                for i4 in range(4):
                    i = pp_i[0] % 4
                    pp_i[0] += 1
                    pp = PS[i // 2][:, (i % 2) * 512:(i % 2) * 512 + n]
                    pk = "pp%d" % i
                    for kc in range(8):
                        P.add("pe", lambda g, pp=pp, kc=kc, i4=i4, f1v=f1v: g.matmul(pp, f1v[:, kc, i4 * 128:(i4 + 1) * 128], hTv[:, kc, t0:t0 + n], start=(kc == 0), stop=(kc == 7)),
                              reads=[("hT", kc, bi), "f1_%d" % (hc % 2)], writes=[pk])
                    r = rl[i4 % 2]
                    rk = "rl%d" % (i4 % 2)
                    P.add("act", lambda g, pp=pp, r=r: g.activation(out=r[:, 0:n], in_=pp, func=AF.Relu), reads=[pk], writes=[rk])
                    P.add("pool", lambda g, r=r, i4=i4: g.tensor_tensor(out=HTv[:, i4, t0:t0 + n], in0=r[:, 0:n], in1=r[:, 0:n], op=ALU.mult), reads=[rk], writes=[("HT", i4, bi)])
            for bi in mb:
                t0, n = BLKS[bi]
                s = 1 if bi == 4 else 0
                for nn in range(8):
                    i = pp_i[0] % 4
                    pp_i[0] += 1
                    pp = PS[i // 2][:, (i % 2) * 512:(i % 2) * 512 + n]
                    pk = "pp%d" % i
                    for i4 in range(4):
                        P.add("pe", lambda g, pp=pp, nn=nn, i4=i4, f2v=f2v: g.matmul(pp, f2v[:, i4, nn * 128:(nn + 1) * 128], HTv[:, i4, t0:t0 + n], start=(i4 == 0), stop=(i4 == 3)),
                              reads=[("HT", i4, bi), "f2_%d" % (hc % 2)], writes=[pk])
                    P.add("dve", lambda g, pp=pp, nn=nn: g.scalar_tensor_tensor(out=xTv[:, nn, t0:t0 + n], in0=pp, scalar=modcol(l, 5, nn, s), in1=xTv[:, nn, t0:t0 + n], op0=ALU.mult, op1=ALU.add),
                          reads=[pk, "modT", ("xT", nn, bi)], writes=[("xT", nn, bi)])
        mod_flush()
        mod_q["alias0"] = []

    ar.reset()
    stg = [ar.f32(1024), ar.f32(1024)]
    for ti in range(16):
        s = stg[ti % 2]
        sk = "ostg%d" % (ti % 2)
        pp = PS[ti % 2]
        pk = "ps%d" % (ti % 2)
        for kc in range(8):
            P.add("pe", lambda g, pp=pp, kc=kc, ti=ti: g.transpose(pp[:, kc * 128:(kc + 1) * 128], xTv[:, kc, ti * 128:(ti + 1) * 128], ident[:, :]),
                  reads=[("xT", kc, ti // 4), "ident"], writes=[pk])
        P.add("act", lambda g, s=s, pp=pp: g.copy(out=s[:, 0:512], in_=pp[:, 0:512]), reads=[pk], writes=[sk + "a"])
        P.add("dve", lambda g, s=s, pp=pp: g.tensor_copy(out=s[:, 512:1024], in_=pp[:, 512:1024]), reads=[pk], writes=[sk + "b"])
        P.add("sp", lambda g, s=s, ti=ti: g.dma_start(out=y_d[ti * 128:(ti + 1) * 128, :], in_=s), reads=[sk + "a", sk + "b"], dma="out%d" % (ti % 2))
    P.add("sp", lambda g: g.nop(), extra_deps=[o.id for o in P.ops if o.dma is not None and str(o.dma).startswith("out")])
    P.emit(st)
    st.close()
    return nc


def _host_consts():
    import ml_dtypes
    ident = np.eye(128, dtype=np.float32)
    R = np.zeros((64, 64), np.float32)
    for f in range(64):
        if (f % 32) < 16:
            R[f, f + 16] = -1.0
        else:
            R[f, f - 16] = 1.0
    R2 = np.zeros((128, 128), np.float32)
    R2[:64, :64] = R
    R2[64:, 64:] = R
    rotT = np.ascontiguousarray(R2.T)
    k = np.arange(128)[:, None]
    q = np.arange(128)[None, :]
    wam = np.concatenate([(k >= q).astype(np.float32), (k <= q).astype(np.float32)], axis=1)
    t = np.arange(2048)
    row = (t // 64).astype(np.float32)
    col = (t % 64).astype(np.float32)
    inv = (np.float32(10000.0) ** (-np.arange(16, dtype=np.float32) / np.float32(16))).astype(np.float32)
    ang = np.zeros((64, 2048), np.float32)
    for f in range(64):
        pos = row if f < 32 else col
        ang[f] = pos * inv[f % 16]
    cos = np.cos(ang).astype(np.float32)
    sin = np.sin(ang).astype(np.float32)
    return ident, rotT, wam, np.concatenate([cos, cos], 0), np.concatenate([sin, sin], 0)


def _nabias(rpb):
    out = np.full((L, 4, 128, 5, 5, 128), NEG, np.float32)
    jj = np.arange(2)[:, None, None, None]
    ck = np.arange(64)[None, :, None, None]
    rr = np.arange(2)[None, None, :, None]
    cq = np.arange(64)[None, None, None, :]
    for v, p in enumerate(VAR_REP):
        base = min(max(2 * p - 4, 0), 22)
        for c in range(5):
            rk = base + 2 * c + jj
            rq = 2 * p + rr
            start = np.clip(rq - 4, 0, 24)
            cs = np.clip(cq - 8, 0, 48)
            valid = (rk >= start) & (rk <= start + 7) & (rk <= 31) & (ck >= cs) & (ck < cs + 16)
            dr = np.clip(rk - rq + 7, 0, 14)
            dc = np.clip(ck - cq, -15, 15) + 15
            valid = np.broadcast_to(valid, (2, 64, 2, 64))
            drb = np.broadcast_to(dr, (2, 64, 2, 64))
            dcb = np.broadcast_to(dc, (2, 64, 2, 64))
            g = rpb[:, :, drb, dcb]
            g = np.where(valid[None, None], g, np.float32(NEG))
            out[:, :, :, v, c, :] = g.reshape(L, 4, 128, 128)
    return out.reshape(L, 4, 128, 5 * 5 * 128)


def _prep(inputs):
    f = lambda a: np.ascontiguousarray(np.asarray(a, dtype=np.float32))
    x, c, ctx, c_ctx = f(inputs["x"]), f(inputs["c"]), f(inputs["ctx"]), f(inputs["c_ctx"])
    w_in, w_o = f(inputs["w_in"]), f(inputs["w_o"])
    wa_q = 1536
    perm = list(range(0, 1536))
    for j in range(4):
        perm += list(range(wa_q + 64 * j, wa_q + 64 * j + 64)) + list(range(wa_q + 64 * (j + 4), wa_q + 64 * (j + 4) + 64))
    perm += list(range(2048, 2304))
    w_in_p = np.ascontiguousarray(w_in[:, :, perm])
    rperm = list(range(0, 512))
    for j in range(4):
        rperm += list(range(512 + 64 * j, 512 + 64 * j + 64)) + list(range(512 + 64 * (j + 4), 512 + 64 * (j + 4) + 64))
    w_o_p = np.ascontiguousarray(w_o[:, rperm, :])
    gout_p = f(inputs["g_out"])[:, rperm]
    col = lambda v: np.ascontiguousarray(v.reshape(-1, 128).T)
    vecs = np.zeros((128, L * 36), np.float32)
    for l in range(L):
        o = l * 36
        vecs[:, o:o + 8] = col(f(inputs["g_norm1"])[l])
        vecs[:, o + 8:o + 16] = col(f(inputs["g_norm2"])[l])
        vecs[:, o + 16:o + 24] = col(gout_p[l])
        for i, nm in enumerate(("na_q_gain", "na_k_gain", "wa_q_gain", "wa_k_gain")):
            vecs[:, o + 24 + i] = np.tile(f(inputs[nm])[l], 2)
        cw = f(inputs["conv_w"])[l]
        for ti in range(2):
            for tap in range(3):
                vecs[:, o + 28 + ti * 3 + tap] = cw[tap, ti * 128:(ti + 1) * 128]
            vecs[:, o + 34 + ti] = f(inputs["conv_bias"])[l, ti * 128:(ti + 1) * 128]
    bm = np.concatenate([col(f(inputs["b_mod"])[l]) for l in range(L)], axis=1)
    sink = np.ascontiguousarray(np.broadcast_to(f(inputs["wa_sink"]).reshape(1, L * 8), (128, L * 8)))
    ident, rotT, wam, cos, sin = _host_consts()
    nab = _nabias(f(inputs["na_rpb"]))
    shared = dict(w_mod=f(inputs["w_mod"]), bm=bm, vecs=vecs, sink=sink, w_in=w_in_p, w_o=w_o_p, w_fc1=f(inputs["w_fc1"]),
                  w_fc2=f(inputs["w_fc2"]), nabias=nab, ident=ident, rotT=rotT, wamask=wam, ropecos=cos, ropesin=sin)
    maps = []
    for b in range(8):
        ccm = np.zeros((128, 16), np.float32)
        ccm[:, 0::2] = c[b].reshape(8, 128).T
        ccm[:, 1::2] = c_ctx.reshape(8, 128).T
        m = dict(shared)
        m.update(x=x[b], ctx=ctx[b], cc=ccm)
        maps.append(m)
    return maps


_NC = {}


def kernel(**inputs):
    maps = _prep(inputs)
    if "nc" not in _NC:
        _NC["nc"] = build()
    res = run_bass_kernel_spmd(_NC["nc"], maps, core_ids=list(range(8)))
    return np.stack([np.asarray(r["y"], dtype=np.float32) for r in res.results], axis=0)
```

```python
import os
import numpy as np
from contextlib import ExitStack
import concourse.bass as bass
import concourse.mybir as mybir
from concourse.bass_utils import run_bass_kernel_spmd

F32 = mybir.dt.float32
BF16 = mybir.dt.bfloat16
ALU = mybir.AluOpType
AF = mybir.ActivationFunctionType
ENGS = ("pe", "act", "dve", "pool", "sp")
L = 2
T = 2304
NEG = -30000.0


class _Op:
    __slots__ = ("id", "eng", "fn", "deps", "dma", "cum", "pos", "waits", "signal", "snap")


class _Rec:
    def __getattr__(self, name):
        def f(*a, **k):
            self.call = (name, a, k)
            return self
        return f


class Prog:
    def __init__(self, nc, same_engine_sync=True):
        self.nc = nc
        self.same = same_engine_sync
        self.ops = []
        self.streams = {e: [] for e in ENGS}
        self.last_w = {}
        self.readers = {}
        self.dma_cum = {}

    def add(self, eng, fn, reads=(), writes=(), dma=None, extra_deps=()):
        op = _Op()
        op.id = len(self.ops)
        rec = _Rec()
        fn(rec)
        op.eng, op.fn, op.dma = eng, rec.call, dma
        op.signal = False
        op.waits = []
        deps = set(extra_deps)
        for k in reads:
            w = self.last_w.get(k)
            if w is not None:
                deps.add(w)
        for k in writes:
            w = self.last_w.get(k)
            if w is not None:
                deps.add(w)
            for r in self.readers.get(k, {}).values():
                deps.update(r)
        deps.discard(op.id)
        op.deps = sorted(deps)
        for k in reads:
            rd = self.readers.setdefault(k, {})
            if dma is not None:
                rd.setdefault("dma", []).append(op.id)
            else:
                rd[eng] = [op.id]
        for k in writes:
            self.last_w[k] = op.id
            self.readers[k] = {}
        if dma is not None:
            self.dma_cum[dma] = self.dma_cum.get(dma, 0) + 16
            op.cum = self.dma_cum[dma]
        op.pos = len(self.streams[eng])
        self.streams[eng].append(op)
        self.ops.append(op)
        return op

    def barrier(self):
        deps = []
        for e in ENGS:
            for o in reversed(self.streams[e]):
                if o.dma is None:
                    deps.append(o.id)
                    break
        lastb = getattr(self, "_lastb", 0)
        deps += [o.id for o in self.ops[lastb:] if o.dma is not None]
        self._lastb = len(self.ops)
        for e in ENGS:
            self.add(e, lambda g: g.nop(), extra_deps=deps)

    def plan(self):
        K = {e: {} for e in ENGS}
        for op in self.ops:
            E = op.eng
            k = K[E]
            dmax = {}
            for d in op.deps:
                dop = self.ops[d]
                if dop.dma is not None:
                    dmax[dop.dma] = max(dmax.get(dop.dma, 0), dop.cum)
            for d in op.deps:
                dop = self.ops[d]
                if dop.dma is not None:
                    key = ("dma", dop.dma)
                    if k.get(key, 0) < dmax[dop.dma]:
                        op.waits.append((key, dmax[dop.dma]))
                        k[key] = dmax[dop.dma]
                else:
                    F = dop.eng
                    if F == E and (not self.same or E == "pe"):
                        continue
                    if k.get(F, -1) >= dop.pos:
                        continue
                    op.waits.append((F, dop.pos))
                    dop.signal = True
                    k[F] = dop.pos
                for kk, vv in dop.snap.items():
                    if k.get(kk, -1) < vv:
                        k[kk] = vv
            op.snap = dict(k)
        self.cnt = {}
        for e in ENGS:
            c = 0
            for op in self.streams[e]:
                if op.dma is None and op.signal:
                    c += 1
                    self.cnt[(e, op.pos)] = c
        nw = sum(len(o.waits) for o in self.ops)
        print(f"[prog] ops={len(self.ops)} per-eng={ {e: len(s) for e, s in self.streams.items()} } waits={nw} "
              f"signals={ {e: sum(1 for o in s if o.signal) for e, s in self.streams.items()} }", flush=True)

    def emit(self, stack):
        nc = self.nc
        self.plan()
        esem = {e: stack.enter_context(nc.semaphore("s_" + e)) for e in ENGS}
        dsem = {k: stack.enter_context(nc.semaphore("d_" + str(k))) for k in self.dma_cum}
        block = stack.enter_context(nc.Block())

        def run(e, g):
            for op in self.streams[e]:
                for (key, val) in op.waits:
                    if isinstance(key, tuple):
                        g.wait_ge(dsem[key[1]], val)
                    else:
                        g.wait_ge(esem[key], self.cnt[(key, val)])
                nm, a_, k_ = op.fn
                ins = getattr(g, nm)(*a_, **k_)
                if op.dma is not None:
                    ins.then_inc(dsem[op.dma], 16)
                elif op.signal:
                    ins.then_inc(esem[e], 1)

        block.tensor(lambda g: run("pe", g))
        block.scalar(lambda g: run("act", g))
        block.vector(lambda g: run("dve", g))
        block.gpsimd(lambda g: run("pool", g))
        block.sync(lambda g: run("sp", g))


BLKS = [(0, 512), (512, 512), (1024, 512), (1536, 512), (2048, 256)]
PAIR_VAR = [0, 1] + [2] * 12 + [3, 4]
VAR_REP = [0, 1, 5, 14, 15]


def build(debug=None):
    nc = bass.Bass("TRN2", target_bir_lowering=False)
    di = lambda n, s: nc.dram_tensor(n, list(s), F32, kind="ExternalInput").ap()
    x_d = di("x", (2048, 1024))
    ctx_d = di("ctx", (256, 1024))
    cc_d = di("cc", (128, 16))
    wmod_d = di("w_mod", (L, 1024, 6144))
    bm_d = di("bm", (128, L * 48))
    vec_d = di("vecs", (128, L * 36))
    sink_d = di("sink", (128, L * 8))
    win_d = di("w_in", (L, 1024, 2304))
    wo_d = di("w_o", (L, 1024, 1024))
    fc1_d = di("w_fc1", (L, 1024, 4096))
    fc2_d = di("w_fc2", (L, 4096, 1024))
    nab_d = di("nabias", (L, 4, 128, 5 * 5 * 128))
    ident_d = di("ident", (128, 128))
    rot_d = di("rotT", (128, 128))
    wam_d = di("wamask", (128, 256))
    cos_d = di("ropecos", (128, 2048))
    sin_d = di("ropesin", (128, 2048))
    y_d = nc.dram_tensor("y", [2048, 1024], F32, kind="ExternalOutput").ap()
    dbg_d = {}
    if debug:
        for n, s in debug.items():
            dbg_d[n] = nc.dram_tensor("dbg_" + n, list(s), F32, kind="ExternalOutput").ap()

    st = ExitStack()
    P = Prog(nc)
    sb = lambda name, shape, dt: st.enter_context(nc.sbuf_tensor("sb_" + name, shape, dt))
    xT = sb("xT", [128, 8 * T], F32)
    xTv = xT[:, :].rearrange("p (k t) -> p k t", k=8)
    hT = sb("hT", [128, 8 * T], BF16)
    hTv = hT[:, :].rearrange("p (k t) -> p k t", k=8)
    ident = sb("ident", [128, 128], F32)
    ones = sb("ones", [128, 128], BF16)
    bd = sb("bd", [128, 128], BF16)
    rotT = sb("rotT", [128, 128], BF16)
    wam = sb("wam", [128, 256], BF16)
    epsc = sb("epsc", [128, 1], F32)
    cc = sb("cc", [128, 16], F32)
    ccb = sb("ccb", [128, 16], BF16)
    bm = sb("bm", [128, L * 48], F32)
    vec = sb("vec", [128, L * 36], F32)
    sinkv = sb("sinkv", [128, L * 8], F32)
    modT = sb("modT", [128, L * 96], F32)
    A12 = sb("A12", [128, L * 32], F32)
    gq8 = sb("gq8", [128, L * 2], F32)
    AR_F = 19328
    AR = sb("arena", [128, AR_F], F32)
    PS = [st.enter_context(nc.psum_tensor("ps%d" % i, [128, 1024], F32)) for i in range(4)]

    class Arena:
        def __init__(s):
            s.off = 0

        def reset(s):
            P.barrier()
            s.off = 0

        def f32(s, n):
            a = s.off
            s.off += n
            assert s.off <= AR_F, ("arena overflow", s.off)
            return AR[:, a:a + n]

        def bf(s, n):
            m = (n + 1) // 2
            a = s.off
            s.off += m
            assert s.off <= AR_F, ("arena overflow", s.off)
            return AR[:, a:a + m].bitcast(BF16)[:, 0:n]

    ar = Arena()
    uid = [0]

    def U(p="k"):
        uid[0] += 1
        return "%s%d" % (p, uid[0])

    def modcol(l, chunk, kc, s):
        i = l * 96 + (chunk * 8 + kc) * 2 + s
        return modT[:, i:i + 1]

    def Acol(l, which, kc, s):
        i = l * 32 + which * 16 + kc * 2 + s
        return A12[:, i:i + 1]

    def vcol(l, i):
        j = l * 36 + i
        return vec[:, j:j + 1]

    P.add("sp", lambda g: g.dma_start(out=ident[:, :], in_=ident_d), writes=["ident"], dma="c0ident")
    P.add("sp", lambda g: g.dma_start(out=cc[:, :], in_=cc_d), writes=["cc"], dma="c0cc")
    P.add("sp", lambda g: g.dma_start(out=bm[:, :], in_=bm_d), writes=["bm"], dma="c0bm")
    P.add("sp", lambda g: g.dma_start(out=vec[:, :], in_=vec_d), writes=["vec"], dma="c0vec")
    P.add("sp", lambda g: g.dma_start(out=sinkv[:, :], in_=sink_d), writes=["sinkv"], dma="c0sinkv")
    P.add("pool", lambda g: g.dma_start(out=rotT[:, :], in_=rot_d), writes=["rotT"], dma="c1rotT")
    P.add("pool", lambda g: g.dma_start(out=wam[:, :], in_=wam_d), writes=["wam"], dma="c1wam")
    P.add("pool", lambda g: g.memset(ones[:, :], 1.0), writes=["ones"])
    P.add("pool", lambda g: g.memset(bd[:, :], 0.0), writes=["bd"])
    P.add("pool", lambda g: g.memset(bd[0:64, 0:64], 1.0), reads=["bd"], writes=["bd"])
    P.add("pool", lambda g: g.memset(bd[64:128, 64:128], 1.0), reads=["bd"], writes=["bd"])
    P.add("pool", lambda g: g.memset(epsc[:, :], 1e-6), writes=["epsc"])
    P.add("act", lambda g: g.activation(out=ccb[:, :], in_=cc[:, :], func=AF.Silu), reads=["cc"], writes=["ccb"])
    P.add("act", lambda g: g.activation(out=sinkv[:, :], in_=sinkv[:, :], func=AF.Exp), reads=["sinkv"], writes=["sinkv"])
    for l in range(L):
        for i, c in enumerate((24, 26)):
            P.add("dve", lambda g, l=l, i=i, c=c: g.tensor_scalar(out=gq8[:, l * 2 + i:l * 2 + i + 1], in0=vcol(l, c), scalar1=0.125, scalar2=None, op0=ALU.mult),
                  reads=["vec"], writes=["gq8"])

    stg = [ar.f32(1024), ar.f32(1024)]
    for ti in range(18):
        s = stg[ti % 2]
        sk = "stg%d" % (ti % 2)
        src = x_d[ti * 128:(ti + 1) * 128, :] if ti < 16 else ctx_d[(ti - 16) * 128:(ti - 15) * 128, :]
        P.add("sp", lambda g, s=s, src=src: g.dma_start(out=s, in_=src), writes=[sk], dma=sk)
        pp = PS[ti % 2]
        pk = "ps%d" % (ti % 2)
        for kc in range(8):
            P.add("pe", lambda g, pp=pp, s=s, kc=kc: g.transpose(pp[:, kc * 128:(kc + 1) * 128], s[:, kc * 128:(kc + 1) * 128], ident[:, :]),
                  reads=[sk, "ident"], writes=[pk])
        for hf in range(2):
            eng = "act" if hf == 0 else "dve"
            outv = xTv[:, hf * 4:(hf + 1) * 4, ti * 128:(ti + 1) * 128]
            inv = pp[:, hf * 512:(hf + 1) * 512].rearrange("p (k t) -> p k t", k=4)
            wk = [("xT", kc, ti // 4 if ti < 16 else 4) for kc in range(hf * 4, hf * 4 + 4)]
            if eng == "act":
                P.add("act", lambda g, o=outv, i=inv: g.copy(out=o, in_=i), reads=[pk], writes=wk)
            else:
                P.add("dve", lambda g, o=outv, i=inv: g.tensor_copy(out=o, in_=i), reads=[pk], writes=wk)

    pp_i = [0]
    mod_cnt = [0]
    mod_q = {"pieces": [], "inflight": []}

    def mod_pieces(l, chunks, bufs):
        out = []
        mod_q["l"], mod_q["chunks"] = l, list(chunks)
        mod_q["need_evac"] = True
        for ch in chunks:
            for q4 in range(2):
                def dma_fn(ch=ch, q4=q4, st_={}):
                    bi_ = mod_cnt[0] % 2
                    mod_cnt[0] += 1
                    st_["b"] = bi_
                    wv = bufs[bi_].rearrange("p (k n) -> p k n", k=8)
                    c0 = ch * 1024 + q4 * 512
                    src = wmod_d[l].rearrange("(k p) n -> p k n", p=128)[:, :, c0:c0 + 512]
                    P.add("pool", lambda g: g.dma_start(out=wv, in_=src), writes=["wmb%d" % bi_] + list(mod_q.get("alias%d" % bi_, [])), dma="wmb%d" % bi_)
                    return st_

                def mm_fn(st_, ch=ch, q4=q4):
                    bi_ = st_["b"]
                    wk = "wmb%d" % bi_
                    wv = bufs[bi_].rearrange("p (k n) -> p k n", k=8)
                    for j in range(4):
                        nt = ch * 8 + q4 * 4 + j
                        for kc in range(8):
                            P.add("pe", lambda g: g.matmul(PS[3][:, 512 + nt * 2:512 + nt * 2 + 2], wv[:, kc, j * 128:(j + 1) * 128], ccb[:, kc * 2:kc * 2 + 2], start=(kc == 0), stop=(kc == 7)),
                                  reads=[wk, "ccb"], writes=["ps3b"])
                out.append((dma_fn, mm_fn))
        return out


    def mod_drain(k):
        for st_, m in mod_q["inflight"]:
            m(st_)
        mod_q["inflight"] = []
        while mod_q["pieces"] and len(mod_q["inflight"]) < 2:
            d, m = mod_q["pieces"].pop(0)
            mod_q["inflight"].append((d(), m))

    def mod_flush():
        if not mod_q.get("need_evac"):
            return
        mod_q["need_evac"] = False
        while mod_q["pieces"] or mod_q["inflight"]:
            mod_drain(1)
        l, chunks = mod_q["l"], mod_q["chunks"]
        c0, c1 = chunks[0], chunks[-1] + 1
        P.add("dve", lambda g: g.tensor_tensor(out=modT[:, l * 96 + c0 * 16:l * 96 + c1 * 16].rearrange("p (c s) -> p c s", s=2), in0=PS[3][:, 512 + c0 * 16:512 + c1 * 16].rearrange("p (c s) -> p c s", s=2),
                                               in1=bm[:, l * 48 + c0 * 8:l * 48 + c1 * 8].unsqueeze(2).to_broadcast([128, (c1 - c0) * 8, 2]), op=ALU.add),
              reads=["ps3b", "bm"], writes=["modT"])
        for ch in chunks:
            if ch in (1, 4):
                which = 0 if ch == 1 else 1
                gcol = 0 if ch == 1 else 8
                P.add("dve", lambda g: g.scalar_tensor_tensor(
                    out=A12[:, l * 32 + which * 16:l * 32 + which * 16 + 16].rearrange("p (k s) -> p k s", s=2),
                    in0=modT[:, l * 96 + ch * 16:l * 96 + ch * 16 + 16].rearrange("p (k s) -> p k s", s=2), scalar=1.0,
                    in1=vec[:, l * 36 + gcol:l * 36 + gcol + 8].unsqueeze(2).to_broadcast([128, 8, 2]), op0=ALU.add, op1=ALU.mult),
                    reads=["modT", "vec"], writes=["A12"])

    wmb0 = [ar.bf(8 * 512), ar.bf(8 * 512)]
    mod_q["pieces"] = mod_pieces(0, [0, 1], wmb0)
    mod_flush()

    def norm_all(l, which, blks, hook=None):
        sq = [ar.bf(512), ar.bf(512)]
        lnv = ar.f32(512)
        rstd = [ar.f32(512), ar.f32(512)]
        tmp = [ar.f32(512), ar.f32(512)]
        shc = 0 if which == 0 else 3
        for bi in blks:
            t0, n = BLKS[bi]
            s = 1 if bi == 4 else 0
            pst = PS[3]
            for kc in range(8):
                q = sq[kc % 2]
                qk = "nsq%d" % (kc % 2)
                P.add("pool", lambda g, q=q, kc=kc: g.tensor_tensor(out=q[:, 0:n], in0=xTv[:, kc, t0:t0 + n], in1=xTv[:, kc, t0:t0 + n], op=ALU.mult),
                      reads=[("xT", kc, bi)], writes=[qk])
                P.add("pe", lambda g, q=q, kc=kc: g.matmul(pst[:, 0:n], ones[:, :], q[:, 0:n], start=(kc == 0), stop=(kc == 7)),
                      reads=[qk, "ones"], writes=["ps3a"])
            r = rstd[bi % 2]
            rk = "nrstd%d" % (bi % 2)
            P.add("act", lambda g: g.activation(out=lnv[:, 0:n], in_=pst[:, 0:n], func=AF.Ln, bias=epsc[:, 0:1], scale=1.0 / 1024), reads=["ps3a", "epsc"], writes=["nlnv"])
            P.add("act", lambda g, r=r: g.activation(out=r[:, 0:n], in_=lnv[:, 0:n], func=AF.Exp, scale=-0.5), reads=["nlnv"], writes=[rk])
            for kc in range(8):
                tm = tmp[kc % 2]
                tk = "ntmp%d" % (kc % 2)
                P.add("dve", lambda g, tm=tm, kc=kc, r=r: g.scalar_tensor_tensor(out=tm[:, 0:n], in0=xTv[:, kc, t0:t0 + n], scalar=Acol(l, which, kc, s), in1=r[:, 0:n], op0=ALU.mult, op1=ALU.mult),
                      reads=[("xT", kc, bi), rk, "A12"], writes=[tk])
                P.add("act", lambda g, tm=tm, kc=kc: g.activation(out=hTv[:, kc, t0:t0 + n], in_=tm[:, 0:n], func=AF.Identity, bias=modcol(l, shc, kc, s), scale=1.0),
                      reads=[tk, "modT"], writes=[("hT", kc, bi)])
            if hook is not None:
                hook()

    def proj(wv, col0, bi, ncols=128, wkey="wb"):
        t0, n = BLKS[bi]
        i = pp_i[0] % 4
        pp_i[0] += 1
        pp = PS[i // 2][:, (i % 2) * 512:(i % 2) * 512 + n]
        pk = "pp%d" % i
        for kc in range(8):
            P.add("pe", lambda g, pp=pp, kc=kc: g.matmul(pp, wv[:, kc, col0:col0 + 128], hTv[:, kc, t0:t0 + n], start=(kc == 0), stop=(kc == 7)),
                  reads=[("hT", kc, bi), wkey], writes=[pk])
        return pp, pk

    hn_bufs = {}
    hn_single = [False]

    def headnorm(pp, pk, n, gain_ap, out_ap, out_keys, out_reads=()):
        if "sq" not in hn_bufs:
            if hn_single[0]:
                a_, b_ = ar.bf(512), ar.f32(512)
                hn_bufs["sq"] = [a_, a_]
                hn_bufs["r"] = [b_, b_]
            else:
                hn_bufs["sq"] = [ar.bf(512), ar.bf(512)]
                hn_bufs["r"] = [ar.f32(512), ar.f32(512)]
            hn_bufs["i"] = 0
        i = 0 if hn_single[0] else hn_bufs["i"] % 2
        hn_bufs["i"] += 1
        q, r = hn_bufs["sq"][i], hn_bufs["r"][i]
        qk, rk = "hsq%d" % i, "hr%d" % i
        hb, hbk = 0, "ps3a"
        pst = PS[3][:, hb:hb + n]
        P.add("act", lambda g: g.activation(out=q[:, 0:n], in_=pp, func=AF.Square), reads=[pk], writes=[qk])
        P.add("pe", lambda g: g.matmul(pst, bd[:, :], q[:, 0:n], start=True, stop=True), reads=[qk, "bd"], writes=[hbk])
        P.add("act", lambda g: g.activation(out=r[:, 0:n], in_=pst, func=AF.Ln, bias=epsc[:, 0:1], scale=1.0 / 64), reads=[hbk, "epsc"], writes=[rk])
        P.add("act", lambda g: g.activation(out=r[:, 0:n], in_=r[:, 0:n], func=AF.Exp, scale=-0.5), reads=[rk], writes=[rk])
        if len(out_ap.shape) == 2:
            i0, i1 = pp, r[:, 0:n]
        else:
            i0, i1 = pp.rearrange("p (a b) -> p a b", b=128), r[:, 0:n].rearrange("p (a b) -> p a b", b=128)
        P.add("dve", lambda g: g.scalar_tensor_tensor(out=out_ap, in0=i0, scalar=gain_ap, in1=i1, op0=ALU.mult, op1=ALU.mult),
              reads=[pk, rk, "vec", "gq8"] + list(out_reads), writes=out_keys)

    def vproj(wv, col0, ncols, bi, dst_fn, wkey="wb"):
        t0, n = BLKS[bi]
        for tt in range(n // 128):
            tile = (t0 // 128) + tt
            i = pp_i[0] % 4
            pp_i[0] += 1
            pp = PS[i // 2][:, (i % 2) * 512:(i % 2) * 512 + ncols]
            pk = "pp%d" % i
            for kc in range(8):
                P.add("pe", lambda g, pp=pp, kc=kc, tile=tile: g.matmul(pp, hTv[:, kc, tile * 128:(tile + 1) * 128], wv[:, kc, col0:col0 + ncols], start=(kc == 0), stop=(kc == 7)),
                      reads=[("hT", kc, bi), wkey], writes=[pk])
            dst_fn(tile, pp, pk)

    def group_out(l, o32, okey, ntl, n, bi, gcol0, wov, nch):
        t0, _ = BLKS[bi]
        s = 1 if bi == 4 else 0
        okeys = list(okey) if isinstance(okey, list) else [okey]
        sq = [ar_go["sq"][0], ar_go["sq"][1]]
        r = ar_go["r"]
        onb = ar_go["onb"]
        pst = PS[3][:, 0:n]
        for t in range(ntl):
            P.add("pool", lambda g, t=t: g.tensor_tensor(out=sq[t % 2][:, 0:n], in0=o32[:, t, :], in1=o32[:, t, :], op=ALU.mult), reads=okeys, writes=["gsq%d" % (t % 2)])
            P.add("pe", lambda g, t=t: g.matmul(pst, ones[:, :], sq[t % 2][:, 0:n], start=(t == 0), stop=(t == ntl - 1)), reads=["gsq%d" % (t % 2), "ones"], writes=["ps3a"])
        P.add("act", lambda g: g.activation(out=r[:, 0:n], in_=pst, func=AF.Ln, bias=epsc[:, 0:1], scale=1.0 / nch), reads=["ps3a", "epsc"], writes=["gr"])
        P.add("act", lambda g: g.activation(out=r[:, 0:n], in_=r[:, 0:n], func=AF.Exp, scale=-0.5), reads=["gr"], writes=["gr"])
        for t in range(ntl):
            P.add("dve", lambda g, t=t: g.scalar_tensor_tensor(out=onb[:, t * 512:t * 512 + n], in0=o32[:, t, :], scalar=vcol(l, 16 + gcol0 + t), in1=r[:, 0:n], op0=ALU.mult, op1=ALU.mult),
                  reads=okeys + ["gr", "vec"], writes=[("onb", t)])
        for nn in range(8):
            i = pp_i[0] % 4
            pp_i[0] += 1
            pp = PS[i // 2][:, (i % 2) * 512:(i % 2) * 512 + n]
            pk = "pp%d" % i
            for t in range(ntl):
                P.add("pe", lambda g, pp=pp, t=t, nn=nn: g.matmul(pp, wov[:, t, nn * 128:(nn + 1) * 128], onb[:, t * 512:t * 512 + n], start=(t == 0), stop=(t == ntl - 1)),
                      reads=[("onb", t), "wo"], writes=[pk])
            P.add("dve", lambda g, pp=pp, nn=nn: g.scalar_tensor_tensor(out=xTv[:, nn, t0:t0 + n], in0=pp, scalar=modcol(l, 2, nn, s), in1=xTv[:, nn, t0:t0 + n], op0=ALU.mult, op1=ALU.add),
                  reads=[pk, "modT", ("xT", nn, bi)], writes=[("xT", nn, bi)])

    ar_go = {}

    def go_alloc():
        ar_go["sq"] = [ar.bf(512), ar.bf(512)]
        ar_go["r"] = ar.f32(512)
        ar_go["onb"] = ar.bf(4 * 512)

    def load_w(dst_view, src, key="wb"):
        P.add("pool", lambda g: g.dma_start(out=dst_view, in_=src), writes=[key], dma=key)

    def dump(name, ap, reads=()):
        if name in dbg_d:
            P.barrier()
            P.add("pool", lambda g: g.dma_start(out=dbg_d[name], in_=ap), reads=list(reads), writes=[], dma="out_" + name)

    for l in range(L):
        last = (l == L - 1)
        lat = [0, 1, 2, 3]
        allb = [0, 1, 2, 3, 4]
        qb = lat if last else allb
        ar.reset()
        if l == 0:
            wmb1 = [ar.bf(8 * 512), ar.bf(8 * 512)]
            mod_q["pieces"] = mod_pieces(0, [2, 3, 4, 5], wmb1)
        norm_all(l, 0, allb, hook=(lambda: mod_drain(2)))
        mod_flush()
        if l == 0:
            dump("h0", hT[:, :])

        o_na = None
        for tpair in range(2):
            if tpair == 0:
                ar.reset()
            else:
                ar.off = 0
            hn_bufs.clear()
            wbt = ar.bf(8 * 384)
            wv = wbt.rearrange("p (k n) -> p k n", k=8)
            wsrc = win_d[l].rearrange("(k p) n -> p k n", p=128)
            load_w(wv[:, :, 0:128], wsrc[:, :, tpair * 128:(tpair + 1) * 128])
            load_w(wv[:, :, 128:256], wsrc[:, :, 256 + tpair * 128:256 + (tpair + 1) * 128])
            load_w(wv[:, :, 256:384], wsrc[:, :, 512 + tpair * 128:512 + (tpair + 1) * 128])
            nak = ar.bf(T)
            vna = ar.bf(18 * 192)
            vnav = vna.rearrange("p (t c) -> p t c", c=192)
            naq = [ar.bf(512), ar.bf(512)]
            Eint = ar.f32(2 * 640)
            Eedge = [ar.f32(640), ar.f32(640)]
            ptmp = [ar.f32(640) for _ in range(4)]
            PT = [ar.bf(896) for _ in range(4)]
            rec = [ar.f32(256) for _ in range(4)]
            ona = AR[:, AR_F - 2 * T:AR_F]
            onav = ona.rearrange("p (t n) -> p t n", t=2)
            assert ar.off <= AR_F - 2 * T
            P.add("pool", lambda g: g.memset(vnav[:, :, 64:128], 1.0), writes=["vna"])
            for hh in range(2):
                h = tpair * 2 + hh
                src = nab_d[l, h].rearrange("p (v x) -> p v x", v=5)[:, 2, :]
                P.add("sp", lambda g, hh=hh, src=src: g.dma_start(out=Eint[:, hh * 640:(hh + 1) * 640], in_=src), writes=["Eint"], dma="Eint")
            P.add("act", lambda g: g.activation(out=Eint, in_=Eint, func=AF.Exp), reads=["Eint"], writes=["Eint"])
            for bi in allb:
                t0, n = BLKS[bi]
                pp, pk = proj(wv, 128, bi)
                headnorm(pp, pk, n, vcol(l, 25), nak[:, t0:t0 + n], [("nak", bi)])

                def vdst(tile, pp, pk):
                    P.add("act", lambda g: g.copy(out=vnav[:, tile, :].rearrange("p (a b) -> p a b", b=64)[:, 0:3:2, :], in_=pp.rearrange("p (a b) -> p a b", b=64)),
                          reads=[pk, "vna"], writes=[("vna", tile)])
                vproj(wv, 256, 128, bi, vdst)
            pend = []
            na_cnt = [0]
            for bi in qb:
                t0, n = BLKS[bi]
                if pend:
                    pend.pop()()
                q = naq[bi % 2]
                qk = "naq%d" % (bi % 2)
                pp, pk = proj(wv, 0, bi)
                headnorm(pp, pk, n, gq8[:, l * 2:l * 2 + 1], q[:, 0:n], [qk])
                if bi < 4:
                    for pi in range(4):
                        p = bi * 4 + pi
                        var = PAIR_VAR[p]
                        base = min(max(2 * p - 4, 0), 22)
                        st_i = na_cnt[0] % 2
                        na_cnt[0] += 1
                        ktiles = [base // 2 + c for c in range(5)] + [16, 17]
                        Es = []
                        for hh in range(2):
                            h = tpair * 2 + hh
                            if var == 2:
                                Es.append((Eint[:, hh * 640:(hh + 1) * 640], "Eint"))
                            else:
                                E = Eedge[hh]
                                Ek = "Eedge%d" % hh
                                src = nab_d[l, h].rearrange("p (v x) -> p v x", v=5)[:, var, :]
                                P.add("sp", lambda g: g.dma_start(out=E, in_=src), writes=[Ek], dma=Ek)
                                P.add("act", lambda g: g.activation(out=E, in_=E, func=AF.Exp), reads=[Ek], writes=[Ek])
                                Es.append((E, Ek))
                        scs = [(PS[2], ["ps2"]), (PS[1], ["pp2", "pp3"])]
                        for c, kt in enumerate(ktiles):
                            for hh in range(2):
                                lo, hi = hh * 64, hh * 64 + 64
                                sc, sck = scs[hh]
                                P.add("pe", lambda g: g.matmul(sc[:, c * 128:(c + 1) * 128], nak[lo:hi, kt * 128:(kt + 1) * 128], q[lo:hi, pi * 128:(pi + 1) * 128], start=True, stop=True),
                                      reads=[("nak", kt // 4 if kt < 16 else 4), qk], writes=sck)
                        for hh in range(2):
                            sc, sck = scs[hh]
                            E, Ek = Es[hh]
                            pt = ptmp[st_i * 2 + hh]
                            ptk = "ptmp%d" % (st_i * 2 + hh)
                            PTi = PT[st_i * 2 + hh]
                            PTk = "PT%d" % (st_i * 2 + hh)
                            P.add("act", lambda g: g.activation(out=pt[:, 0:512], in_=sc[:, 0:512], func=AF.Exp), reads=sck, writes=[ptk])
                            P.add("act", lambda g: g.activation(out=pt[:, 512:640], in_=sc[:, 512:640], func=AF.Exp), reads=sck + [ptk], writes=[ptk])
                            P.add("act", lambda g: g.activation(out=PTi[:, 640:896], in_=sc[:, 640:896], func=AF.Exp), reads=sck, writes=[PTk])
                            P.add("pool", lambda g: g.tensor_tensor(out=PTi[:, 0:640], in0=pt[:, 0:640], in1=E, op=ALU.mult), reads=[ptk, Ek, PTk], writes=[PTk])

                        def stageB(st_i=st_i, ktiles=ktiles, pi=pi, t0=t0, bi=bi):
                            for hh in range(2):
                                lo, hi = hh * 64, hh * 64 + 64
                                dlo, dhi = (64, 128) if hh == 0 else (0, 64)
                                vc0 = 0 if hh == 0 else 64
                                PTi = PT[st_i * 2 + hh]
                                PTk = "PT%d" % (st_i * 2 + hh)
                                pv = PS[3][:, 512 + hh * 128:512 + (hh + 1) * 128]
                                pvk = "pv%d" % hh
                                for c, kt in enumerate(ktiles):
                                    P.add("pe", lambda g: g.matmul(pv, vnav[:, kt, vc0:vc0 + 128], PTi[:, c * 128:(c + 1) * 128], start=(c == 0), stop=(c == 6)),
                                          reads=[("vna", kt), PTk], writes=[pvk, "ps3b"])
                                rc = rec[st_i * 2 + hh]
                                rck = "rec%d" % (st_i * 2 + hh)
                                P.add("dve", lambda g: g.reciprocal(out=rc[lo:hi, 0:128], in_=pv[dlo:dhi, :]), reads=[pvk, "ps3b"], writes=[rck])
                                P.add("dve", lambda g: g.tensor_tensor(out=onav[lo:hi, tpair, t0 + pi * 128:t0 + (pi + 1) * 128], in0=pv[lo:hi, :], in1=rc[lo:hi, 0:128], op=ALU.mult),
                                      reads=[pvk, "ps3b", rck], writes=[("ona", tpair, bi, hh)])
                        if pend:
                            pend.pop()()
                        pend.append(stageB)
                for hh in range(2):
                    h = tpair * 2 + hh
                    lo, hi = hh * 64, hh * 64 + 64
                    dlo, dhi = (64, 128) if hh == 0 else (0, 64)
                    vc0 = 0 if hh == 0 else 64
                    if bi < 4:
                        pass
                    else:
                        sc = PS[2]
                        sck = "ps2"
                        for c in range(2):
                            P.add("pe", lambda g, c=c: g.matmul(sc[:, c * 256:(c + 1) * 256], nak[lo:hi, 2048 + c * 128:2048 + (c + 1) * 128], q[lo:hi, 0:256], start=True, stop=True),
                                  reads=[("nak", 4), qk], writes=[sck])
                        PTi = PT[hh]
                        PTk = "PT%d" % hh
                        P.add("act", lambda g, PTi=PTi: g.activation(out=PTi[:, 0:512], in_=sc[:, 0:512], func=AF.Exp), reads=[sck], writes=[PTk])
                        pv = PS[3][:, 512 + hh * 256:512 + (hh + 1) * 256]
                        pvk = "pv%d" % hh
                        for c in range(2):
                            P.add("pe", lambda g, c=c, pv=pv, PTi=PTi: g.matmul(pv, vnav[:, 16 + c, vc0:vc0 + 128], PTi[:, c * 256:(c + 1) * 256], start=(c == 0), stop=(c == 1)),
                                  reads=[("vna", 16 + c), PTk], writes=[pvk, "ps3b"])
                        rc = rec[hh]
                        rck = "rec%d" % hh
                        P.add("dve", lambda g, rc=rc, pv=pv: g.reciprocal(out=rc[lo:hi, 0:256], in_=pv[dlo:dhi, :]), reads=[pvk, "ps3b"], writes=[rck])
                        P.add("dve", lambda g, rc=rc, pv=pv: g.tensor_tensor(out=onav[lo:hi, tpair, 2048:2304], in0=pv[lo:hi, :], in1=rc[lo:hi, 0:256], op=ALU.mult),
                              reads=[pvk, "ps3b", rck], writes=[("ona", tpair, bi, hh)])
            if pend:
                pend.pop()()
        if l == 0:
            dump("ona", ona)
        ar.reset()
        go_alloc()
        wo = ar.bf(2 * 1024)
        wov = wo.rearrange("p (t n) -> p t n", t=2)
        load_w(wov, wo_d[l].rearrange("(t p) n -> p t n", p=128)[:, 0:2, :], key="wo")
        assert ar.off <= AR_F - 2 * T
        for bi in qb:
            t0, n = BLKS[bi]
            okeys = [("ona", tp, bi, hh) for tp in range(2) for hh in range(2)]
            group_out(l, onav[:, :, t0:t0 + n], okeys, 2, n, bi, 0, wov, 256.0)
        if l == 0:
            dump("xna", xT[:, :])

        ar.reset()
        hn_bufs.clear()
        go_alloc()
        wbt = ar.bf(8 * 768)
        wv = wbt.rearrange("p (k n) -> p k n", k=8)
        load_w(wv, win_d[l].rearrange("(k p) n -> p k n", p=128)[:, :, 768:1536])
        wo = ar.bf(2 * 1024)
        wov = wo.rearrange("p (t n) -> p t n", t=2)
        load_w(wov, wo_d[l].rearrange("(t p) n -> p t n", p=128)[:, 2:4, :], key="wo")
        UW = 2308
        u = ar.f32(2 * UW)
        uv = u.rearrange("p (t n) -> p t n", t=2)
        bb = ar.f32(2 * T)
        bbv = bb.rearrange("p (t n) -> p t n", t=2)
        xs = [ar.f32(512), ar.f32(512)]
        yb = [ar.f32(1024), ar.f32(1024)]
        P.add("pool", lambda g: g.memset(u, 0.0), writes=["u"])
        cb = qb
        for bi in cb:
            t0, n = BLKS[bi]
            uo = 1 + t0 if bi < 4 else 2 + t0
            for ti in range(2):
                px, pxk = proj(wv, ti * 128, bi)
                xsi = xs[ti]
                P.add("act", lambda g, px=px, xsi=xsi: g.copy(out=xsi[:, 0:n], in_=px), reads=[pxk], writes=["xs%d" % ti])
                pc, pck = proj(wv, 512 + ti * 128, bi)
                P.add("dve", lambda g, pc=pc, xsi=xsi, ti=ti: g.tensor_tensor(out=uv[:, ti, uo:uo + n], in0=pc, in1=xsi[:, 0:n], op=ALU.mult),
                      reads=[pck, "xs%d" % ti, "u"], writes=[("u", ti, bi)])
                pb, pbk = proj(wv, 256 + ti * 128, bi)
                P.add("act", lambda g, pb=pb, ti=ti: g.copy(out=bbv[:, ti, t0:t0 + n], in_=pb), reads=[pbk], writes=[("bb", ti, bi)])
        pieces = [(0, 1024, [0, 1]), (1024, 1024, [2, 3])] + ([] if last else [(2048, 256, [4])])
        for ti in range(2):
            for pi_, (t0, n, bis) in enumerate(pieces):
                uo = 1 + t0 if t0 < 2048 else 2 + t0
                y = yb[pi_ % 2]
                yk = "yb%d" % (pi_ % 2)
                ukeys = [("u", ti, b) for b in range(5) if (b in cb)]
                P.add("act", lambda g, y=y, ti=ti, uo=uo, n=n: g.activation(out=y[:, 0:n], in_=uv[:, ti, uo:uo + n], func=AF.Identity, bias=vcol(l, 34 + ti), scale=vcol(l, 28 + ti * 3 + 1)),
                      reads=ukeys + ["vec", "u"], writes=[yk])
                P.add("dve", lambda g, y=y, ti=ti, uo=uo, n=n: g.scalar_tensor_tensor(out=y[:, 0:n], in0=uv[:, ti, uo - 1:uo - 1 + n], scalar=vcol(l, 28 + ti * 3 + 0), in1=y[:, 0:n], op0=ALU.mult, op1=ALU.add),
                      reads=ukeys + ["vec", yk, "u"], writes=[yk])
                P.add("dve", lambda g, y=y, ti=ti, uo=uo, n=n: g.scalar_tensor_tensor(out=y[:, 0:n], in0=uv[:, ti, uo + 1:uo + 1 + n], scalar=vcol(l, 28 + ti * 3 + 2), in1=y[:, 0:n], op0=ALU.mult, op1=ALU.add),
                      reads=ukeys + ["vec", yk, "u"], writes=[yk])
                P.add("pool", lambda g, y=y, ti=ti, t0=t0, n=n: g.tensor_tensor(out=bbv[:, ti, t0:t0 + n], in0=bbv[:, ti, t0:t0 + n], in1=y[:, 0:n], op=ALU.mult),
                      reads=[yk] + [("bb", ti, b) for b in bis], writes=[("bb", ti, b) for b in bis])
        if l == 0:
            dump("ocv", bb)
        for bi in cb:
            t0, n = BLKS[bi]
            group_out(l, bbv[:, :, t0:t0 + n], [("bb", 0, bi), ("bb", 1, bi)], 2, n, bi, 2, wov, 256.0)
        if l == 0:
            dump("xcv", xT[:, :])

        ar.reset()
        hn_bufs.clear()
        hn_single[0] = True
        go_alloc()
        wqv = ar.bf(8 * 512).rearrange("p (k n) -> p k n", k=8)
        wkv_flat = ar.bf(8 * 256)
        wkvv = wkv_flat.rearrange("p (k n) -> p k n", k=8)
        load_w(wkvv, win_d[l].rearrange("(k p) n -> p k n", p=128)[:, :, 2048:2304], key="wbkv")
        load_w(wqv, win_d[l].rearrange("(k p) n -> p k n", p=128)[:, :, 1536:2048], key="wbq")
        wo = ar.bf(4 * 1024)
        wov = wo.rearrange("p (t n) -> p t n", t=4)
        load_w(wov, wo_d[l].rearrange("(t p) n -> p t n", p=128)[:, 4:8, :], key="wo")
        wak = ar.bf(T)
        vwa = ar.bf(18 * 192)
        vwav = vwa.rearrange("p (t c) -> p t c", c=192)
        waqs = [ar.bf(2048), wkv_flat]
        waqvs = [w_.rearrange("p (a j q) -> p a j q", a=4, j=4) for w_ in waqs]
        cosb = ar.f32(512)
        sinb = ar.f32(512)
        tn = ar.f32(512)
        tnb = ar.bf(512)
        t1 = ar.f32(512)
        PTv2 = [ar.bf(5 * 512).rearrange("p (c n) -> p c n", c=5) for _ in range(2)]
        owa = ar.f32(4 * 512)
        owav = owa.rearrange("p (j n) -> p j n", j=4)
        recw = ar.f32(512)
        t2 = recw
        wa_cnt = [0]
        sc_cnt = [0]
        P.add("pool", lambda g: g.memset(vwav[:, :, 64:128], 1.0), writes=["vwa"])

        def rope_to(pp, pk, n, gain_ap, t0, out3, okeys):
            headnorm(pp, pk, n, gain_ap, tn[:, 0:n], ["tn"])
            P.add("act", lambda g: g.copy(out=tnb[:, 0:n], in_=tn[:, 0:n]), reads=["tn"], writes=["tnb"])
            i = pp_i[0] % 4
            pp_i[0] += 1
            pr = PS[i // 2][:, (i % 2) * 512:(i % 2) * 512 + n]
            prk = "pp%d" % i
            P.add("pe", lambda g: g.matmul(pr, rotT[:, :], tnb[:, 0:n], start=True, stop=True), reads=["tnb", "rotT"], writes=[prk])
            P.add("dve", lambda g: g.tensor_tensor(out=t1[:, 0:n], in0=tn[:, 0:n], in1=cosb[:, 0:n], op=ALU.mult), reads=["tn", "cosb"], writes=["t1"])
            P.add("dve", lambda g: g.tensor_tensor(out=t2[:, 0:n], in0=pr, in1=sinb[:, 0:n], op=ALU.mult), reads=[prk, "sinb"], writes=["recw"])
            if len(out3.shape) == 2:
                P.add("pool", lambda g: g.tensor_tensor(out=out3, in0=t1[:, 0:n], in1=t2[:, 0:n], op=ALU.add), reads=["t1", "recw"], writes=okeys)
            else:
                P.add("pool", lambda g: g.tensor_tensor(out=out3, in0=t1[:, 0:n].rearrange("p (a b) -> p a b", b=128), in1=t2[:, 0:n].rearrange("p (a b) -> p a b", b=128), op=ALU.add),
                      reads=["t1", "recw"], writes=okeys)

        def load_rope(t0, n):
            P.add("sp", lambda g: g.dma_start(out=cosb[:, 0:n], in_=cos_d[:, t0:t0 + n]), writes=["cosb"], dma="cosb")
            P.add("sp", lambda g: g.dma_start(out=sinb[:, 0:n], in_=sin_d[:, t0:t0 + n]), writes=["sinb"], dma="sinb")

        for bi in allb:
            t0, n = BLKS[bi]
            pp, pk = proj(wkvv, 0, bi, wkey="wbkv")
            if bi < 4:
                load_rope(t0, n)
                rope_to(pp, pk, n, vcol(l, 27), t0, wak[:, t0:t0 + n], [("wak", bi)])
            else:
                headnorm(pp, pk, n, vcol(l, 27), wak[:, t0:t0 + n], [("wak", bi)])

            def vdst(tile, pp, pk):
                P.add("act", lambda g: g.copy(out=vwav[:, tile, :].rearrange("p (a b) -> p a b", b=64)[:, 0:3:2, :], in_=pp.rearrange("p (a b) -> p a b", b=64)),
                      reads=[pk, "vwa"], writes=[("vwa", tile)])
            vproj(wkvv, 128, 128, bi, vdst, wkey="wbkv")

        def q_chain(bi, j):
            t0, n = BLKS[bi]
            nq = n // 128
            wb_ = bi % 2
            okeys = [("waq", wb_, j)] + (["wbkv"] if wb_ == 1 else [])
            if j == 0 and bi < 4:
                load_rope(t0, n)
            pp, pk = proj(wqv, j * 128, bi, wkey="wbq")
            if bi < 4:
                rope_to(pp, pk, n, gq8[:, l * 2 + 1:l * 2 + 2], t0, waqvs[wb_][:, 0:nq, j, :], okeys)
            else:
                headnorm(pp, pk, n, gq8[:, l * 2 + 1:l * 2 + 2], waqvs[wb_][:, 0:nq, j, :], okeys)

        chainq = []
        for j in range(4):
            q_chain(qb[0], j)
        for qi, bi in enumerate(qb):
            t0, n = BLKS[bi]
            nq = n // 128
            waq = waqs[bi % 2]
            wqk = [("waq", bi % 2, j) for j in range(4)]
            if qi + 1 < len(qb):
                chainq = [(qb[qi + 1], j) for j in range(4)]
            else:
                chainq = []
            pendw = []
            for a in range(nq):
                nb = (t0 // 128) + a
                for gi in range(2):
                    lo, hi = gi * 64, gi * 64 + 64
                    dlo, dhi = (64, 128) if gi == 0 else (0, 64)
                    vc0 = 0 if gi == 0 else 64
                    chunks = []
                    if bi < 4:
                        if nb > 0:
                            chunks.append((nb - 1, 0))
                        chunks.append((nb, None))
                        if nb < 15:
                            chunks.append((nb + 1, 1))
                    chunks += [(16, None), (17, None)]
                    rhs = waq[lo:hi, a * 512:(a + 1) * 512]
                    wi = wa_cnt[0] % 2
                    wa_cnt[0] += 1
                    PTc = PTv2[wi]
                    for ci, (kt, mk) in enumerate(chunks):
                        si = sc_cnt[0] % 2
                        sc_cnt[0] += 1
                        sc = PS[2][:, si * 512:si * 512 + 512]
                        sck = "sc%d" % si
                        P.add("pe", lambda g: g.matmul(sc, wak[lo:hi, kt * 128:(kt + 1) * 128], rhs, start=True, stop=True),
                              reads=[("wak", kt // 4 if kt < 16 else 4)] + wqk, writes=[sck])
                        P.add("act", lambda g: g.activation(out=PTc[:, ci, :], in_=sc, func=AF.Exp), reads=[sck], writes=[("PTw", wi, ci)])
                        if mk is not None:
                            P.add("pool", lambda g: g.tensor_tensor(out=PTc[:, ci, :].rearrange("p (j q) -> p j q", j=4), in0=PTc[:, ci, :].rearrange("p (j q) -> p j q", j=4),
                                                                    in1=wam[:, mk * 128:(mk + 1) * 128].unsqueeze(1).to_broadcast([128, 4, 128]), op=ALU.mult),
                                  reads=[("PTw", wi, ci), "wam"], writes=[("PTw", wi, ci)])

                    def stageBw(chunks=chunks, PTc=PTc, wi=wi, lo=lo, hi=hi, dlo=dlo, dhi=dhi, vc0=vc0, a=a, gi=gi):
                        pv = PS[3][:, 512:1024]
                        for ci, (kt, mk) in enumerate(chunks):
                            P.add("pe", lambda g: g.matmul(pv, vwav[:, kt, vc0:vc0 + 128], PTc[:, ci, :], start=(ci == 0), stop=(ci == len(chunks) - 1)),
                                  reads=[("vwa", kt), ("PTw", wi, ci)], writes=["ps3b"])
                        P.add("dve", lambda g: g.tensor_tensor(out=recw[lo:hi, :].rearrange("p (j q) -> p j q", j=4), in0=pv[dlo:dhi, :].rearrange("p (j q) -> p j q", j=4),
                                                               in1=sinkv[dlo:dhi, l * 8 + gi * 4:l * 8 + gi * 4 + 4].unsqueeze(2).to_broadcast([64, 4, 128]), op=ALU.add),
                              reads=["ps3b", "sinkv"], writes=["recw"])
                        P.add("dve", lambda g: g.reciprocal(out=recw[lo:hi, :], in_=recw[lo:hi, :]), reads=["recw"], writes=["recw"])
                        P.add("dve", lambda g: g.tensor_tensor(out=owav[lo:hi, :, a * 128:(a + 1) * 128], in0=pv[lo:hi, :].rearrange("p (j q) -> p j q", j=4), in1=recw[lo:hi, :].rearrange("p (j q) -> p j q", j=4), op=ALU.mult),
                              reads=["ps3b", "recw"], writes=["owa"])
                    if pendw:
                        pendw.pop()()
                    pendw.append(stageBw)
                    if chainq and gi == 1:
                        q_chain(*chainq.pop(0))
            if pendw:
                pendw.pop()()
            while chainq:
                q_chain(*chainq.pop(0))
            if l == 0 and bi == 0:
                dump("owa0", owa)
            group_out(l, owav[:, :, 0:n], "owa", 4, n, bi, 4, wov, 512.0)
        if l == 0:
            dump("xmid", xT[:, :])

        hn_single[0] = False
        ar.reset()
        mb = lat if last else allb
        norm_all(l, 1, mb)
        f1 = [ar.bf(8 * 512), ar.bf(8 * 512)]
        f2 = [ar.bf(4 * 1024), ar.bf(4 * 1024)]
        HT = ar.bf(4 * T)
        HTv = HT.rearrange("p (i t) -> p i t", i=4)
        rl = [ar.f32(512), ar.f32(512)]
        if l + 1 < L:
            wmbA = AR[:, 0:2048].bitcast(BF16)
            wmbB = ar.bf(8 * 512)
            mod_q["alias0"] = ["nsq0", "nsq1", "nlnv", "nrstd0", "nrstd1", "ntmp0", "ntmp1"]
            mod_q["alias1"] = []
            mod_cnt[0] = 0
            mod_q["pieces"] = mod_pieces(l + 1, [0, 1, 2, 3, 4, 5], [wmbA, wmbB])
        for hc in range(8):
            f1v = f1[hc % 2].rearrange("p (k n) -> p k n", k=8)
            f2v = f2[hc % 2].rearrange("p (i n) -> p i n", i=4)
            load_w(f1v, fc1_d[l].rearrange("(k p) n -> p k n", p=128)[:, :, hc * 512:(hc + 1) * 512], key="f1_%d" % (hc % 2))
            load_w(f2v, fc2_d[l].rearrange("(i p) n -> p i n", p=128)[:, hc * 4:(hc + 1) * 4, :], key="f2_%d" % (hc % 2))
            mod_drain(2)
            for bi in mb:
                t0, n = BLKS[bi]
                for i4 in range(4):
                    i = pp_i[0] % 4
                    pp_i[0] += 1
                    pp = PS[i // 2][:, (i % 2) * 512:(i % 2) * 512 + n]
                    pk = "pp%d" % i
                    for kc in range(8):
                        P.add("pe", lambda g, pp=pp, kc=kc, i4=i4, f1v=f1v: g.matmul(pp, f1v[:, kc, i4 * 128:(i4 + 1) * 128], hTv[:, kc, t0:t0 + n], start=(kc == 0), stop=(kc == 7)),
                              reads=[("hT", kc, bi), "f1_%d" % (hc % 2)], writes=[pk])
                    r = rl[i4 % 2]
                    rk = "rl%d" % (i4 % 2)
                    P.add("act", lambda g, pp=pp, r=r: g.activation(out=r[:, 0:n], in_=pp, func=AF.Relu), reads=[pk], writes=[rk])
                    P.add("pool", lambda g, r=r, i4=i4: g.tensor_tensor(out=HTv[:, i4, t0:t0 + n], in0=r[:, 0:n], in1=r[:, 0:n], op=ALU.mult), reads=[rk], writes=[("HT", i4, bi)])
            for bi in mb:
                t0, n = BLKS[bi]
                s = 1 if bi == 4 else 0
                for nn in range(8):
                    i = pp_i[0] % 4
                    pp_i[0] += 1
                    pp = PS[i // 2][:, (i % 2) * 512:(i % 2) * 512 + n]
                    pk = "pp%d" % i
                    for i4 in range(4):
                        P.add("pe", lambda g, pp=pp, nn=nn, i4=i4, f2v=f2v: g.matmul(pp, f2v[:, i4, nn * 128:(nn + 1) * 128], HTv[:, i4, t0:t0 + n], start=(i4 == 0), stop=(i4 == 3)),
                              reads=[("HT", i4, bi), "f2_%d" % (hc % 2)], writes=[pk])
                    P.add("dve", lambda g, pp=pp, nn=nn: g.scalar_tensor_tensor(out=xTv[:, nn, t0:t0 + n], in0=pp, scalar=modcol(l, 5, nn, s), in1=xTv[:, nn, t0:t0 + n], op0=ALU.mult, op1=ALU.add),
                          reads=[pk, "modT", ("xT", nn, bi)], writes=[("xT", nn, bi)])
        mod_flush()
        mod_q["alias0"] = []

    ar.reset()
    stg = [ar.f32(1024), ar.f32(1024)]
    for ti in range(16):
        s = stg[ti % 2]
        sk = "ostg%d" % (ti % 2)
        pp = PS[ti % 2]
        pk = "ps%d" % (ti % 2)
        for kc in range(8):
            P.add("pe", lambda g, pp=pp, kc=kc, ti=ti: g.transpose(pp[:, kc * 128:(kc + 1) * 128], xTv[:, kc, ti * 128:(ti + 1) * 128], ident[:, :]),
                  reads=[("xT", kc, ti // 4), "ident"], writes=[pk])
        P.add("act", lambda g, s=s, pp=pp: g.copy(out=s[:, 0:512], in_=pp[:, 0:512]), reads=[pk], writes=[sk + "a"])
        P.add("dve", lambda g, s=s, pp=pp: g.tensor_copy(out=s[:, 512:1024], in_=pp[:, 512:1024]), reads=[pk], writes=[sk + "b"])
        P.add("sp", lambda g, s=s, ti=ti: g.dma_start(out=y_d[ti * 128:(ti + 1) * 128, :], in_=s), reads=[sk + "a", sk + "b"], dma="out%d" % (ti % 2))
    P.add("sp", lambda g: g.nop(), extra_deps=[o.id for o in P.ops if o.dma is not None and str(o.dma).startswith("out")])
    P.emit(st)
    st.close()
    return nc


def _host_consts():
    import ml_dtypes
    ident = np.eye(128, dtype=np.float32)
    R = np.zeros((64, 64), np.float32)
    for f in range(64):
        if (f % 32) < 16:
            R[f, f + 16] = -1.0
        else:
            R[f, f - 16] = 1.0
    R2 = np.zeros((128, 128), np.float32)
    R2[:64, :64] = R
    R2[64:, 64:] = R
    rotT = np.ascontiguousarray(R2.T)
    k = np.arange(128)[:, None]
    q = np.arange(128)[None, :]
    wam = np.concatenate([(k >= q).astype(np.float32), (k <= q).astype(np.float32)], axis=1)
    t = np.arange(2048)
    row = (t // 64).astype(np.float32)
    col = (t % 64).astype(np.float32)
    inv = (np.float32(10000.0) ** (-np.arange(16, dtype=np.float32) / np.float32(16))).astype(np.float32)
    ang = np.zeros((64, 2048), np.float32)
    for f in range(64):
        pos = row if f < 32 else col
        ang[f] = pos * inv[f % 16]
    cos = np.cos(ang).astype(np.float32)
    sin = np.sin(ang).astype(np.float32)
    return ident, rotT, wam, np.concatenate([cos, cos], 0), np.concatenate([sin, sin], 0)


def _nabias(rpb):
    out = np.full((L, 4, 128, 5, 5, 128), NEG, np.float32)
    jj = np.arange(2)[:, None, None, None]
    ck = np.arange(64)[None, :, None, None]
    rr = np.arange(2)[None, None, :, None]
    cq = np.arange(64)[None, None, None, :]
    for v, p in enumerate(VAR_REP):
        base = min(max(2 * p - 4, 0), 22)
        for c in range(5):
            rk = base + 2 * c + jj
            rq = 2 * p + rr
            start = np.clip(rq - 4, 0, 24)
            cs = np.clip(cq - 8, 0, 48)
            valid = (rk >= start) & (rk <= start + 7) & (rk <= 31) & (ck >= cs) & (ck < cs + 16)
            dr = np.clip(rk - rq + 7, 0, 14)
            dc = np.clip(ck - cq, -15, 15) + 15
            valid = np.broadcast_to(valid, (2, 64, 2, 64))
            drb = np.broadcast_to(dr, (2, 64, 2, 64))
            dcb = np.broadcast_to(dc, (2, 64, 2, 64))
            g = rpb[:, :, drb, dcb]
            g = np.where(valid[None, None], g, np.float32(NEG))
            out[:, :, :, v, c, :] = g.reshape(L, 4, 128, 128)
    return out.reshape(L, 4, 128, 5 * 5 * 128)


def _prep(inputs):
    f = lambda a: np.ascontiguousarray(np.asarray(a, dtype=np.float32))
    x, c, ctx, c_ctx = f(inputs["x"]), f(inputs["c"]), f(inputs["ctx"]), f(inputs["c_ctx"])
    w_in, w_o = f(inputs["w_in"]), f(inputs["w_o"])
    wa_q = 1536
    perm = list(range(0, 1536))
    for j in range(4):
        perm += list(range(wa_q + 64 * j, wa_q + 64 * j + 64)) + list(range(wa_q + 64 * (j + 4), wa_q + 64 * (j + 4) + 64))
    perm += list(range(2048, 2304))
    w_in_p = np.ascontiguousarray(w_in[:, :, perm])
    rperm = list(range(0, 512))
    for j in range(4):
        rperm += list(range(512 + 64 * j, 512 + 64 * j + 64)) + list(range(512 + 64 * (j + 4), 512 + 64 * (j + 4) + 64))
    w_o_p = np.ascontiguousarray(w_o[:, rperm, :])
    gout_p = f(inputs["g_out"])[:, rperm]
    col = lambda v: np.ascontiguousarray(v.reshape(-1, 128).T)
    vecs = np.zeros((128, L * 36), np.float32)
    for l in range(L):
        o = l * 36
        vecs[:, o:o + 8] = col(f(inputs["g_norm1"])[l])
        vecs[:, o + 8:o + 16] = col(f(inputs["g_norm2"])[l])
        vecs[:, o + 16:o + 24] = col(gout_p[l])
        for i, nm in enumerate(("na_q_gain", "na_k_gain", "wa_q_gain", "wa_k_gain")):
            vecs[:, o + 24 + i] = np.tile(f(inputs[nm])[l], 2)
        cw = f(inputs["conv_w"])[l]
        for ti in range(2):
            for tap in range(3):
                vecs[:, o + 28 + ti * 3 + tap] = cw[tap, ti * 128:(ti + 1) * 128]
            vecs[:, o + 34 + ti] = f(inputs["conv_bias"])[l, ti * 128:(ti + 1) * 128]
    bm = np.concatenate([col(f(inputs["b_mod"])[l]) for l in range(L)], axis=1)
    sink = np.ascontiguousarray(np.broadcast_to(f(inputs["wa_sink"]).reshape(1, L * 8), (128, L * 8)))
    ident, rotT, wam, cos, sin = _host_consts()
    nab = _nabias(f(inputs["na_rpb"]))
    shared = dict(w_mod=f(inputs["w_mod"]), bm=bm, vecs=vecs, sink=sink, w_in=w_in_p, w_o=w_o_p, w_fc1=f(inputs["w_fc1"]),
                  w_fc2=f(inputs["w_fc2"]), nabias=nab, ident=ident, rotT=rotT, wamask=wam, ropecos=cos, ropesin=sin)
    maps = []
    for b in range(8):
        ccm = np.zeros((128, 16), np.float32)
        ccm[:, 0::2] = c[b].reshape(8, 128).T
        ccm[:, 1::2] = c_ctx.reshape(8, 128).T
        m = dict(shared)
        m.update(x=x[b], ctx=ctx[b], cc=ccm)
        maps.append(m)
    return maps


_NC = {}


def kernel(**inputs):
    maps = _prep(inputs)
    if "nc" not in _NC:
        _NC["nc"] = build()
    res = run_bass_kernel_spmd(_NC["nc"], maps, core_ids=list(range(8)))
    return np.stack([np.asarray(r["y"], dtype=np.float32) for r in res.results], axis=0)
```
